# Optimizing a Trainium2 kernel written in Bass

```python
import math
import jax, jax.numpy as jnp
from jax import lax
import numpy as np

D_MODEL = 1024
BATCH = 2
SEQ = 8192
DEPTH = 1

D_MIX = D_MODEL
DIL_HEAD_DIM = 64
DIL_HEADS = (D_MIX // 2) // DIL_HEAD_DIM
DIL_WIDTH = DIL_HEADS * DIL_HEAD_DIM
DIL_BRANCHES = ((128, 1), (512, 4), (2048, 16))
MLA_NOPE = 128
MLA_ROPE = 64
MLA_QK_DIM = MLA_NOPE + MLA_ROPE
MLA_V_DIM = 128
MLA_HEADS = (D_MIX - DIL_WIDTH) // MLA_V_DIM
MLA_WIDTH = MLA_HEADS * MLA_V_DIM
MLA_Q_RANK = 256
MLA_KV_RANK = 128
ROPE_BASE = 10000.0
REL_BUCKETS = 32
REL_MAX_DIST = 2048
D_FF = 2816
FFN_RESID = 0.5
Q_BLOCK = 128
EPS = 1e-6
IN_COLS = 3 * DIL_WIDTH + MLA_Q_RANK + MLA_KV_RANK + MLA_ROPE

kernel_name = "hymba_dilated_mla_macaron"


def _rms(x, g):
    xf = x.astype(jnp.float32)
    y = xf * lax.rsqrt(jnp.mean(xf * xf, axis=-1, keepdims=True) + EPS)
    return (y * g.astype(jnp.float32)).astype(x.dtype)


def _swiglu(x, g, w_gate, w_up, w_down):
    h = _rms(x, g)
    return (jax.nn.silu(h @ w_gate) * (h @ w_up)) @ w_down


def _t5_bucket(dist):
    max_exact = REL_BUCKETS // 2
    d = np.maximum(dist, 1).astype(np.float32)
    large = max_exact + (np.log(d / max_exact) / np.log(REL_MAX_DIST / max_exact)
                         * (REL_BUCKETS - max_exact)).astype(np.int32)
    large = np.minimum(large, REL_BUCKETS - 1)
    return np.where(dist < max_exact, dist, large).astype(np.int32)


def _dilated_branch(q, k, v, rel_bias, window, dilation):
    B, S, H, hd = q.shape
    L = S // dilation
    W = window // dilation
    Bq = math.gcd(L, Q_BLOCK)
    nb = L // Bq

    def to_sub(t):
        return t.reshape(B, L, dilation, H, hd).transpose(0, 2, 3, 1, 4)

    qs = to_sub(q).reshape(B, dilation, H, nb, Bq, hd)
    pad = ((0, 0), (0, 0), (0, 0), (W, 0), (0, 0))
    ks = jnp.pad(to_sub(k), pad)
    vs = jnp.pad(to_sub(v), pad)
    idx = np.arange(nb)[:, None] * Bq + np.arange(Bq + W)[None, :]
    kb = ks[:, :, :, idx]
    vb = vs[:, :, :, idx]
    logits = jnp.einsum('brhnqc,brhnkc->brhnqk', qs, kb) * (hd ** -0.5)

    i = np.arange(Bq)[:, None]
    j = np.arange(Bq + W)[None, :]
    delta = i + W - j
    key_sub = idx[:, None, :] - W
    valid = (delta >= 0) & (delta <= W) & (key_sub >= 0)
    bucket = _t5_bucket(np.clip(delta, 0, None) * dilation)
    bias = jnp.take(rel_bias.astype(jnp.float32), jnp.asarray(bucket), axis=1)
    logits = logits + bias[None, None, :, None]
    logits = jnp.where(jnp.asarray(valid)[None, None, None], logits, -jnp.inf)
    lse = jax.nn.logsumexp(logits, axis=-1)
    p = jnp.exp(logits - lse[..., None])
    o = jnp.einsum('brhnqk,brhnkc->brhnqc', p, vb)
    o = o.reshape(B, dilation, H, L, hd).transpose(0, 3, 1, 2, 4).reshape(B, S, H, hd)
    lse = lse.reshape(B, dilation, H, L).transpose(0, 3, 1, 2).reshape(B, S, H)
    return o, lse


def _dilated_attention(q, k, v, q_g, k_g, rel_bias):
    B, S = q.shape[:2]
    sh = (B, S, DIL_HEADS, DIL_HEAD_DIM)
    q = _rms(q.reshape(sh).astype(jnp.float32), q_g)
    k = _rms(k.reshape(sh).astype(jnp.float32), k_g)
    v = v.reshape(sh).astype(jnp.float32)
    outs, lses = [], []
    for window, dilation in DIL_BRANCHES:
        o, lse = _dilated_branch(q, k, v, rel_bias, window, dilation)
        outs.append(o)
        lses.append(lse)
    alpha = jax.nn.softmax(jnp.stack(lses, 0), axis=0)
    o = jnp.sum(alpha[..., None] * jnp.stack(outs, 0), axis=0)
    return o.reshape(B, S, DIL_WIDTH)


def _rope(x, pos):
    dim = x.shape[-1]
    inv_freq = ROPE_BASE ** (-jnp.arange(0, dim, 2, dtype=jnp.float32) / dim)
    ang = pos[:, None] * inv_freq[None, :]
    cos = jnp.cos(ang)[None, :, None, :]
    sin = jnp.sin(ang)[None, :, None, :]
    x1, x2 = x[..., : dim // 2], x[..., dim // 2:]
    return jnp.concatenate([x1 * cos - x2 * sin, x2 * cos + x1 * sin], axis=-1)


def _mla(cq, ckv, k_pe, q_a_norm, w_q_b, kv_a_norm, w_kv_b, q_g, k_g):
    B, S = cq.shape[:2]
    H = MLA_HEADS
    pos = jnp.arange(S, dtype=jnp.float32)
    q = (_rms(cq, q_a_norm) @ w_q_b).reshape(B, S, H, MLA_QK_DIM).astype(jnp.float32)
    kv = (_rms(ckv, kv_a_norm) @ w_kv_b).reshape(B, S, H, MLA_NOPE + MLA_V_DIM).astype(jnp.float32)
    k_nope, v = kv[..., :MLA_NOPE], kv[..., MLA_NOPE:]
    k_pe = jnp.broadcast_to(k_pe.astype(jnp.float32)[:, :, None, :], (B, S, H, MLA_ROPE))
    k = jnp.concatenate([k_nope, k_pe], axis=-1)
    q = _rms(q, q_g)
    k = _rms(k, k_g)
    q = jnp.concatenate([q[..., :MLA_NOPE], _rope(q[..., MLA_NOPE:], pos)], axis=-1)
    k = jnp.concatenate([k[..., :MLA_NOPE], _rope(k[..., MLA_NOPE:], pos)], axis=-1)

    nb = S // Q_BLOCK
    qb = (q * MLA_QK_DIM ** -0.5).transpose(0, 2, 1, 3).reshape(B, H, nb, Q_BLOCK, MLA_QK_DIM)
    qb = qb.transpose(2, 0, 1, 3, 4)
    kt = k.transpose(0, 2, 1, 3)
    vt = v.transpose(0, 2, 1, 3)
    key_pos = jnp.arange(S)

    def block(args):
        q_blk, n = args
        logits = jnp.einsum('bhqc,bhkc->bhqk', q_blk, kt)
        q_pos = n * Q_BLOCK + jnp.arange(Q_BLOCK)
        mask = key_pos[None, :] <= q_pos[:, None]
        p = jax.nn.softmax(jnp.where(mask, logits, -jnp.inf), axis=-1)
        return jnp.einsum('bhqk,bhkc->bhqc', p, vt)

    o = lax.map(block, (qb, jnp.arange(nb)))
    return o.transpose(1, 0, 3, 2, 4).reshape(B, S, MLA_WIDTH)


def setup_inputs(seed: int = 0) -> dict:
    key = jax.random.key(seed)
    ks = jax.random.split(key, 24)
    f32 = jnp.float32

    def w(k, shape, fan_in):
        return jax.random.normal(k, (DEPTH,) + shape, f32) * fan_in ** -0.5

    def g(k, dim):
        return 1.0 + 0.02 * jax.random.normal(k, (DEPTH, dim), f32)

    return {
        "x": jax.random.normal(ks[0], (BATCH, SEQ, D_MODEL), f32),
        "ffn1_norm": g(ks[1], D_MODEL),
        "ffn1_w_gate": w(ks[2], (D_MODEL, D_FF), D_MODEL),
        "ffn1_w_up": w(ks[3], (D_MODEL, D_FF), D_MODEL),
        "ffn1_w_down": w(ks[4], (D_FF, D_MODEL), D_FF),
        "mix_norm": g(ks[5], D_MODEL),
        "w_in": w(ks[6], (D_MODEL, IN_COLS), D_MODEL),
        "dil_q_norm": g(ks[7], DIL_HEAD_DIM),
        "dil_k_norm": g(ks[8], DIL_HEAD_DIM),
        "rel_bias": 0.2 * jax.random.normal(ks[9], (DIL_HEADS, REL_BUCKETS), f32),
        "mla_q_a_norm": g(ks[10], MLA_Q_RANK),
        "mla_w_q_b": w(ks[11], (MLA_Q_RANK, MLA_HEADS * MLA_QK_DIM), MLA_Q_RANK),
        "mla_kv_a_norm": g(ks[12], MLA_KV_RANK),
        "mla_w_kv_b": w(ks[13], (MLA_KV_RANK, MLA_HEADS * (MLA_NOPE + MLA_V_DIM)), MLA_KV_RANK),
        "mla_q_norm": g(ks[14], MLA_QK_DIM),
        "mla_k_norm": g(ks[15], MLA_QK_DIM),
        "out_norm_dil": g(ks[16], DIL_WIDTH),
        "out_norm_mla": g(ks[17], MLA_WIDTH),
        "w_out": w(ks[18], (D_MIX, D_MODEL), D_MIX),
        "ffn2_norm": g(ks[19], D_MODEL),
        "ffn2_w_gate": w(ks[20], (D_MODEL, D_FF), D_MODEL),
        "ffn2_w_up": w(ks[21], (D_MODEL, D_FF), D_MODEL),
        "ffn2_w_down": w(ks[22], (D_FF, D_MODEL), D_FF),
    }


def reference(x, ffn1_norm, ffn1_w_gate, ffn1_w_up, ffn1_w_down, mix_norm, w_in,
              dil_q_norm, dil_k_norm, rel_bias, mla_q_a_norm, mla_w_q_b, mla_kv_a_norm,
              mla_w_kv_b, mla_q_norm, mla_k_norm, out_norm_dil, out_norm_mla, w_out,
              ffn2_norm, ffn2_w_gate, ffn2_w_up, ffn2_w_down):
    splits = np.cumsum([DIL_WIDTH, DIL_WIDTH, DIL_WIDTH, MLA_Q_RANK, MLA_KV_RANK])
    for l in range(DEPTH):
        x = x + FFN_RESID * _swiglu(x, ffn1_norm[l], ffn1_w_gate[l], ffn1_w_up[l], ffn1_w_down[l])
        h = _rms(x, mix_norm[l])
        proj = h @ w_in[l]
        q_a, k_a, v_a, cq, ckv, k_pe = jnp.split(proj, splits, axis=-1)
        o_dil = _dilated_attention(q_a, k_a, v_a, dil_q_norm[l], dil_k_norm[l], rel_bias)
        o_mla = _mla(cq, ckv, k_pe, mla_q_a_norm[l], mla_w_q_b[l], mla_kv_a_norm[l],
                     mla_w_kv_b[l], mla_q_norm[l], mla_k_norm[l])
        o = jnp.concatenate([_rms(o_dil, out_norm_dil[l]), _rms(o_mla, out_norm_mla[l])], axis=-1)
        x = x + o.astype(x.dtype) @ w_out[l]
        x = x + FFN_RESID * _swiglu(x, ffn2_norm[l], ffn2_w_gate[l], ffn2_w_up[l], ffn2_w_down[l])
    return x
```

```python
import math
import numpy as np
import ml_dtypes
import concourse.bass as bass
import concourse.mybir as mybir
from concourse.bass_utils import run_bass_kernel_spmd

F32 = mybir.dt.float32
BF16 = mybir.dt.bfloat16
AF = mybir.ActivationFunctionType
ALU = mybir.AluOpType

D = 1024
DFF = 2816
NFF = DFF // 128
SEQ = 8192
TOK = 2048
TT = 512
NT = TOK // TT
EPS = 1e-6
ENGS = ("pe", "act", "dve", "pool", "sp")


class H:
    __slots__ = ("eng", "sig", "val", "dma")

    def __init__(self, eng):
        self.eng = eng
        self.sig = False
        self.val = None
        self.dma = None


class Prog:
    def __init__(self, nc):
        self.nc = nc
        self.streams = {e: [] for e in ENGS}
        self.dma_cnt = {}
        self.trk = {}
        self.pending = {}
        self.all_dma = []
        self.cap = None

    def op(self, eng, fn, deps=(), reads=(), writes=(), _async=False):
        if self.cap is not None:
            self.cap.append(("op", eng, fn, tuple(reads), tuple(writes)))
            return None
        h = H(eng)
        deps = [d for d in deps if d is not None] + self.pending.pop(eng, [])
        trk = self.trk
        for k in list(reads) + list(writes):
            w = trk.setdefault(k, [None, {}])
            if w[0] is not None:
                deps.append(w[0])
        for k in writes:
            deps.extend(trk[k][1].values())
        for k in writes:
            trk[k][0] = h
            trk[k][1] = {}
        for k in reads:
            rk = id(h) if _async else eng
            trk[k][1][rk] = h
        deps = [d for d in deps if d is not h]
        for d in deps:
            if d.dma is None and d.eng != eng:
                d.sig = True
        self.streams[eng].append((h, fn, deps))
        return h

    def dma(self, eng, key, fn, deps=(), reads=(), writes=()):
        if self.cap is not None:
            self.cap.append(("dma", eng, key, fn, tuple(reads), tuple(writes)))
            return None
        h = self.op(eng, fn, deps, reads, writes, _async=True)
        self.dma_cnt[key] = self.dma_cnt.get(key, 0) + 16
        h.dma = (key, self.dma_cnt[key])
        self.all_dma.append(h)
        return h

    def capture(self, gen):
        steps = []
        self.cap = cur = []
        for y in gen:
            if cur or y == "pad":
                steps.append(cur)
            self.cap = cur = []
        if cur:
            steps.append(cur)
        self.cap = None
        return steps

    def replay(self, step):
        for rec in step:
            if rec[0] == "op":
                self.op(rec[1], rec[2], (), rec[3], rec[4])
            elif rec[0] == "dma":
                self.dma(rec[1], rec[2], rec[3], (), rec[4], rec[5])
            else:
                self.cc(rec[1], rec[2], (), rec[3], rec[4])

    def cc(self, key, fn, deps=(), reads=(), writes=()):
        if self.cap is not None:
            self.cap.append(("cc", key, fn, tuple(reads), tuple(writes)))
            return None
        h = self.op("pool", fn, deps, reads, writes, _async=True)
        self.dma_cnt[key] = self.dma_cnt.get(key, 0) + 1
        h.dma = (key, self.dma_cnt[key])
        self.all_dma.append(h)
        return h

    def barrier(self):
        deps = [h for h in self.all_dma if not (isinstance(h.dma[0], tuple) and h.dma[0][0] == "cc")]
        self.all_dma = []
        for e in ENGS:
            for (h, fn, d) in reversed(self.streams[e]):
                if h.dma is None:
                    deps.append(h)
                    break
        for d in deps:
            if d.dma is None:
                d.sig = True
        self.pending = {e: list(deps) for e in ENGS}

    def emit(self, final_waits=()):
        nc = self.nc
        for e in ENGS:
            c = 0
            for (h, fn, deps) in self.streams[e]:
                if h.dma is None and h.sig:
                    c += 1
                    h.val = c
        import contextlib
        with contextlib.ExitStack() as es:
            esem = {e: es.enter_context(nc.semaphore("s_" + e)) for e in ENGS}
            dsem = {k: es.enter_context(nc.semaphore("d%d" % i)) for i, k in enumerate(self.dma_cnt)}
            block = es.enter_context(nc.Block())

            def run(e, engobj):
                seen = {}
                for (h, fn, deps) in self.streams[e]:
                    for d in deps:
                        if d.dma is not None:
                            k, v = ("d", d.dma[0]), d.dma[1]
                            sem = dsem[d.dma[0]]
                        else:
                            if d.eng == e:
                                continue
                            k, v = ("e", d.eng), d.val
                            sem = esem[d.eng]
                        if seen.get(k, 0) >= v:
                            continue
                        seen[k] = v
                        engobj.wait_ge(sem, v)
                    ins = fn(engobj)
                    if h.dma is not None:
                        ins.then_inc(dsem[h.dma[0]], 1 if (isinstance(h.dma[0], tuple) and h.dma[0][0] == "cc") else 16)
                    elif h.sig:
                        ins.then_inc(esem[e], 1)
                if e == "sp":
                    for d in final_waits:
                        engobj.wait_ge(dsem[d.dma[0]], d.dma[1])

            @block.tensor
            def _(eng):
                run("pe", eng)

            @block.scalar
            def _(eng):
                run("act", eng)

            @block.vector
            def _(eng):
                run("dve", eng)

            @block.gpsimd
            def _(eng):
                run("pool", eng)

            @block.sync
            def _(eng):
                run("sp", eng)


class Arena:
    def __init__(self, t, nbytes):
        self.t = t
        self.views = {BF16: t, F32: t.bitcast(F32)}
        self.nbytes = nbytes
        self.off = 0
        self.marks = []

    def alloc(self, cols, dtype, parts=128):
        sz = 4 if dtype == F32 else 2
        self.off = (self.off + 63) // 64 * 64
        a = self.off
        self.last = a
        self.off += cols * sz
        assert self.off <= self.nbytes, ("SBUF arena overflow", self.off, self.nbytes)
        return self.views[dtype][0:parts, a // sz: a // sz + cols]

    def mark(self):
        self.marks.append(self.off)

    def release(self):
        self.off = self.marks.pop()


NB = SEQ // 128
SC_D = 0.125
SC_M = 192.0 ** -0.5
FLEN = 3072
MW = 2944


def build_nc(stage="full"):
    nc = bass.Bass("TRN2", target_bir_lowering=False)
    P = Prog(nc)

    def din(name, shape, dt=F32):
        return nc.dram_tensor(name, list(shape), dt, kind="ExternalInput")

    x_d = din("x", [TOK, D])
    ident_d = din("ident", [128, 128])
    gv_d = din("gv", [128, 64])
    wf_d = {(n, k): din("w%d%s" % (n, k), [D, DFF] if k != "d" else [DFF, D])
            for n in (1, 2) for k in ("g", "u", "d")}
    wsel_d = din("wsel", [D, 896])
    wqb_d = din("wqb", [256, 256])
    wkvb_d = din("wkvb", [128, 256])
    wo_d = din("wo", [D, D])
    relbT_d = din("relbT", [32, 2])
    cm_d = din("cm", [32, FLEN])
    tri_d = din("tri", [128, 128])
    cos_d = din("cos2", [64, SEQ])
    sin_d = din("sin2", [64, SEQ])
    out_d = nc.dram_tensor("out", [TOK, D], F32, kind="ExternalOutput")
    xs_d = nc.dram_tensor("xs", [D, TOK], F32)
    b1_d = [nc.dram_tensor("b1_%d" % t, [D, TT], BF16) for t in range(NT)]
    g1_d = [nc.dram_tensor("g1_%d" % t, [4 * D, TT], BF16) for t in range(NT)]
    b2_d = [nc.dram_tensor("b2_%d" % u, [256, 1024], F32) for u in range(8)]
    g2_d = nc.dram_tensor("g2", [8 * 1024, 1024], F32)
    fvec_d = nc.dram_tensor("fvec", [2, FLEN], BF16)
    w2bf_d = {"g": nc.dram_tensor("w2g_bf", [D, DFF], BF16), "u": nc.dram_tensor("w2u_bf", [D, DFF], BF16),
              "d": nc.dram_tensor("w2d_bf", [DFF, D], BF16), "o": nc.dram_tensor("wo_bf", [D, D], BF16)}
    GROUPS = [[0, 1, 2, 3], [4, 5, 6, 7]]
    wselbf_d = nc.dram_tensor("wsel_bf", [D, 896], BF16)
    wqbbf_d = nc.dram_tensor("wqb_bf", [256, 256], BF16)
    wkvbbf_d = nc.dram_tensor("wkvb_bf", [128, 256], BF16)

    import contextlib
    with contextlib.ExitStack() as es:
        ARENA_BYTES = 206 * 1024
        big = es.enter_context(nc.sbuf_tensor("arena", [128, ARENA_BYTES // 2], BF16))
        A = Arena(big, ARENA_BYTES)
        ps = [es.enter_context(nc.psum_tensor("ps%d" % i, [128, 512], F32)) for i in range(8)]

        def MM(out, lhsT, rhs, start, stop, reads, writes):
            return P.op("pe", lambda e: e.matmul(out, lhsT=lhsT, rhs=rhs, start=start, stop=stop),
                        reads=reads, writes=writes)

        def ACT(out, in_, func, reads, writes, scale=1.0, bias=None):
            if bias is None:
                return P.op("act", lambda e: e.activation(out=out, in_=in_, func=func, scale=scale),
                            reads=reads, writes=writes)
            return P.op("act", lambda e: e.activation(out=out, in_=in_, func=func, scale=scale, bias=bias),
                        reads=reads, writes=writes)

        def STT(eng, out, in0, scalar, in1, op0, op1, reads, writes):
            return P.op(eng, lambda e: e.scalar_tensor_tensor(out=out, in0=in0, scalar=scalar, in1=in1,
                                                              op0=op0, op1=op1), reads=reads, writes=writes)

        def TTO(eng, out, in0, in1, op, reads, writes):
            return P.op(eng, lambda e: e.tensor_tensor(out=out, in0=in0, in1=in1, op=op),
                        reads=reads, writes=writes)

        def CP(eng, out, in_, reads, writes):
            return P.op(eng, lambda e: e.tensor_copy(out=out, in_=in_), reads=reads, writes=writes)

        def RECIP(out, in_, reads, writes):
            return P.op("dve", lambda e: e.reciprocal(out=out, in_=in_), reads=reads, writes=writes)

        def MEMSET(ap, v, writes):
            return P.op("dve", lambda e: e.memset(ap, v), writes=writes)

        def DMA(q, key, out, in_, reads, writes):
            return P.dma(q, key, lambda e: e.dma_start(out=out, in_=in_), reads=reads, writes=writes)

        ident = A.alloc(128, F32)
        o1024 = A.alloc(128, F32)
        o512 = A.alloc(128, F32)
        o256 = A.alloc(128, F32)
        o192 = A.alloc(128, F32)
        o128 = A.alloc(128, F32)
        bd64 = A.alloc(128, F32)
        one_f = A.alloc(128, F32)
        one_bf = A.alloc(128, BF16)
        tri = A.alloc(128, BF16)
        gv = A.alloc(64, F32)
        eps_c = A.alloc(1, F32)
        e64 = A.alloc(128, F32)
        DMA("sp", "c_ident", ident, ident_d[:, :], [], ["ident"])
        DMA("sp", "c_gv", gv, gv_d[:, :], [], ["gv"])
        DMA("pool", "c_tri", tri, tri_d[:, :], [], ["tri"])
        for ap_, v_ in ((o1024, 1.0 / 1024), (o512, 1.0 / 512), (o256, 1.0 / 256), (o192, 1.0 / 192),
                        (o128, 1.0 / 128), (one_f, 1.0), (one_bf, 1.0), (eps_c, EPS), (bd64, 0.0)):
            MEMSET(ap_, v_, ["cmat"])
        MEMSET(e64, 0.0, ["cmat"])
        MEMSET(e64[:, 64:65], 1.0, ["cmat"])
        MEMSET(bd64[0:64, 0:64], 1.0 / 64, ["cmat"])
        MEMSET(bd64[64:128, 64:128], 1.0 / 64, ["cmat"])
        A.mark()

        wg3 = A.alloc(8 * DFF, BF16).rearrange("p (c f) -> p c f", c=8)
        wu3 = A.alloc(8 * DFF, BF16).rearrange("p (c f) -> p c f", c=8)
        wd3 = A.alloc(NFF * D, BF16).rearrange("p (f d) -> p f d", f=NFF)
        A.mark()
        xin = [A.alloc(TT, F32) for _ in range(2)]
        _xo = A.last - TT * 4
        h2b = A.alloc(8 * TT, BF16)
        assert A.last == _xo + 2 * TT * 4
        wo_resA = A.views[BF16][0:128, _xo // 2:_xo // 2 + 8 * 768].rearrange("p (k d) -> p k d", k=8)
        h2v = h2b.rearrange("p (c t) -> p c t", c=8)
        ostg = [h2b.bitcast(F32)[:, i * D:(i + 1) * D] for i in range(2)]
        xTb = [A.alloc(8 * TT, F32).rearrange("p (c t) -> p c t", c=8) for _ in range(2)]
        xT3 = xTb[0]
        oT3 = xTb[1]
        hT3 = A.alloc(8 * TT, BF16).rearrange("p (c t) -> p c t", c=8)
        GF = [(0, 6), (6, 12), (12, 17), (17, 22)]
        actT = A.alloc(6 * TT, BF16)
        actT3 = actT.rearrange("p (f t) -> p f t", f=6)
        wo_resB = actT[:, 0:2048].rearrange("p (k d) -> p k d", k=8)
        sg = [A.alloc(TT, F32) for _ in range(2)]
        rstd = A.alloc(TT, F32)
        rstdQ = A.alloc(TT, F32)
        sq2 = A.alloc(TT, F32)

        FP, FD = 512, 4

        def load_ffn_weights(n):
            ops = []
            if n == 1:
                srcs = {k: wf_d[(n, k)].ap() for k in "gud"}
                q, rd = "pool", {k: [] for k in "gud"}
            else:
                srcs = {k: w2bf_d[k].ap() for k in "gud"}
                q, rd = "act", {k: [("w2bf", k)] for k in "gud"}
            gv_ = srcs["g"].rearrange("(c p) f -> p c f", p=128)
            uv_ = srcs["u"].rearrange("(c p) f -> p c f", p=128)
            dv_ = srcs["d"].rearrange("(f p) d -> p f d", p=128)
            for i, f0 in enumerate(range(0, DFF, FP)):
                f1 = min(DFF, f0 + FP)
                ops.append(lambda i=i, f0=f0, f1=f1: DMA(q, ("wg", i), wg3[:, :, f0:f1], gv_[:, :, f0:f1], rd["g"], [("wg", i)]))
                ops.append(lambda i=i, f0=f0, f1=f1: DMA(q, ("wu", i), wu3[:, :, f0:f1], uv_[:, :, f0:f1], rd["u"], [("wu", i)]))
            for i, f0 in enumerate(range(0, NFF, FD)):
                f1 = min(NFF, f0 + FD)
                ops.append(lambda i=i, f0=f0, f1=f1: DMA(q, ("wd", i), wd3[:, f0:f1, :], dv_[:, f0:f1, :], rd["d"], [("wd", i)]))
            return ops

        def stats(srcs, ones_m, out_rstd, out_key, bank=6):
            for i, (src, p, rk) in enumerate(srcs):
                if i == 0:
                    ACT(out_rstd[0:p, :], src, AF.Square, rk, [out_key])
                else:
                    s_ = i % 2
                    ACT(sg[s_][0:p, :], src, AF.Square, rk, [("sg", s_)])
                    TTO("dve", out_rstd[0:p, :], out_rstd[0:p, :], sg[s_][0:p, :], ALU.add, [out_key, ("sg", s_)], [out_key])
            MM(ps[bank][:, :], ones_m, out_rstd, True, True, [out_key, "cmat"], [("ps", bank)])
            ACT(out_rstd, ps[bank][:, :], AF.Ln, [("ps", bank), "cmat"], [out_key], bias=eps_c)
            ACT(out_rstd, out_rstd, AF.Exp, [out_key], [out_key], scale=-0.5)

        from collections import deque

        def stats8_g(xt3, xkey, out_rstd, out_key, bank):
            ACT(out_rstd, xt3[:, 0, :], AF.Square, [xkey], [out_key])
            yield
            for c in range(1, 8):
                ACT(sq2, xt3[:, c, :], AF.Square, [xkey], ["sq2"])
                yield
                TTO("dve", out_rstd, out_rstd, sq2, ALU.add, [out_key, "sq2"], [out_key])
                yield
            MM(ps[bank][:, :], o1024, out_rstd, True, True, [out_key, "cmat"], [("ps", bank)])
            yield
            ACT(out_rstd, ps[bank][:, :], AF.Ln, [("ps", bank), "cmat"], [out_key], bias=eps_c)
            ACT(out_rstd, out_rstd, AF.Exp, [out_key], [out_key], scale=-0.5)
            yield

        def ht_stage(b, gcol):
            for c in range(8):
                STT("dve", hT3[:, c, :], xTb[b][:, c, :], gv[:, gcol + c:gcol + c + 1], rstd, ALU.mult, ALU.mult,
                    [("xT", b), "gv", "rstd"], ["hT"])

        def ffn_main(b, dq, nxt_ht=None):
            xt = xTb[b]
            xk = ("xT", b)
            slots = [2 * NFF + 8 * (len(GF) - 1)]

            def fill():
                if dq:
                    k = -(-len(dq) // max(1, slots[0]))
                    for _ in range(min(k, len(dq))):
                        P.replay(dq.popleft())
                slots[0] -= 1

            for gi, (f0, f1) in enumerate(GF):
                for f in range(f0, f1):
                    s_ = f % 2
                    for c in range(8):
                        MM(ps[s_][:, :], wg3[:, c, f * 128:(f + 1) * 128], hT3[:, c, :], c == 0, c == 7,
                           ["hT", ("wg", f * 128 // FP)], [("ps", s_)])
                    fill()
                    for c in range(8):
                        MM(ps[2 + s_][:, :], wu3[:, c, f * 128:(f + 1) * 128], hT3[:, c, :], c == 0, c == 7,
                           ["hT", ("wu", f * 128 // FP)], [("ps", 2 + s_)])
                    ACT(sg[s_], ps[s_][:, :], AF.Silu, [("ps", s_)], [("sg", s_)])
                    TTO("dve", actT3[:, f - f0, :], ps[2 + s_][:, :], sg[s_], ALU.mult,
                        [("ps", 2 + s_), ("sg", s_)], ["actT"])
                    fill()
                if gi == len(GF) - 1:
                    while dq:
                        P.replay(dq.popleft())
                    if nxt_ht is not None:
                        nxt_ht()
                for d in range(8):
                    s_ = d % 2
                    for f in range(f0, f1):
                        MM(ps[4 + s_][:, :], wd3[:, f, d * 128:(d + 1) * 128], actT3[:, f - f0, :], f == f0, f == f1 - 1,
                           ["actT", ("wd", f // FD)], [("ps", 4 + s_)])
                    STT("dve", xt[:, d, :], ps[4 + s_][:, :], 0.5, xt[:, d, :], ALU.mult, ALU.add,
                        [("ps", 4 + s_), xk], [xk])
                    if gi < len(GF) - 1:
                        fill()

        xs_v = xs_d.ap().rearrange("(c p) t -> p c t", p=128)

        def pre_A_loads(t):
            for k in range(2):
                s_, hf = divmod(k, 2)
                r0 = t * TT + s_ * 128
                DMA("sp", ("xin", k % 2), xin[k % 2], x_d[r0:r0 + 128, hf * 512:(hf + 1) * 512], [], [("xin", k % 2)])
                yield

        def pre_A(t):
            b = t % 2
            for k in range(8):
                s_, hf = divmod(k, 2)
                slot = k % 2
                for cc in range(4):
                    P.op("pe", lambda e, hf=hf, cc=cc, slot=slot: e.transpose(
                        out=ps[6 + hf][:, cc * 128:(cc + 1) * 128], in_=xin[slot][:, cc * 128:(cc + 1) * 128],
                        identity=ident), reads=[("xin", slot), "ident"], writes=[("ps", 6 + hf)])
                yield
                dst = xTb[b][:, hf * 4:hf * 4 + 4, s_ * 128:(s_ + 1) * 128]
                src = ps[6 + hf][:, :].rearrange("p (c t) -> p c t", c=4)
                if hf == 0:
                    ACT(dst, src, AF.Copy, [("ps", 6)], [("xT", b)])
                else:
                    CP("dve", dst, src, [("ps", 7)], [("xT", b)])
                if k + 2 < 8:
                    s2, hf2 = divmod(k + 2, 2)
                    r0 = t * TT + s2 * 128
                    DMA("sp", ("xin", slot), xin[slot], x_d[r0:r0 + 128, hf2 * 512:(hf2 + 1) * 512], [], [("xin", slot)])
                yield

        def pre_A_stats(t):
            b = t % 2
            yield from stats8_g(xTb[b], ("xT", b), rstd, "rstd", 6)

        def post_A(t):
            b = t % 2
            DMA("sp", ("xT", b), xs_v[:, :, t * TT:(t + 1) * TT], xTb[b], [("xT", b)], [("xs", t)])
            yield
            yield from stats8_g(xTb[b], ("xT", b), rstdQ, "rstdQ", 7)
            for c in range(8):
                STT("dve", h2v[:, c, :], xTb[b][:, c, :], gv[:, 8 + c:9 + c], rstdQ, ALU.mult, ALU.mult,
                    [("xT", b), "gv", "rstdQ"], ["h2b"])
                if c % 4 == 3:
                    yield

        def post_A2(t):
            DMA("sp", "h2b", b1_d[t].ap().rearrange("(c p) n -> p c n", p=128), h2v, ["h2b"], [("b1", t)])
            yield
            P.cc(("cc", "g1", t), lambda e, t=t: e.collective_compute(
                "AllGather", ALU.bypass, replica_groups=GROUPS, ins=[b1_d[t][:, :]], outs=[g1_d[t][:, :]]),
                reads=[("b1", t)], writes=[("g1", t)])
            yield

        wl1 = load_ffn_weights(1)
        gu = lambda i: [wl1[2 * i], wl1[2 * i + 1]]
        dd = lambda i: [wl1[12 + i]]
        order = gu(0) + gu(1) + dd(0) + dd(1) + gu(2) + dd(2) + gu(3) + gu(4) + dd(3) + dd(4) + gu(5) + dd(5)
        assert len(order) == len(wl1) == 18
        for w in order:
            w()
        for _ in pre_A_loads(0):
            pass
        for _ in pre_A(0):
            pass
        for _ in pre_A_stats(0):
            pass
        ht_stage(0, 0)
        dqA = deque()
        for t in range(NT):
            if t + 1 < NT:
                dqA.extend(P.capture(pre_A_loads(t + 1)))
            if t >= 1:
                dqA.extend(P.capture(post_A(t - 1)))
            if t + 1 < NT:
                dqA.extend(P.capture(pre_A(t + 1)))
            if t >= 1:
                dqA.extend(P.capture(post_A2(t - 1)))
            if t + 1 < NT:
                dqA.extend(P.capture(pre_A_stats(t + 1)))
            if t == 1:
                def precast():
                    DMA("pool", ("pc", "wsel"), wselbf_d[:, :], wsel_d[:, :], [], [("pc", "wsel")])
                    yield
                    DMA("pool", ("pc", "wqb"), wqbbf_d[:, :], wqb_d[:, :], [], [("pc", "wqb")])
                    DMA("pool", ("pc", "wkvb"), wkvbbf_d[:, :], wkvb_d[:, :], [], [("pc", "wkvb")])
                    yield
                dqA.extend(P.capture(precast()))
            ffn_main(t % 2, dqA, (lambda t=t: ht_stage((t + 1) % 2, 0)) if t + 1 < NT else None)
        AB = Arena(big, ARENA_BYTES)
        AB.off = A.marks[0]
        e_wsel3 = AB.alloc(8 * 896, BF16).rearrange("p (c f) -> p c f", c=8)
        e_wqb3 = AB.alloc(2 * 256, BF16).rearrange("p (c f) -> p c f", c=2)
        e_wkvb = AB.alloc(256, BF16)
        AB.alloc(FLEN, BF16)
        AB.alloc(FLEN, BF16)
        e_h2t3 = AB.alloc(8 * TT, BF16).rearrange("p (c t) -> p c t", c=8)
        e_end = AB.off
        WGK = [("wg", i) for i in range(6)]
        DMA("sp", "wsel", e_wsel3, wselbf_d.ap().rearrange("(c p) f -> p c f", p=128), [("pc", "wsel")], ["wsel"] + WGK)
        DMA("sp", "wqb", e_wqb3, wqbbf_d.ap().rearrange("(c p) f -> p c f", p=128), [("pc", "wqb")], ["wqb"] + WGK)
        DMA("sp", "wkvb", e_wkvb, wkvbbf_d[:, :], [("pc", "wkvb")], ["wkvb"] + WGK)
        DMA("sp", "h2t", e_h2t3, g1_d[0].ap()[0:D, :].rearrange("(c p) n -> p c n", p=128), [("g1", 0)], ["h2t"] + WGK)
        for _ in post_A(NT - 1):
            pass
        for _ in post_A2(NT - 1):
            pass
        P.barrier()

        A.release()
        A.release()
        A.mark()
        wsel3 = A.alloc(8 * 896, BF16).rearrange("p (c f) -> p c f", c=8)
        wqb3 = A.alloc(2 * 256, BF16).rearrange("p (c f) -> p c f", c=2)
        wkvb = A.alloc(256, BF16)
        Mh = [A.alloc(FLEN, BF16) for _ in range(2)]
        h2t3 = A.alloc(8 * TT, BF16).rearrange("p (c t) -> p c t", c=8)
        assert A.off == e_end and e_end <= A.marks[0] + 8 * DFF * 2, "early-load buffers must sit inside the FFN1 gate-weight region"
        qdp = [[A.alloc(TT, BF16) for _ in range(2)] for _ in range(2)]
        kdT = A.alloc(SEQ, BF16)
        Vd4 = A.alloc(NB * 2 * 128, BF16).rearrange("p (b h c) -> p b h c", b=NB, h=2)
        KTn = A.alloc(SEQ, BF16)
        KTr = A.alloc(SEQ, BF16)
        Vm3 = A.alloc(NB * 128, BF16).rearrange("p (b c) -> p b c", b=NB)
        cqn3 = A.alloc(2 * TT, BF16).rearrange("p (c t) -> p c t", c=2)
        ckvn = A.alloc(TT, BF16)
        Qn2 = [A.alloc(TT, BF16) for _ in range(2)]
        Qr2 = [A.alloc(TT, BF16) for _ in range(2)]
        sg_A = sg
        sg = [A.alloc(TT, F32) for _ in range(2)]
        rs0 = A.alloc(TT, F32)
        rs1 = A.alloc(TT, F32)
        ra = A.alloc(TT, F32)
        rb = A.alloc(TT, F32)
        cst = A.alloc(TT, F32)
        snt = A.alloc(TT, F32)
        Pt = [A.alloc(TT, BF16) for _ in range(6)]
        pe_a = A.alloc(TT, F32)
        pe_b = A.alloc(TT, F32)
        ocp = [A.alloc(TT, F32) for _ in range(3)]
        cq0_s = A.alloc(TT, F32)
        rinv = A.alloc(TT, F32)
        acc64 = A.alloc(TT, F32)
        rinvd = A.alloc(TT, F32)
        cm_s = A.alloc(FLEN, F32)
        rb_s = A.alloc(2, F32)
        et_s = A.alloc(2, F32)
        fsb = A.alloc(FLEN, BF16)
        V64 = cm_s[:, 0:NB]

        DMA("sp", "cst", cst[0:64, :], cos_d[:, 0:TT], [], ["cst"])
        DMA("sp", "snt", snt[0:64, :], sin_d[:, 0:TT], [], ["snt"])
        MEMSET(Vd4[:, :, :, 64:128], 1.0, ["Vd_ones"])
        MEMSET(KTr[64:128, :], 0.0, ["zpad"])
        MEMSET(Vm3[:, :, 64:65], 1.0, ["Vm_ones"])
        for pb_ in range(2):
            MEMSET(Qr2[pb_][64:128, :], 0.0, ["zpad"])
            MEMSET(qdp[pb_][0][64:128, :], 0.0, ["zpad"])
            MEMSET(qdp[pb_][1][0:64, :], 0.0, ["zpad"])
        DMA("sp", "cm", cm_s[0:32, :], cm_d[:, :], [], ["cm"])
        DMA("sp", "rb", rb_s[0:32, :], relbT_d[:, :], [], ["rb"])
        ACT(et_s[0:32, :], rb_s[0:32, :], AF.Exp, ["rb"], ["et"])
        for n in range(FLEN // 512):
            MM(ps[0][0:2, :], et_s[0:32, 0:2], cm_s[0:32, n * 512:(n + 1) * 512], True, True, ["et", "cm"], [("ps", 0)])
            CP("dve", fsb[0:2, n * 512:(n + 1) * 512], ps[0][0:2, :], [("ps", 0)], ["fsb"])
        DMA("sp", "fsb", fvec_d[:, :], fsb[0:2, :], ["fsb"], ["fvec"])
        LW = FLEN - 1
        MQ = ("sp", "sp")
        for hh in range(2):
            MEMSET(Mh[hh], 0.0, [("Mh", hh)])
        for hh in range(2):
            DMA(MQ[hh], ("Mh", hh), Mh[hh][0:1, 0:LW], fvec_d[hh:hh + 1, 0:LW], ["fvec"], [("Mh", hh)])
        for r_ in range(7):
            n_ = 1 << r_
            for hh in range(2):
                DMA(MQ[hh], ("Mh", hh), Mh[hh][n_:2 * n_, n_:LW], Mh[hh][0:n_, 0:LW - n_], [("Mh", hh)], [("Mh", hh)])

        def stats_g(srcs, ones_m, out_rstd, out_key, bank):
            for i, (src, p, rk) in enumerate(srcs):
                if i == 0:
                    ACT(out_rstd[0:p, :], src, AF.Square, rk, [out_key])
                else:
                    ACT(sg[i % 2][0:p, :], src, AF.Square, rk, [("sg", i % 2)])
            yield
            if len(srcs) > 1:
                for i, (src, p, rk) in enumerate(srcs):
                    if i > 0:
                        TTO("dve", out_rstd[0:p, :], out_rstd[0:p, :], sg[i % 2][0:p, :], ALU.add,
                            [out_key, ("sg", i % 2)], [out_key])
                yield
            MM(ps[bank][:, :], ones_m, out_rstd, True, True, [out_key, "cmat"], [("ps", bank)])
            yield
            ACT(out_rstd, ps[bank][:, :], AF.Ln, [("ps", bank), "cmat"], [out_key], bias=eps_c)
            ACT(out_rstd, out_rstd, AF.Exp, [out_key], [out_key], scale=-0.5)
            yield

        def b0(T):
            r, t = divmod(T, 4)
            pb = T % 2
            Qn, Qr = Qn2[pb], Qr2[pb]
            def load_h2t(T_):
                r_, t_ = divmod(T_, 4)
                DMA("sp", "h2t", h2t3, g1_d[t_].ap()[r_ * D:(r_ + 1) * D, :].rearrange("(c p) n -> p c n", p=128),
                    [("g1", t_)], ["h2t"])

            def load_cs(T_):
                DMA("sp", "cst", cst[0:64, :], cos_d[:, T_ * TT:(T_ + 1) * TT], [], ["cst"])
                DMA("sp", "snt", snt[0:64, :], sin_d[:, T_ * TT:(T_ + 1) * TT], [], ["snt"])


            def proj(bank, lo, ncols):
                for c in range(8):
                    MM(ps[bank][0:ncols, :], wsel3[:, c, lo:lo + ncols], h2t3[:, c, :], c == 0, c == 7,
                       ["h2t", "wsel"], [("ps", bank)])
                    if c == 3:
                        yield
                yield

            def rope(gcol, rs, rskey, dst, dkey):
                STT("dve", ra[0:64, :], pe_a[0:64, :], gv[0:64, gcol:gcol + 1], rs[0:64, :], ALU.mult, ALU.mult,
                    ["pe_a", "gv", rskey], ["ra"])
                STT("dve", rb[0:64, :], pe_b[0:64, :], gv[0:64, gcol + 1:gcol + 2], rs[0:64, :], ALU.mult, ALU.mult,
                    ["pe_b", "gv", rskey], ["rb_"])
                yield
                TTO("pool", ra[0:64, :], ra[0:64, :], cst[0:64, :], ALU.mult, ["ra", "cst"], ["ra"])
                TTO("pool", rb[0:64, :], rb[0:64, :], snt[0:64, :], ALU.mult, ["rb_", "snt"], ["rb_"])
                TTO("pool", dst, ra[0:64, :], rb[0:64, :], ALU.add, ["ra", "rb_"], [dkey])
                yield

            yield from proj(6, 0, 128)
            yield from stats_g([(ps[6][:, :], 128, [("ps", 6)])], bd64, rs0, "rs0", 7)
            STT("dve", qdp[pb][0][0:64, :], ps[6][0:64, :], gv[0:64, 24:25], rs0[0:64, :], ALU.mult, ALU.mult,
                [("ps", 6), "gv", "rs0"], [("qd", pb)])
            STT("dve", qdp[pb][1][64:128, :], ps[6][64:128, :], gv[64:128, 24:25], rs0[64:128, :], ALU.mult, ALU.mult,
                [("ps", 6), "gv", "rs0"], [("qd", pb)])
            yield
            yield from proj(7, 128, 128)
            yield from stats_g([(ps[7][:, :], 128, [("ps", 7)])], bd64, rs0, "rs0", 6)
            STT("dve", kdT[:, T * TT:(T + 1) * TT], ps[7][:, :], gv[:, 25:26], rs0, ALU.mult, ALU.mult,
                [("ps", 7), "gv", "rs0"], [("kd", T)])
            yield
            for sb_ in range(4):
                for c in range(8):
                    MM(ps[6][:, sb_ * 128:(sb_ + 1) * 128], h2t3[:, c, sb_ * 128:(sb_ + 1) * 128],
                       wsel3[:, c, 256:384], c == 0, c == 7, ["h2t", "wsel"], [("ps", 6)])
                yield
            CP("dve", Vd4[:, 4 * T:4 * T + 4, :, 0:64],
               ps[6][:, :].rearrange("p (b h c) -> p b h c", b=4, h=2), [("ps", 6)], [("Vd", T)])
            yield
            yield from proj(7, 384, 128)
            ACT(cq0_s, ps[7][:, :], AF.Copy, [("ps", 7)], ["cq0"])
            yield
            yield from proj(6, 512, 128)
            yield from stats_g([(cq0_s, 128, ["cq0"]), (ps[6][:, :], 128, [("ps", 6)])], o256, rs0, "rs0", 7)
            STT("dve", cqn3[:, 0, :], cq0_s, gv[:, 26:27], rs0, ALU.mult, ALU.mult, ["cq0", "gv", "rs0"], ["cqn"])
            STT("dve", cqn3[:, 1, :], ps[6][:, :], gv[:, 27:28], rs0, ALU.mult, ALU.mult, [("ps", 6), "gv", "rs0"], ["cqn"])
            yield
            yield from proj(7, 640, 128)
            yield from stats_g([(ps[7][:, :], 128, [("ps", 7)])], o128, rs0, "rs0", 6)
            STT("dve", ckvn, ps[7][:, :], gv[:, 28:29], rs0, ALU.mult, ALU.mult, [("ps", 7), "gv", "rs0"], ["ckvn"])
            yield
            yield from proj(6, 768, 64)
            ACT(pe_a[0:64, :], ps[6][0:64, :], AF.Copy, [("ps", 6)], ["pe_a"])
            yield
            yield from proj(7, 832, 64)
            CP("dve", pe_b[0:64, :], ps[7][0:64, :], [("ps", 7)], ["pe_b"])
            if T + 1 < SEQ // TT:
                load_h2t(T + 1)
            yield
            MM(ps[6][:, :], wkvb[:, 0:128], ckvn, True, True, ["wkvb", "ckvn"], [("ps", 6)])
            yield
            yield from stats_g([(ps[6][:, :], 128, [("ps", 6)]), (pe_a[0:64, :], 64, ["pe_a"])], o192, rs1, "rs1", 7)
            STT("dve", KTn[:, T * TT:(T + 1) * TT], ps[6][:, :], gv[:, 32:33], rs1, ALU.mult, ALU.mult,
                [("ps", 6), "gv", "rs1"], [("KTn", T)])
            yield
            yield from rope(33, rs1, "rs1", KTr[0:64, T * TT:(T + 1) * TT], ("KTr", T))
            for sb_ in range(4):
                MM(ps[7][:, sb_ * 128:(sb_ + 1) * 128], ckvn[:, sb_ * 128:(sb_ + 1) * 128], wkvb[:, 128:256],
                   True, True, ["wkvb", "ckvn"], [("ps", 7)])
            yield
            pv_ = ps[7][:, :].rearrange("p (b c) -> p b c", b=4)
            CP("dve", Vm3[:, 4 * T:4 * T + 4, 0:64], pv_[:, :, 0:64], [("ps", 7)], [("Vm", T)])
            CP("dve", Vm3[:, 4 * T:4 * T + 4, 65:128], pv_[:, :, 65:128], [("ps", 7)], [("Vm", T)])
            CP("dve", V64[:, 4 * T:4 * T + 4], pv_[:, :, 64], [("ps", 7)], [("V64", T), "cm"])
            yield
            for c in range(2):
                MM(ps[6][0:64, :], wqb3[:, c, 128:192], cqn3[:, c, :], c == 0, c == 1, ["wqb", "cqn"], [("ps", 6)])
            yield
            ACT(pe_a[0:64, :], ps[6][0:64, :], AF.Copy, [("ps", 6)], ["pe_a"])
            yield
            for c in range(2):
                MM(ps[7][0:64, :], wqb3[:, c, 192:256], cqn3[:, c, :], c == 0, c == 1, ["wqb", "cqn"], [("ps", 7)])
            yield
            CP("dve", pe_b[0:64, :], ps[7][0:64, :], [("ps", 7)], ["pe_b"])
            yield
            for c in range(2):
                MM(ps[6][:, :], wqb3[:, c, 0:128], cqn3[:, c, :], c == 0, c == 1, ["wqb", "cqn"], [("ps", 6)])
            yield
            yield from stats_g([(ps[6][:, :], 128, [("ps", 6)]), (pe_a[0:64, :], 64, ["pe_a"])], o192, rs1, "rs1", 7)
            STT("dve", Qn, ps[6][:, :], gv[:, 29:30], rs1, ALU.mult, ALU.mult, [("ps", 6), "gv", "rs1"], [("Qn", pb)])
            yield
            yield from rope(30, rs1, "rs1", Qr[0:64, :], ("Qr", pb))
            if T + 1 < SEQ // TT:
                load_cs(T + 1)
                yield

        LA = 2
        NPT = len(Pt)
        NQT = SEQ // TT
        stream = []
        job_id = 0
        for T in range(NQT):
            for kind, hh in (("mla", 0), ("dil", 0), ("dil", 1)):
                kbs = list(range(0, 4 * T + 4)) if kind == "mla" else list(range(max(0, 4 * T - 16), 4 * T + 4))
                for idx, kb in enumerate(kbs):
                    stream.append(dict(T=T, kind=kind, hh=hh, kb=kb, idx=idx, n=len(kbs), ob=3 + job_id % 2))
                job_id += 1
        first_pos = {}
        for p_, tk in enumerate(stream):
            first_pos.setdefault(tk["T"], p_)
        last_use = {0: -10, 1: -9, 2: -8, 5: -7}
        for p_, tk in enumerate(stream):
            four = True
            allowed = (0, 1, 2, 5) if four else (0, 1, 2)
            b_ = min(allowed, key=lambda x: last_use[x])
            last_use[b_] = p_
            tk["sb"] = b_
            tk["la"] = 3 if four else 2

        def s_stage(p_):
            tk = stream[p_]
            T, kb, hh = tk["T"], tk["kb"], tk["hh"]
            pb = T % 2
            sbk = tk["sb"]
            i = kb - 4 * T
            if tk["kind"] == "mla":
                c0 = 128 * max(i, 0)
                MM(ps[sbk][:, c0:512], KTn[:, kb * 128:(kb + 1) * 128], Qn2[pb][:, c0:512], True, False,
                   [("KTn", kb // 4), ("Qn", pb)], [("ps", sbk)])
                MM(ps[sbk][:, c0:512], KTr[:, kb * 128:(kb + 1) * 128], Qr2[pb][:, c0:512], False, True,
                   [("KTr", kb // 4), ("Qr", pb), "zpad"], [("ps", sbk)])
            else:
                c0 = 128 * max(i, 0)
                MM(ps[sbk][:, c0:512], kdT[:, kb * 128:(kb + 1) * 128], qdp[pb][hh][:, c0:512],
                   True, True, [("kd", kb // 4), ("qd", pb), "zpad"], [("ps", sbk)])

        def main_stage(p_, part):
            tk = stream[p_]
            T, kb, hh, idx, n, ob = tk["T"], tk["kb"], tk["hh"], tk["idx"], tk["n"], tk["ob"]
            sbk = tk["sb"]
            pslot = p_ % NPT
            pk = ("Pt", pslot)
            i = kb - 4 * T
            if tk["kind"] == "mla":
                c0 = 128 * max(i, 0)
                if part == 0:
                    ACT(Pt[pslot][:, c0:512], ps[sbk][:, c0:512], AF.Exp, [("ps", sbk)], [pk], scale=SC_M)
                    if i >= 0:
                        TTO("dve", Pt[pslot][:, c0:c0 + 128], Pt[pslot][:, c0:c0 + 128], tri, ALU.mult, [pk, "tri"], [pk])
                else:
                    MM(ps[ob][:, c0:512], Vm3[:, kb, :], Pt[pslot][:, c0:512], idx == 0, idx == n - 1,
                       [pk, ("Vm", kb // 4), "Vm_ones"], [("ps", ob)])
                    if idx == 0:
                        P.op("dve", lambda e, pslot=pslot, kb=kb: e.tensor_scalar(
                            out=acc64, in0=Pt[pslot], scalar1=V64[:, kb:kb + 1], scalar2=None, op0=ALU.mult),
                            reads=[pk, ("V64", kb // 4)], writes=["acc64"])
                    else:
                        STT("dve", acc64[:, c0:512], Pt[pslot][:, c0:512], V64[:, kb:kb + 1], acc64[:, c0:512],
                            ALU.mult, ALU.add, [pk, ("V64", kb // 4), "acc64"], ["acc64"])
            else:
                moff = 128 * (4 * T - kb) + 384
                c0 = 128 * max(i, 0)
                if part == 0:
                    ACT(Pt[pslot][:, c0:512], ps[sbk][:, c0:512], AF.Exp, [("ps", sbk)], [pk], scale=SC_D)
                    TTO("dve", Pt[pslot][:, c0:512], Pt[pslot][:, c0:512], Mh[hh][:, 127 + moff + c0:127 + moff + 512],
                        ALU.mult, [pk, ("Mh", hh)], [pk])
                else:
                    MM(ps[ob][:, c0:512], Vd4[:, kb, hh, :], Pt[pslot][:, c0:512], idx == 0, idx == n - 1,
                       [pk, ("Vd", kb // 4), "Vd_ones"], [("ps", ob)])

        def evac(tk):
            ob = tk["ob"]
            if tk["kind"] == "mla":
                ACT(ocp[2], ps[ob][:, :], AF.Copy, [("ps", ob)], [("ocp", 2)])
            else:
                CP("dve", ocp[tk["hh"]][0:65, :], ps[ob][0:65, :], [("ps", ob)], [("ocp", tk["hh"])])

        def tail(tk):
            T, hh = tk["T"], tk["hh"]
            u, col = T // 2, (T % 2) * TT
            if tk["kind"] == "mla":
                o2 = ocp[2]
                MM(ps[6][:, :], e64, acc64, True, True, ["acc64", "cmat"], [("ps", 6)])
                yield
                ACT(o2[64:65, :], o2[64:65, :], AF.Ln, [("ocp", 2)], [("ocp", 2)])
                ACT(o2[64:65, :], o2[64:65, :], AF.Exp, [("ocp", 2)], [("ocp", 2)], scale=-1.0)
                yield
                MM(ps[7][:, :], one_f[64:65, :], o2[64:65, :], True, True, [("ocp", 2), "cmat"], [("ps", 7)])
                yield
                ACT(o2[64:65, :], ps[6][64:65, :], AF.Copy, [("ps", 6), ("ocp", 2)], [("ocp", 2)])
                yield
                TTO("dve", o2, o2, ps[7][:, :], ALU.mult, [("ocp", 2), ("ps", 7)], [("ocp", 2)])
                DMA("sp", ("ocp", 2), b2_d[u][128:256, col:col + TT], ocp[2], [("ocp", 2)], [("b2", u)])
                yield
            else:
                o_ = ocp[hh]
                ACT(o_[64:65, :], o_[64:65, :], AF.Ln, [("ocp", hh)], [("ocp", hh)])
                ACT(o_[64:65, :], o_[64:65, :], AF.Exp, [("ocp", hh)], [("ocp", hh)], scale=-1.0)
                yield
                MM(ps[6][0:64, :], one_f[64:65, 0:64], o_[64:65, :], True, True, [("ocp", hh), "cmat"], [("ps", 6)])
                yield
                TTO("dve", o_[0:64, :], o_[0:64, :], ps[6][0:64, :], ALU.mult, [("ocp", hh), ("ps", 6)], [("ocp", hh)])
                DMA("sp", ("ocp", hh), b2_d[u][64 * hh:64 * hh + 64, col:col + TT], o_[0:64, :],
                    [("ocp", hh)], [("b2", u)])
                yield

        def cc_step(u):
            P.cc(("cc", "g2", u), lambda e, u=u: e.collective_compute(
                "AllGather", ALU.bypass, replica_groups=GROUPS, ins=[b2_d[u][:, :]],
                outs=[g2_d[u * 1024:(u + 1) * 1024, :]]),
                reads=[("b2", u)] + ([("g2", u - 1)] if u > 0 else []), writes=[("g2", u)])
            yield

        from collections import deque
        dq = deque()

        def cast_step(kind, r0, r1):
            src = wo_d if kind == "o" else wf_d[(2, kind)]
            DMA("pool", ("w2bf", kind), w2bf_d[kind][r0:r1, :], src[r0:r1, :], [], [("w2bf", kind)])
            yield

        cast_jobs = ([("g", 256 * i, 256 * i + 256) for i in range(4)] + [("u", 256 * i, 256 * i + 256) for i in range(4)]
                     + [("d", 704 * i, 704 * i + 704) for i in range(4)] + [("o", 0, D)])
        for _ in b0(0):
            pass
        NS = len(stream)
        pend_evac = {}
        next_s = [0]

        def emit_s_upto(p_):
            while next_s[0] < NS and next_s[0] - stream[next_s[0]]["la"] <= p_:
                s_stage(next_s[0])
                next_s[0] += 1

        emit_s_upto(-1)
        for p_ in range(NS):
            tk = stream[p_]
            T = tk["T"]
            if p_ == first_pos[T] and T + 1 < NQT:
                b0s = P.capture(b0(T + 1))
                if 1 <= T <= len(cast_jobs):
                    b0s[30:30] = P.capture(cast_step(*cast_jobs[T - 1]))
                dq.extend(b0s)
            if p_ in pend_evac:
                etk = pend_evac.pop(p_)
                evac(etk)
                dq.extend(P.capture(tail(etk)))
                if etk["kind"] == "dil" and etk["hh"] == 1 and etk["T"] % 2 == 1:
                    dq.extend(P.capture(cc_step(etk["T"] // 2)))
            emit_s_upto(p_)
            main_stage(p_, 0)
            if tk["idx"] == tk["n"] - 1:
                pend_evac[p_ + 2] = tk
            nxt_first = first_pos.get(T + 1, NS)
            rem = nxt_first - LA - 1 - p_
            if rem <= 0:
                k = len(dq)
            else:
                k = -(-len(dq) // rem)
            for _ in range(min(k, len(dq))):
                P.replay(dq.popleft())
            main_stage(p_, 1)
        for p_ in sorted(pend_evac):
            etk = pend_evac[p_]
            evac(etk)
            dq.extend(P.capture(tail(etk)))
            if etk["kind"] == "dil" and etk["hh"] == 1 and etk["T"] % 2 == 1:
                dq.extend(P.capture(cc_step(etk["T"] // 2)))
        while dq:
            P.replay(dq.popleft())
        P.barrier()

        A.release()
        sg = sg_A
        wl = load_ffn_weights(2)
        g2v = g2_d.ap().rearrange("(u k p) n -> u p k n", u=8, k=8, p=128)
        wo_v = w2bf_d["o"].ap().rearrange("(k p) d -> p k d", p=128)
        XK0, XK1 = ("xT", 0), ("xT", 1)

        def oT_load(t):
            col = (t % 2) * TT

            def ld(e, t=t, col=col):
                pid = nc.partition_id([e.engine])
                uu = (pid % 4) * 2 + (t // 2)
                return e.dma_start(out=oT3, in_=g2v[bass.ds(uu, 1), :, :, col:col + TT].rearrange("1 p k n -> p k n"))
            P.dma("sp", XK1, ld, reads=[("g2", 6 + t // 2)], writes=[XK1])

        def stats_pair_g():
            ACT(rstd, oT3[:, 0, :], AF.Square, [XK1], ["rstd"])
            ACT(rstdQ, oT3[:, 1, :], AF.Square, [XK1], ["rstdQ"])
            yield
            for r in range(1, 4):
                ACT(sg[0], oT3[:, 2 * r, :], AF.Square, [XK1], [("sg", 0)])
                ACT(sq2, oT3[:, 2 * r + 1, :], AF.Square, [XK1], ["sq2"])
                yield
                TTO("dve", rstd, rstd, sg[0], ALU.add, ["rstd", ("sg", 0)], ["rstd"])
                TTO("dve", rstdQ, rstdQ, sq2, ALU.add, ["rstdQ", "sq2"], ["rstdQ"])
                yield
            MM(ps[6][:, :], o512, rstd, True, True, ["rstd", "cmat"], [("ps", 6)])
            MM(ps[7][:, :], o512, rstdQ, True, True, ["rstdQ", "cmat"], [("ps", 7)])
            yield
            ACT(rstd, ps[6][:, :], AF.Ln, [("ps", 6), "cmat"], ["rstd"], bias=eps_c)
            ACT(rstdQ, ps[7][:, :], AF.Ln, [("ps", 7), "cmat"], ["rstdQ"], bias=eps_c)
            ACT(rstd, rstd, AF.Exp, ["rstd"], ["rstd"], scale=-0.5)
            ACT(rstdQ, rstdQ, AF.Exp, ["rstdQ"], ["rstdQ"], scale=-0.5)
            yield

        oT_load(0)
        DMA("sp", XK0, xT3, xs_v[:, :, 0:TT], [("xs", 0)], [XK0])
        DMA("sp", "woA", wo_resA, wo_v[:, :, 0:768], [("w2bf", "o")],
            ["woA", ("xin", 0), ("xin", 1), "h2b", ("ostg", 0), ("ostg", 1)])
        DMA("sp", "woB", wo_resB, wo_v[:, :, 768:1024], [("w2bf", "o")], ["woB", "actT"])
        for _ in stats_pair_g():
            pass
        for t in range(NT):
            for r in range(4):
                STT("dve", hT3[:, 2 * r, :], oT3[:, 2 * r, :], gv[:, 35 + r:36 + r], rstd, ALU.mult, ALU.mult,
                    [XK1, "gv", "rstd"], ["hT"])
                STT("dve", hT3[:, 2 * r + 1, :], oT3[:, 2 * r + 1, :], gv[:, 39 + r:40 + r], rstdQ, ALU.mult, ALU.mult,
                    [XK1, "gv", "rstdQ"], ["hT"])
            nxt = deque()
            if t + 1 < NT:
                oT_load(t + 1)
                nxt.extend(P.capture(stats_pair_g()))
            for w in wl[t * 5:(t + 1) * 5] if t + 1 < NT else wl[(NT - 1) * 5:]:
                w()
            for d in range(8):
                s = d % 2
                for k in range(8):
                    if d < 6:
                        MM(ps[4 + s][:, :], wo_resA[:, k, d * 128:(d + 1) * 128], hT3[:, k, :], k == 0, k == 7,
                           ["hT", "woA"], [("ps", 4 + s)])
                    else:
                        MM(ps[4 + s][:, :], wo_resB[:, k, (d - 6) * 128:(d - 5) * 128], hT3[:, k, :], k == 0, k == 7,
                           ["hT", "woB"], [("ps", 4 + s)])
                TTO("dve", xT3[:, d, :], ps[4 + s][:, :], xT3[:, d, :], ALU.add, [("ps", 4 + s), XK0], [XK0])
                if t == NT - 1:
                    if d == 0:
                        ACT(rstd, xT3[:, 0, :], AF.Square, [XK0], ["rstd"])
                    else:
                        ACT(sq2, xT3[:, d, :], AF.Square, [XK0], ["sq2"])
                        TTO("dve", rstd, rstd, sq2, ALU.add, ["rstd", "sq2"], ["rstd"])
                if d >= 3:
                    for _ in range(2):
                        if nxt:
                            P.replay(nxt.popleft())
            while nxt:
                P.replay(nxt.popleft())
            if t == NT - 1:
                MM(ps[6][:, :], o1024, rstd, True, True, ["rstd", "cmat"], [("ps", 6)])
                ACT(rstd, ps[6][:, :], AF.Ln, [("ps", 6), "cmat"], ["rstd"], bias=eps_c)
                ACT(rstd, rstd, AF.Exp, ["rstd"], ["rstd"], scale=-0.5)
            if t + 1 < NT:
                DMA("sp", XK0, xs_v[:, :, t * TT:(t + 1) * TT], xT3, [XK0], [("xs", t)])
                DMA("sp", XK0, xT3, xs_v[:, :, (t + 1) * TT:(t + 2) * TT], [("xs", t + 1)], [XK0])

        stores = []

        C2_ORDER = [NT - 1] + list(range(NT - 1))

        def pre_C(i):
            b = i % 2
            t = C2_ORDER[i]
            if i > 0:
                DMA("sp", ("xT", b), xTb[b], xs_v[:, :, t * TT:(t + 1) * TT], [("xs", t)], [("xT", b)])
                yield
                for _ in range(5):
                    yield "pad"
                yield from stats8_g(xTb[b], ("xT", b), rstd, "rstd", 6)

        def post_C(i):
            b = i % 2
            t = C2_ORDER[i]
            for s_ in range(4):
                slot = s_ % 2
                for half in range(2):
                    bank = 6 + half
                    for cc in range(4):
                        c = half * 4 + cc
                        P.op("pe", lambda e, bank=bank, cc=cc, c=c, s_=s_, b=b: e.transpose(
                            out=ps[bank][:, cc * 128:(cc + 1) * 128], in_=xTb[b][:, c, s_ * 128:(s_ + 1) * 128],
                            identity=ident), reads=[("xT", b), "ident"], writes=[("ps", bank)])
                yield
                ACT(ostg[slot][:, 0:512], ps[6][:, :], AF.Copy, [("ps", 6)], [("ostg", slot)])
                CP("dve", ostg[slot][:, 512:1024], ps[7][:, :], [("ps", 7)], [("ostg", slot)])
                yield
                r0 = t * TT + s_ * 128
                DMA("sp", ("ostg", slot), out_d[r0:r0 + 128, :], ostg[slot], [("ostg", slot)], [("out", t, s_)])
                yield

        for _ in pre_C(0):
            pass
        ht_stage(0, 16)
        dqC = deque()
        post_caps = []
        for t in range(NT):
            if t >= 1:
                dqC.extend(P.capture(post_C(t - 1)))
            if t + 1 < NT:
                dqC.extend(P.capture(pre_C(t + 1)))
            ffn_main(t % 2, dqC, (lambda t=t: ht_stage((t + 1) % 2, 16)) if t + 1 < NT else None)
        for _ in post_C(NT - 1):
            pass

        stores = [P.trk[("out", t, s_)][0] for t in range(NT) for s_ in range(4)]
        P.emit(final_waits=stores)
    return nc


def _t5_bucket(dist):
    max_exact = 16
    d = np.maximum(dist, 1).astype(np.float32)
    large = max_exact + (np.log(d / max_exact) / np.log(2048 / max_exact) * (32 - max_exact)).astype(np.int32)
    large = np.minimum(large, 31)
    return np.where(dist < max_exact, dist, large).astype(np.int32)


def _consts():
    ident = np.eye(128, dtype=np.float32)
    tri = (np.arange(128)[:, None] <= np.arange(128)[None, :]).astype(np.float32)
    dist = np.arange(0, 2049)
    mult = (dist <= 128).astype(np.float32) + ((dist % 4 == 0) & (dist <= 512)) + ((dist % 16 == 0) & (dist <= 2048))
    bucket = _t5_bucket(dist)
    cm = np.zeros((32, FLEN), np.float32)
    cm[bucket, dist + 511] = mult
    inv_freq = (np.float32(10000.0) ** (-np.arange(0, 64, 2, dtype=np.float32) / np.float32(64))).astype(np.float32)
    ang = (np.arange(SEQ, dtype=np.float32)[:, None] * inv_freq[None, :]).astype(np.float32)
    cos = np.cos(ang).astype(np.float32).T
    sin = np.sin(ang).astype(np.float32).T
    cos2 = np.ascontiguousarray(np.concatenate([cos, cos], 0))
    sin2 = np.ascontiguousarray(np.concatenate([-sin, sin], 0))
    return ident, tri, cm, cos2, sin2


def _prep_inputs(inputs):
    f = lambda k: np.asarray(inputs[k], dtype=np.float32)
    x = f("x")
    ident, tri, cm, cos2, sin2 = _consts()
    w_in = f("w_in")[0]
    w_qb = f("mla_w_q_b")[0]
    w_kvb = f("mla_w_kv_b")[0]
    w_out = f("w_out")[0]
    rel = f("rel_bias")
    swp = np.concatenate([np.arange(32, 64), np.arange(0, 32)])

    def pc(v, nc_):
        return np.asarray(v, np.float32).reshape(nc_, 128).T

    gq, gk = f("mla_q_norm")[0], f("mla_k_norm")[0]
    gv = np.zeros((128, 64), np.float32)
    gv[:, 0:8] = pc(f("ffn1_norm")[0], 8)
    gv[:, 8:16] = pc(f("mix_norm")[0], 8)
    gv[:, 16:24] = pc(f("ffn2_norm")[0], 8)
    gv[:, 24] = np.tile(f("dil_q_norm")[0], 2)
    gv[:, 25] = np.tile(f("dil_k_norm")[0], 2)
    gv[:, 26:28] = pc(f("mla_q_a_norm")[0], 2)
    gv[:, 28] = f("mla_kv_a_norm")[0]
    gv[:, 29] = gq[0:128]
    gv[0:64, 30] = gq[128:192]
    gv[0:64, 31] = gq[128:192][swp]
    gv[:, 32] = gk[0:128]
    gv[0:64, 33] = gk[128:192]
    gv[0:64, 34] = gk[128:192][swp]
    gv[:, 35:39] = pc(f("out_norm_dil")[0], 4)
    gv[:, 39:43] = pc(f("out_norm_mla")[0], 4)

    wmaps = {}
    for n, p in ((1, "ffn1"), (2, "ffn2")):
        wmaps["w%dg" % n] = np.ascontiguousarray(f(p + "_w_gate")[0])
        wmaps["w%du" % n] = np.ascontiguousarray(f(p + "_w_up")[0])
        wmaps["w%dd" % n] = np.ascontiguousarray(f(p + "_w_down")[0])
    rows = np.concatenate([np.concatenate([np.arange(128 * r, 128 * r + 128), 512 + np.arange(128 * r, 128 * r + 128)])
                           for r in range(4)])
    wo = np.ascontiguousarray(w_out[rows, :])
    maps = []
    for core in range(8):
        b, j = divmod(core, 4)
        kpe = w_in[:, 1920:1984]
        wsel = np.concatenate([w_in[:, 128 * j:128 * j + 128], w_in[:, 512 + 128 * j:512 + 128 * j + 128],
                               w_in[:, 1024 + 128 * j:1024 + 128 * j + 128], w_in[:, 1536:1792],
                               w_in[:, 1792:1920], kpe, kpe[:, swp]], axis=1)
        qr = w_qb[:, 192 * j + 128:192 * j + 192]
        wqb = np.concatenate([w_qb[:, 192 * j:192 * j + 128], qr, qr[:, swp]], axis=1)
        m = {
            "x": np.ascontiguousarray(x[b, j * TOK:(j + 1) * TOK, :]),
            "ident": ident, "gv": gv, "tri": tri, "cm": cm, "cos2": cos2, "sin2": sin2,
            "wsel": np.ascontiguousarray(wsel), "wqb": np.ascontiguousarray(wqb),
            "wkvb": np.ascontiguousarray(w_kvb[:, 256 * j:256 * j + 256]),
            "wo": wo, "relbT": np.ascontiguousarray(rel[2 * j:2 * j + 2, :].T),
        }
        m.update(wmaps)
        maps.append(m)
    return maps


_NC_CACHE = {}


def run_raw(inputs, stage="full", trace=False):
    if stage not in _NC_CACHE:
        _NC_CACHE[stage] = build_nc(stage)
    nc = _NC_CACHE[stage]
    maps = _prep_inputs(inputs)
    return run_bass_kernel_spmd(nc, maps, core_ids=list(range(8)), trace=trace)


def kernel(**inputs):
    res = run_raw(inputs)
    out = np.zeros((2, SEQ, D), np.float32)
    for core in range(8):
        b, j = divmod(core, 4)
        out[b, j * TOK:(j + 1) * TOK, :] = res.results[core]["out"]
    return out
```

```python
import math
import numpy as np
import ml_dtypes
import concourse.bass as bass
import concourse.mybir as mybir
from concourse.bass_utils import run_bass_kernel_spmd

F32 = mybir.dt.float32
BF16 = mybir.dt.bfloat16
AF = mybir.ActivationFunctionType
ALU = mybir.AluOpType

D = 1024
DFF = 2816
NFF = DFF // 128
SEQ = 8192
TOK = 2048
TT = 512
NT = TOK // TT
EPS = 1e-6
ENGS = ("pe", "act", "dve", "pool", "sp")


class H:
    __slots__ = ("eng", "sig", "val", "dma")

    def __init__(self, eng):
        self.eng = eng
        self.sig = False
        self.val = None
        self.dma = None


class Prog:
    def __init__(self, nc):
        self.nc = nc
        self.streams = {e: [] for e in ENGS}
        self.dma_cnt = {}
        self.trk = {}
        self.pending = {}
        self.all_dma = []
        self.cap = None

    def op(self, eng, fn, deps=(), reads=(), writes=(), _async=False):
        if self.cap is not None:
            self.cap.append(("op", eng, fn, tuple(reads), tuple(writes)))
            return None
        h = H(eng)
        deps = [d for d in deps if d is not None] + self.pending.pop(eng, [])
        trk = self.trk
        for k in list(reads) + list(writes):
            w = trk.setdefault(k, [None, {}])
            if w[0] is not None:
                deps.append(w[0])
        for k in writes:
            deps.extend(trk[k][1].values())
        for k in writes:
            trk[k][0] = h
            trk[k][1] = {}
        for k in reads:
            rk = id(h) if _async else eng
            trk[k][1][rk] = h
        deps = [d for d in deps if d is not h]
        for d in deps:
            if d.dma is None and d.eng != eng:
                d.sig = True
        self.streams[eng].append((h, fn, deps))
        return h

    def dma(self, eng, key, fn, deps=(), reads=(), writes=()):
        if self.cap is not None:
            self.cap.append(("dma", eng, key, fn, tuple(reads), tuple(writes)))
            return None
        h = self.op(eng, fn, deps, reads, writes, _async=True)
        self.dma_cnt[key] = self.dma_cnt.get(key, 0) + 16
        h.dma = (key, self.dma_cnt[key])
        self.all_dma.append(h)
        return h

    def capture(self, gen):
        steps = []
        self.cap = cur = []
        for y in gen:
            if cur or y == "pad":
                steps.append(cur)
            self.cap = cur = []
        if cur:
            steps.append(cur)
        self.cap = None
        return steps

    def replay(self, step):
        for rec in step:
            if rec[0] == "op":
                self.op(rec[1], rec[2], (), rec[3], rec[4])
            elif rec[0] == "dma":
                self.dma(rec[1], rec[2], rec[3], (), rec[4], rec[5])
            else:
                self.cc(rec[1], rec[2], (), rec[3], rec[4])

    def cc(self, key, fn, deps=(), reads=(), writes=()):
        if self.cap is not None:
            self.cap.append(("cc", key, fn, tuple(reads), tuple(writes)))
            return None
        h = self.op("pool", fn, deps, reads, writes, _async=True)
        self.dma_cnt[key] = self.dma_cnt.get(key, 0) + 1
        h.dma = (key, self.dma_cnt[key])
        self.all_dma.append(h)
        return h

    def barrier(self):
        deps = [h for h in self.all_dma if not (isinstance(h.dma[0], tuple) and h.dma[0][0] == "cc")]
        self.all_dma = []
        for e in ENGS:
            for (h, fn, d) in reversed(self.streams[e]):
                if h.dma is None:
                    deps.append(h)
                    break
        for d in deps:
            if d.dma is None:
                d.sig = True
        self.pending = {e: list(deps) for e in ENGS}

    def emit(self, final_waits=()):
        nc = self.nc
        for e in ENGS:
            c = 0
            for (h, fn, deps) in self.streams[e]:
                if h.dma is None and h.sig:
                    c += 1
                    h.val = c
        import contextlib
        with contextlib.ExitStack() as es:
            esem = {e: es.enter_context(nc.semaphore("s_" + e)) for e in ENGS}
            dsem = {k: es.enter_context(nc.semaphore("d%d" % i)) for i, k in enumerate(self.dma_cnt)}
            block = es.enter_context(nc.Block())

            def run(e, engobj):
                seen = {}
                for (h, fn, deps) in self.streams[e]:
                    for d in deps:
                        if d.dma is not None:
                            k, v = ("d", d.dma[0]), d.dma[1]
                            sem = dsem[d.dma[0]]
                        else:
                            if d.eng == e:
                                continue
                            k, v = ("e", d.eng), d.val
                            sem = esem[d.eng]
                        if seen.get(k, 0) >= v:
                            continue
                        seen[k] = v
                        engobj.wait_ge(sem, v)
                    ins = fn(engobj)
                    if h.dma is not None:
                        ins.then_inc(dsem[h.dma[0]], 1 if (isinstance(h.dma[0], tuple) and h.dma[0][0] == "cc") else 16)
                    elif h.sig:
                        ins.then_inc(esem[e], 1)
                if e == "sp":
                    for d in final_waits:
                        engobj.wait_ge(dsem[d.dma[0]], d.dma[1])

            @block.tensor
            def _(eng):
                run("pe", eng)

            @block.scalar
            def _(eng):
                run("act", eng)

            @block.vector
            def _(eng):
                run("dve", eng)

            @block.gpsimd
            def _(eng):
                run("pool", eng)

            @block.sync
            def _(eng):
                run("sp", eng)


class Arena:
    def __init__(self, t, nbytes):
        self.t = t
        self.views = {BF16: t, F32: t.bitcast(F32)}
        self.nbytes = nbytes
        self.off = 0
        self.marks = []

    def alloc(self, cols, dtype, parts=128):
        sz = 4 if dtype == F32 else 2
        self.off = (self.off + 63) // 64 * 64
        a = self.off
        self.last = a
        self.off += cols * sz
        assert self.off <= self.nbytes, ("SBUF arena overflow", self.off, self.nbytes)
        return self.views[dtype][0:parts, a // sz: a // sz + cols]

    def mark(self):
        self.marks.append(self.off)

    def release(self):
        self.off = self.marks.pop()


NB = SEQ // 128
SC_D = 0.125
SC_M = 192.0 ** -0.5
FLEN = 3072
MW = 2944


def build_nc(stage="full"):
    nc = bass.Bass("TRN2", target_bir_lowering=False)
    P = Prog(nc)

    def din(name, shape, dt=F32):
        return nc.dram_tensor(name, list(shape), dt, kind="ExternalInput")

    x_d = din("x", [TOK, D])
    ident_d = din("ident", [128, 128])
    gv_d = din("gv", [128, 64])
    wf_d = {(n, k): din("w%d%s" % (n, k), [D, DFF] if k != "d" else [DFF, D])
            for n in (1, 2) for k in ("g", "u", "d")}
    wsel_d = din("wsel", [D, 896])
    wqb_d = din("wqb", [256, 256])
    wkvb_d = din("wkvb", [128, 256])
    wo_d = din("wo", [D, D])
    relbT_d = din("relbT", [32, 2])
    cm_d = din("cm", [32, FLEN])
    tri_d = din("tri", [128, 128])
    cos_d = din("cos2", [64, SEQ])
    sin_d = din("sin2", [64, SEQ])
    out_d = nc.dram_tensor("out", [TOK, D], F32, kind="ExternalOutput")
    xs_d = nc.dram_tensor("xs", [D, TOK], F32)
    b1_d = [nc.dram_tensor("b1_%d" % t, [D, TT], BF16) for t in range(NT)]
    g1_d = [nc.dram_tensor("g1_%d" % t, [4 * D, TT], BF16) for t in range(NT)]
    b2_d = [nc.dram_tensor("b2_%d" % u, [256, 1024], F32) for u in range(8)]
    g2_d = nc.dram_tensor("g2", [8 * 1024, 1024], F32)
    fvec_d = nc.dram_tensor("fvec", [2, FLEN], BF16)
    w2bf_d = {"g": nc.dram_tensor("w2g_bf", [D, DFF], BF16), "u": nc.dram_tensor("w2u_bf", [D, DFF], BF16),
              "d": nc.dram_tensor("w2d_bf", [DFF, D], BF16), "o": nc.dram_tensor("wo_bf", [D, D], BF16)}
    GROUPS = [[0, 1, 2, 3], [4, 5, 6, 7]]
    wselbf_d = nc.dram_tensor("wsel_bf", [D, 896], BF16)
    wqbbf_d = nc.dram_tensor("wqb_bf", [256, 256], BF16)
    wkvbbf_d = nc.dram_tensor("wkvb_bf", [128, 256], BF16)

    import contextlib
    with contextlib.ExitStack() as es:
        ARENA_BYTES = 206 * 1024
        big = es.enter_context(nc.sbuf_tensor("arena", [128, ARENA_BYTES // 2], BF16))
        A = Arena(big, ARENA_BYTES)
        ps = [es.enter_context(nc.psum_tensor("ps%d" % i, [128, 512], F32)) for i in range(8)]

        def MM(out, lhsT, rhs, start, stop, reads, writes):
            return P.op("pe", lambda e: e.matmul(out, lhsT=lhsT, rhs=rhs, start=start, stop=stop),
                        reads=reads, writes=writes)

        def ACT(out, in_, func, reads, writes, scale=1.0, bias=None):
            if bias is None:
                return P.op("act", lambda e: e.activation(out=out, in_=in_, func=func, scale=scale),
                            reads=reads, writes=writes)
            return P.op("act", lambda e: e.activation(out=out, in_=in_, func=func, scale=scale, bias=bias),
                        reads=reads, writes=writes)

        def STT(eng, out, in0, scalar, in1, op0, op1, reads, writes):
            return P.op(eng, lambda e: e.scalar_tensor_tensor(out=out, in0=in0, scalar=scalar, in1=in1,
                                                              op0=op0, op1=op1), reads=reads, writes=writes)

        def TTO(eng, out, in0, in1, op, reads, writes):
            return P.op(eng, lambda e: e.tensor_tensor(out=out, in0=in0, in1=in1, op=op),
                        reads=reads, writes=writes)

        def CP(eng, out, in_, reads, writes):
            return P.op(eng, lambda e: e.tensor_copy(out=out, in_=in_), reads=reads, writes=writes)

        def RECIP(out, in_, reads, writes):
            return P.op("dve", lambda e: e.reciprocal(out=out, in_=in_), reads=reads, writes=writes)

        def MEMSET(ap, v, writes):
            return P.op("dve", lambda e: e.memset(ap, v), writes=writes)

        def DMA(q, key, out, in_, reads, writes):
            return P.dma(q, key, lambda e: e.dma_start(out=out, in_=in_), reads=reads, writes=writes)

        ident = A.alloc(128, F32)
        o1024 = A.alloc(128, F32)
        o512 = A.alloc(128, F32)
        o256 = A.alloc(128, F32)
        o192 = A.alloc(128, F32)
        o128 = A.alloc(128, F32)
        bd64 = A.alloc(128, F32)
        one_f = A.alloc(128, F32)
        one_bf = A.alloc(128, BF16)
        tri = A.alloc(128, BF16)
        gv = A.alloc(64, F32)
        eps_c = A.alloc(1, F32)
        e64 = A.alloc(128, F32)
        DMA("sp", "c_ident", ident, ident_d[:, :], [], ["ident"])
        DMA("sp", "c_gv", gv, gv_d[:, :], [], ["gv"])
        DMA("pool", "c_tri", tri, tri_d[:, :], [], ["tri"])
        for ap_, v_ in ((o1024, 1.0 / 1024), (o512, 1.0 / 512), (o256, 1.0 / 256), (o192, 1.0 / 192),
                        (o128, 1.0 / 128), (one_f, 1.0), (one_bf, 1.0), (eps_c, EPS), (bd64, 0.0)):
            MEMSET(ap_, v_, ["cmat"])
        MEMSET(e64, 0.0, ["cmat"])
        MEMSET(e64[:, 64:65], 1.0, ["cmat"])
        MEMSET(bd64[0:64, 0:64], 1.0 / 64, ["cmat"])
        MEMSET(bd64[64:128, 64:128], 1.0 / 64, ["cmat"])
        A.mark()

        wg3 = A.alloc(8 * DFF, BF16).rearrange("p (c f) -> p c f", c=8)
        wu3 = A.alloc(8 * DFF, BF16).rearrange("p (c f) -> p c f", c=8)
        wd3 = A.alloc(NFF * D, BF16).rearrange("p (f d) -> p f d", f=NFF)
        A.mark()
        xin = [A.alloc(TT, F32) for _ in range(2)]
        _xo = A.last - TT * 4
        h2b = A.alloc(8 * TT, BF16)
        assert A.last == _xo + 2 * TT * 4
        wo_resA = A.views[BF16][0:128, _xo // 2:_xo // 2 + 8 * 768].rearrange("p (k d) -> p k d", k=8)
        h2v = h2b.rearrange("p (c t) -> p c t", c=8)
        ostg = [h2b.bitcast(F32)[:, i * D:(i + 1) * D] for i in range(2)]
        xTb = [A.alloc(8 * TT, F32).rearrange("p (c t) -> p c t", c=8) for _ in range(2)]
        xT3 = xTb[0]
        oT3 = xTb[1]
        hT3 = A.alloc(8 * TT, BF16).rearrange("p (c t) -> p c t", c=8)
        GF = [(0, 6), (6, 12), (12, 17), (17, 22)]
        actT = A.alloc(6 * TT, BF16)
        actT3 = actT.rearrange("p (f t) -> p f t", f=6)
        wo_resB = actT[:, 0:2048].rearrange("p (k d) -> p k d", k=8)
        sg = [A.alloc(TT, F32) for _ in range(2)]
        rstd = A.alloc(TT, F32)
        rstdQ = A.alloc(TT, F32)
        sq2 = A.alloc(TT, F32)

        FP, FD = 512, 4

        def load_ffn_weights(n):
            ops = []
            if n == 1:
                srcs = {k: wf_d[(n, k)].ap() for k in "gud"}
                q, rd = "pool", {k: [] for k in "gud"}
            else:
                srcs = {k: w2bf_d[k].ap() for k in "gud"}
                q, rd = "act", {k: [("w2bf", k)] for k in "gud"}
            gv_ = srcs["g"].rearrange("(c p) f -> p c f", p=128)
            uv_ = srcs["u"].rearrange("(c p) f -> p c f", p=128)
            dv_ = srcs["d"].rearrange("(f p) d -> p f d", p=128)
            for i, f0 in enumerate(range(0, DFF, FP)):
                f1 = min(DFF, f0 + FP)
                ops.append(lambda i=i, f0=f0, f1=f1: DMA(q, ("wg", i), wg3[:, :, f0:f1], gv_[:, :, f0:f1], rd["g"], [("wg", i)]))
                ops.append(lambda i=i, f0=f0, f1=f1: DMA(q, ("wu", i), wu3[:, :, f0:f1], uv_[:, :, f0:f1], rd["u"], [("wu", i)]))
            for i, f0 in enumerate(range(0, NFF, FD)):
                f1 = min(NFF, f0 + FD)
                ops.append(lambda i=i, f0=f0, f1=f1: DMA(q, ("wd", i), wd3[:, f0:f1, :], dv_[:, f0:f1, :], rd["d"], [("wd", i)]))
            return ops

        def stats(srcs, ones_m, out_rstd, out_key, bank=6):
            for i, (src, p, rk) in enumerate(srcs):
                if i == 0:
                    ACT(out_rstd[0:p, :], src, AF.Square, rk, [out_key])
                else:
                    s_ = i % 2
                    ACT(sg[s_][0:p, :], src, AF.Square, rk, [("sg", s_)])
                    TTO("dve", out_rstd[0:p, :], out_rstd[0:p, :], sg[s_][0:p, :], ALU.add, [out_key, ("sg", s_)], [out_key])
            MM(ps[bank][:, :], ones_m, out_rstd, True, True, [out_key, "cmat"], [("ps", bank)])
            ACT(out_rstd, ps[bank][:, :], AF.Ln, [("ps", bank), "cmat"], [out_key], bias=eps_c)
            ACT(out_rstd, out_rstd, AF.Exp, [out_key], [out_key], scale=-0.5)

        from collections import deque

        def stats8_g(xt3, xkey, out_rstd, out_key, bank):
            ACT(out_rstd, xt3[:, 0, :], AF.Square, [xkey], [out_key])
            yield
            for c in range(1, 8):
                ACT(sq2, xt3[:, c, :], AF.Square, [xkey], ["sq2"])
                yield
                TTO("dve", out_rstd, out_rstd, sq2, ALU.add, [out_key, "sq2"], [out_key])
                yield
            MM(ps[bank][:, :], o1024, out_rstd, True, True, [out_key, "cmat"], [("ps", bank)])
            yield
            ACT(out_rstd, ps[bank][:, :], AF.Ln, [("ps", bank), "cmat"], [out_key], bias=eps_c)
            ACT(out_rstd, out_rstd, AF.Exp, [out_key], [out_key], scale=-0.5)
            yield

        def ht_stage(b, gcol):
            for c in range(8):
                STT("dve", hT3[:, c, :], xTb[b][:, c, :], gv[:, gcol + c:gcol + c + 1], rstd, ALU.mult, ALU.mult,
                    [("xT", b), "gv", "rstd"], ["hT"])

        def ffn_main(b, dq, nxt_ht=None):
            xt = xTb[b]
            xk = ("xT", b)
            slots = [2 * NFF + 8 * (len(GF) - 1)]

            def fill():
                if dq:
                    k = -(-len(dq) // max(1, slots[0]))
                    for _ in range(min(k, len(dq))):
                        P.replay(dq.popleft())
                slots[0] -= 1

            for gi, (f0, f1) in enumerate(GF):
                for f in range(f0, f1):
                    s_ = f % 2
                    for c in range(8):
                        MM(ps[s_][:, :], wg3[:, c, f * 128:(f + 1) * 128], hT3[:, c, :], c == 0, c == 7,
                           ["hT", ("wg", f * 128 // FP)], [("ps", s_)])
                    fill()
                    for c in range(8):
                        MM(ps[2 + s_][:, :], wu3[:, c, f * 128:(f + 1) * 128], hT3[:, c, :], c == 0, c == 7,
                           ["hT", ("wu", f * 128 // FP)], [("ps", 2 + s_)])
                    ACT(sg[s_], ps[s_][:, :], AF.Silu, [("ps", s_)], [("sg", s_)])
                    TTO("dve", actT3[:, f - f0, :], ps[2 + s_][:, :], sg[s_], ALU.mult,
                        [("ps", 2 + s_), ("sg", s_)], ["actT"])
                    fill()
                if gi == len(GF) - 1:
                    while dq:
                        P.replay(dq.popleft())
                    if nxt_ht is not None:
                        nxt_ht()
                for d in range(8):
                    s_ = d % 2
                    for f in range(f0, f1):
                        MM(ps[4 + s_][:, :], wd3[:, f, d * 128:(d + 1) * 128], actT3[:, f - f0, :], f == f0, f == f1 - 1,
                           ["actT", ("wd", f // FD)], [("ps", 4 + s_)])
                    STT("dve", xt[:, d, :], ps[4 + s_][:, :], 0.5, xt[:, d, :], ALU.mult, ALU.add,
                        [("ps", 4 + s_), xk], [xk])
                    if gi < len(GF) - 1:
                        fill()

        xs_v = xs_d.ap().rearrange("(c p) t -> p c t", p=128)

        def pre_A_loads(t):
            for k in range(2):
                s_, hf = divmod(k, 2)
                r0 = t * TT + s_ * 128
                DMA("sp", ("xin", k % 2), xin[k % 2], x_d[r0:r0 + 128, hf * 512:(hf + 1) * 512], [], [("xin", k % 2)])
                yield

        def pre_A(t):
            b = t % 2
            for k in range(8):
                s_, hf = divmod(k, 2)
                slot = k % 2
                for cc in range(4):
                    P.op("pe", lambda e, hf=hf, cc=cc, slot=slot: e.transpose(
                        out=ps[6 + hf][:, cc * 128:(cc + 1) * 128], in_=xin[slot][:, cc * 128:(cc + 1) * 128],
                        identity=ident), reads=[("xin", slot), "ident"], writes=[("ps", 6 + hf)])
                yield
                dst = xTb[b][:, hf * 4:hf * 4 + 4, s_ * 128:(s_ + 1) * 128]
                src = ps[6 + hf][:, :].rearrange("p (c t) -> p c t", c=4)
                if hf == 0:
                    ACT(dst, src, AF.Copy, [("ps", 6)], [("xT", b)])
                else:
                    CP("dve", dst, src, [("ps", 7)], [("xT", b)])
                if k + 2 < 8:
                    s2, hf2 = divmod(k + 2, 2)
                    r0 = t * TT + s2 * 128
                    DMA("sp", ("xin", slot), xin[slot], x_d[r0:r0 + 128, hf2 * 512:(hf2 + 1) * 512], [], [("xin", slot)])
                yield

        def pre_A_stats(t):
            b = t % 2
            yield from stats8_g(xTb[b], ("xT", b), rstd, "rstd", 6)

        def post_A(t):
            b = t % 2
            DMA("sp", ("xT", b), xs_v[:, :, t * TT:(t + 1) * TT], xTb[b], [("xT", b)], [("xs", t)])
            yield
            yield from stats8_g(xTb[b], ("xT", b), rstdQ, "rstdQ", 7)
            for c in range(8):
                STT("dve", h2v[:, c, :], xTb[b][:, c, :], gv[:, 8 + c:9 + c], rstdQ, ALU.mult, ALU.mult,
                    [("xT", b), "gv", "rstdQ"], ["h2b"])
                if c % 4 == 3:
                    yield

        def post_A2(t):
            DMA("sp", "h2b", b1_d[t].ap().rearrange("(c p) n -> p c n", p=128), h2v, ["h2b"], [("b1", t)])
            yield
            P.cc(("cc", "g1", t), lambda e, t=t: e.collective_compute(
                "AllGather", ALU.bypass, replica_groups=GROUPS, ins=[b1_d[t][:, :]], outs=[g1_d[t][:, :]]),
                reads=[("b1", t)], writes=[("g1", t)])
            yield

        wl1 = load_ffn_weights(1)
        gu = lambda i: [wl1[2 * i], wl1[2 * i + 1]]
        dd = lambda i: [wl1[12 + i]]
        order = gu(0) + gu(1) + dd(0) + dd(1) + gu(2) + dd(2) + gu(3) + gu(4) + dd(3) + dd(4) + gu(5) + dd(5)
        assert len(order) == len(wl1) == 18
        for w in order:
            w()
        for _ in pre_A_loads(0):
            pass
        for _ in pre_A(0):
            pass
        for _ in pre_A_stats(0):
            pass
        ht_stage(0, 0)
        dqA = deque()
        for t in range(NT):
            if t + 1 < NT:
                dqA.extend(P.capture(pre_A_loads(t + 1)))
            if t >= 1:
                dqA.extend(P.capture(post_A(t - 1)))
            if t + 1 < NT:
                dqA.extend(P.capture(pre_A(t + 1)))
            if t >= 1:
                dqA.extend(P.capture(post_A2(t - 1)))
            if t + 1 < NT:
                dqA.extend(P.capture(pre_A_stats(t + 1)))
            if t == 1:
                def precast():
                    DMA("pool", ("pc", "wsel"), wselbf_d[:, :], wsel_d[:, :], [], [("pc", "wsel")])
                    yield
                    DMA("pool", ("pc", "wqb"), wqbbf_d[:, :], wqb_d[:, :], [], [("pc", "wqb")])
                    DMA("pool", ("pc", "wkvb"), wkvbbf_d[:, :], wkvb_d[:, :], [], [("pc", "wkvb")])
                    yield
                dqA.extend(P.capture(precast()))
            ffn_main(t % 2, dqA, (lambda t=t: ht_stage((t + 1) % 2, 0)) if t + 1 < NT else None)
        AB = Arena(big, ARENA_BYTES)
        AB.off = A.marks[0]
        e_wsel3 = AB.alloc(8 * 896, BF16).rearrange("p (c f) -> p c f", c=8)
        e_wqb3 = AB.alloc(2 * 256, BF16).rearrange("p (c f) -> p c f", c=2)
        e_wkvb = AB.alloc(256, BF16)
        AB.alloc(FLEN, BF16)
        AB.alloc(FLEN, BF16)
        e_h2t3 = AB.alloc(8 * TT, BF16).rearrange("p (c t) -> p c t", c=8)
        e_end = AB.off
        WGK = [("wg", i) for i in range(6)]
        DMA("sp", "wsel", e_wsel3, wselbf_d.ap().rearrange("(c p) f -> p c f", p=128), [("pc", "wsel")], ["wsel"] + WGK)
        DMA("sp", "wqb", e_wqb3, wqbbf_d.ap().rearrange("(c p) f -> p c f", p=128), [("pc", "wqb")], ["wqb"] + WGK)
        DMA("sp", "wkvb", e_wkvb, wkvbbf_d[:, :], [("pc", "wkvb")], ["wkvb"] + WGK)
        DMA("sp", "h2t", e_h2t3, g1_d[0].ap()[0:D, :].rearrange("(c p) n -> p c n", p=128), [("g1", 0)], ["h2t"] + WGK)
        for _ in post_A(NT - 1):
            pass
        for _ in post_A2(NT - 1):
            pass
        P.barrier()

        A.release()
        A.release()
        A.mark()
        wsel3 = A.alloc(8 * 896, BF16).rearrange("p (c f) -> p c f", c=8)
        wqb3 = A.alloc(2 * 256, BF16).rearrange("p (c f) -> p c f", c=2)
        wkvb = A.alloc(256, BF16)
        Mh = [A.alloc(FLEN, BF16) for _ in range(2)]
        h2t3 = A.alloc(8 * TT, BF16).rearrange("p (c t) -> p c t", c=8)
        assert A.off == e_end and e_end <= A.marks[0] + 8 * DFF * 2, "early-load buffers must sit inside the FFN1 gate-weight region"
        qdp = [[A.alloc(TT, BF16) for _ in range(2)] for _ in range(2)]
        kdT = A.alloc(SEQ, BF16)
        Vd4 = A.alloc(NB * 2 * 128, BF16).rearrange("p (b h c) -> p b h c", b=NB, h=2)
        KTn = A.alloc(SEQ, BF16)
        KTr = A.alloc(SEQ, BF16)
        Vm3 = A.alloc(NB * 128, BF16).rearrange("p (b c) -> p b c", b=NB)
        cqn3 = A.alloc(2 * TT, BF16).rearrange("p (c t) -> p c t", c=2)
        ckvn = A.alloc(TT, BF16)
        Qn2 = [A.alloc(TT, BF16) for _ in range(2)]
        Qr2 = [A.alloc(TT, BF16) for _ in range(2)]
        sg_A = sg
        sg = [A.alloc(TT, F32) for _ in range(2)]
        rs0 = A.alloc(TT, F32)
        rs1 = A.alloc(TT, F32)
        ra = A.alloc(TT, F32)
        rb = A.alloc(TT, F32)
        cst = A.alloc(TT, F32)
        snt = A.alloc(TT, F32)
        Pt = [A.alloc(TT, BF16) for _ in range(6)]
        pe_a = A.alloc(TT, F32)
        pe_b = A.alloc(TT, F32)
        ocp = [A.alloc(TT, F32) for _ in range(3)]
        cq0_s = A.alloc(TT, F32)
        rinv = A.alloc(TT, F32)
        acc64 = A.alloc(TT, F32)
        rinvd = A.alloc(TT, F32)
        cm_s = A.alloc(FLEN, F32)
        rb_s = A.alloc(2, F32)
        et_s = A.alloc(2, F32)
        fsb = A.alloc(FLEN, BF16)
        V64 = cm_s[:, 0:NB]

        DMA("sp", "cst", cst[0:64, :], cos_d[:, 0:TT], [], ["cst"])
        DMA("sp", "snt", snt[0:64, :], sin_d[:, 0:TT], [], ["snt"])
        MEMSET(Vd4[:, :, :, 64:128], 1.0, ["Vd_ones"])
        MEMSET(KTr[64:128, :], 0.0, ["zpad"])
        MEMSET(Vm3[:, :, 64:65], 1.0, ["Vm_ones"])
        for pb_ in range(2):
            MEMSET(Qr2[pb_][64:128, :], 0.0, ["zpad"])
            MEMSET(qdp[pb_][0][64:128, :], 0.0, ["zpad"])
            MEMSET(qdp[pb_][1][0:64, :], 0.0, ["zpad"])
        DMA("sp", "cm", cm_s[0:32, :], cm_d[:, :], [], ["cm"])
        DMA("sp", "rb", rb_s[0:32, :], relbT_d[:, :], [], ["rb"])
        ACT(et_s[0:32, :], rb_s[0:32, :], AF.Exp, ["rb"], ["et"])
        for n in range(FLEN // 512):
            MM(ps[0][0:2, :], et_s[0:32, 0:2], cm_s[0:32, n * 512:(n + 1) * 512], True, True, ["et", "cm"], [("ps", 0)])
            CP("dve", fsb[0:2, n * 512:(n + 1) * 512], ps[0][0:2, :], [("ps", 0)], ["fsb"])
        DMA("sp", "fsb", fvec_d[:, :], fsb[0:2, :], ["fsb"], ["fvec"])
        LW = FLEN - 1
        MQ = ("sp", "sp")
        for hh in range(2):
            MEMSET(Mh[hh], 0.0, [("Mh", hh)])
        for hh in range(2):
            DMA(MQ[hh], ("Mh", hh), Mh[hh][0:1, 0:LW], fvec_d[hh:hh + 1, 0:LW], ["fvec"], [("Mh", hh)])
        for r_ in range(7):
            n_ = 1 << r_
            for hh in range(2):
                DMA(MQ[hh], ("Mh", hh), Mh[hh][n_:2 * n_, n_:LW], Mh[hh][0:n_, 0:LW - n_], [("Mh", hh)], [("Mh", hh)])

        def stats_g(srcs, ones_m, out_rstd, out_key, bank):
            for i, (src, p, rk) in enumerate(srcs):
                if i == 0:
                    ACT(out_rstd[0:p, :], src, AF.Square, rk, [out_key])
                else:
                    ACT(sg[i % 2][0:p, :], src, AF.Square, rk, [("sg", i % 2)])
            yield
            if len(srcs) > 1:
                for i, (src, p, rk) in enumerate(srcs):
                    if i > 0:
                        TTO("dve", out_rstd[0:p, :], out_rstd[0:p, :], sg[i % 2][0:p, :], ALU.add,
                            [out_key, ("sg", i % 2)], [out_key])
                yield
            MM(ps[bank][:, :], ones_m, out_rstd, True, True, [out_key, "cmat"], [("ps", bank)])
            yield
            ACT(out_rstd, ps[bank][:, :], AF.Ln, [("ps", bank), "cmat"], [out_key], bias=eps_c)
            ACT(out_rstd, out_rstd, AF.Exp, [out_key], [out_key], scale=-0.5)
            yield

        def b0(T):
            r, t = divmod(T, 4)
            pb = T % 2
            Qn, Qr = Qn2[pb], Qr2[pb]
            def load_h2t(T_):
                r_, t_ = divmod(T_, 4)
                DMA("sp", "h2t", h2t3, g1_d[t_].ap()[r_ * D:(r_ + 1) * D, :].rearrange("(c p) n -> p c n", p=128),
                    [("g1", t_)], ["h2t"])

            def load_cs(T_):
                DMA("sp", "cst", cst[0:64, :], cos_d[:, T_ * TT:(T_ + 1) * TT], [], ["cst"])
                DMA("sp", "snt", snt[0:64, :], sin_d[:, T_ * TT:(T_ + 1) * TT], [], ["snt"])


            def proj(bank, lo, ncols):
                for c in range(8):
                    MM(ps[bank][0:ncols, :], wsel3[:, c, lo:lo + ncols], h2t3[:, c, :], c == 0, c == 7,
                       ["h2t", "wsel"], [("ps", bank)])
                    if c == 3:
                        yield
                yield

            def rope(gcol, rs, rskey, dst, dkey):
                STT("dve", ra[0:64, :], pe_a[0:64, :], gv[0:64, gcol:gcol + 1], rs[0:64, :], ALU.mult, ALU.mult,
                    ["pe_a", "gv", rskey], ["ra"])
                STT("dve", rb[0:64, :], pe_b[0:64, :], gv[0:64, gcol + 1:gcol + 2], rs[0:64, :], ALU.mult, ALU.mult,
                    ["pe_b", "gv", rskey], ["rb_"])
                yield
                TTO("pool", ra[0:64, :], ra[0:64, :], cst[0:64, :], ALU.mult, ["ra", "cst"], ["ra"])
                TTO("pool", rb[0:64, :], rb[0:64, :], snt[0:64, :], ALU.mult, ["rb_", "snt"], ["rb_"])
                TTO("pool", dst, ra[0:64, :], rb[0:64, :], ALU.add, ["ra", "rb_"], [dkey])
                yield

            yield from proj(6, 0, 128)
            yield from stats_g([(ps[6][:, :], 128, [("ps", 6)])], bd64, rs0, "rs0", 7)
            STT("dve", qdp[pb][0][0:64, :], ps[6][0:64, :], gv[0:64, 24:25], rs0[0:64, :], ALU.mult, ALU.mult,
                [("ps", 6), "gv", "rs0"], [("qd", pb)])
            STT("dve", qdp[pb][1][64:128, :], ps[6][64:128, :], gv[64:128, 24:25], rs0[64:128, :], ALU.mult, ALU.mult,
                [("ps", 6), "gv", "rs0"], [("qd", pb)])
            yield
            yield from proj(7, 128, 128)
            yield from stats_g([(ps[7][:, :], 128, [("ps", 7)])], bd64, rs1, "rs1", 6)
            STT("dve", kdT[:, T * TT:(T + 1) * TT], ps[7][:, :], gv[:, 25:26], rs1, ALU.mult, ALU.mult,
                [("ps", 7), "gv", "rs1"], [("kd", T)])
            yield
            for sb_ in range(4):
                for c in range(8):
                    MM(ps[6][:, sb_ * 128:(sb_ + 1) * 128], h2t3[:, c, sb_ * 128:(sb_ + 1) * 128],
                       wsel3[:, c, 256:384], c == 0, c == 7, ["h2t", "wsel"], [("ps", 6)])
                yield
            CP("dve", Vd4[:, 4 * T:4 * T + 4, :, 0:64],
               ps[6][:, :].rearrange("p (b h c) -> p b h c", b=4, h=2), [("ps", 6)], [("Vd", T)])
            yield
            yield from proj(7, 384, 128)
            ACT(cq0_s, ps[7][:, :], AF.Copy, [("ps", 7)], ["cq0"])
            yield
            yield from proj(6, 512, 128)
            yield from stats_g([(cq0_s, 128, ["cq0"]), (ps[6][:, :], 128, [("ps", 6)])], o256, rs0, "rs0", 7)
            STT("dve", cqn3[:, 0, :], cq0_s, gv[:, 26:27], rs0, ALU.mult, ALU.mult, ["cq0", "gv", "rs0"], ["cqn"])
            STT("dve", cqn3[:, 1, :], ps[6][:, :], gv[:, 27:28], rs0, ALU.mult, ALU.mult, [("ps", 6), "gv", "rs0"], ["cqn"])
            yield
            yield from proj(7, 640, 128)
            yield from stats_g([(ps[7][:, :], 128, [("ps", 7)])], o128, rs1, "rs1", 6)
            STT("dve", ckvn, ps[7][:, :], gv[:, 28:29], rs1, ALU.mult, ALU.mult, [("ps", 7), "gv", "rs1"], ["ckvn"])
            yield
            yield from proj(6, 768, 64)
            ACT(pe_a[0:64, :], ps[6][0:64, :], AF.Copy, [("ps", 6)], ["pe_a"])
            yield
            yield from proj(7, 832, 64)
            CP("dve", pe_b[0:64, :], ps[7][0:64, :], [("ps", 7)], ["pe_b"])
            if T + 1 < SEQ // TT:
                load_h2t(T + 1)
            yield
            MM(ps[6][:, :], wkvb[:, 0:128], ckvn, True, True, ["wkvb", "ckvn"], [("ps", 6)])
            yield
            yield from stats_g([(ps[6][:, :], 128, [("ps", 6)]), (pe_a[0:64, :], 64, ["pe_a"])], o192, rs0, "rs0", 7)
            STT("dve", KTn[:, T * TT:(T + 1) * TT], ps[6][:, :], gv[:, 32:33], rs0, ALU.mult, ALU.mult,
                [("ps", 6), "gv", "rs0"], [("KTn", T)])
            yield
            yield from rope(33, rs0, "rs0", KTr[0:64, T * TT:(T + 1) * TT], ("KTr", T))
            for sb_ in range(4):
                MM(ps[7][:, sb_ * 128:(sb_ + 1) * 128], ckvn[:, sb_ * 128:(sb_ + 1) * 128], wkvb[:, 128:256],
                   True, True, ["wkvb", "ckvn"], [("ps", 7)])
            yield
            pv_ = ps[7][:, :].rearrange("p (b c) -> p b c", b=4)
            CP("dve", Vm3[:, 4 * T:4 * T + 4, 0:64], pv_[:, :, 0:64], [("ps", 7)], [("Vm", T)])
            CP("dve", Vm3[:, 4 * T:4 * T + 4, 65:128], pv_[:, :, 65:128], [("ps", 7)], [("Vm", T)])
            CP("dve", V64[:, 4 * T:4 * T + 4], pv_[:, :, 64], [("ps", 7)], [("V64", T), "cm"])
            yield
            for c in range(2):
                MM(ps[6][0:64, :], wqb3[:, c, 128:192], cqn3[:, c, :], c == 0, c == 1, ["wqb", "cqn"], [("ps", 6)])
            yield
            ACT(pe_a[0:64, :], ps[6][0:64, :], AF.Copy, [("ps", 6)], ["pe_a"])
            yield
            for c in range(2):
                MM(ps[7][0:64, :], wqb3[:, c, 192:256], cqn3[:, c, :], c == 0, c == 1, ["wqb", "cqn"], [("ps", 7)])
            yield
            CP("dve", pe_b[0:64, :], ps[7][0:64, :], [("ps", 7)], ["pe_b"])
            yield
            for c in range(2):
                MM(ps[6][:, :], wqb3[:, c, 0:128], cqn3[:, c, :], c == 0, c == 1, ["wqb", "cqn"], [("ps", 6)])
            yield
            yield from stats_g([(ps[6][:, :], 128, [("ps", 6)]), (pe_a[0:64, :], 64, ["pe_a"])], o192, rs1, "rs1", 7)
            STT("dve", Qn, ps[6][:, :], gv[:, 29:30], rs1, ALU.mult, ALU.mult, [("ps", 6), "gv", "rs1"], [("Qn", pb)])
            yield
            yield from rope(30, rs1, "rs1", Qr[0:64, :], ("Qr", pb))
            if T + 1 < SEQ // TT:
                load_cs(T + 1)
                yield

        LA = 2
        NPT = len(Pt)
        NQT = SEQ // TT
        stream = []
        job_id = 0
        for T in range(NQT):
            for kind, hh in (("mla", 0), ("dil", 0), ("dil", 1)):
                kbs = list(range(0, 4 * T + 4)) if kind == "mla" else list(range(max(0, 4 * T - 16), 4 * T + 4))
                for idx, kb in enumerate(kbs):
                    stream.append(dict(T=T, kind=kind, hh=hh, kb=kb, idx=idx, n=len(kbs), ob=3 + job_id % 2))
                job_id += 1
        first_pos = {}
        for p_, tk in enumerate(stream):
            first_pos.setdefault(tk["T"], p_)
        last_use = {0: -10, 1: -9, 2: -8, 5: -7}
        for p_, tk in enumerate(stream):
            four = True
            allowed = (0, 1, 2, 5) if four else (0, 1, 2)
            b_ = min(allowed, key=lambda x: last_use[x])
            last_use[b_] = p_
            tk["sb"] = b_
            tk["la"] = 3 if four else 2

        def s_stage(p_):
            tk = stream[p_]
            T, kb, hh = tk["T"], tk["kb"], tk["hh"]
            pb = T % 2
            sbk = tk["sb"]
            i = kb - 4 * T
            if tk["kind"] == "mla":
                c0 = 128 * max(i, 0)
                MM(ps[sbk][:, c0:512], KTn[:, kb * 128:(kb + 1) * 128], Qn2[pb][:, c0:512], True, False,
                   [("KTn", kb // 4), ("Qn", pb)], [("ps", sbk)])
                MM(ps[sbk][:, c0:512], KTr[:, kb * 128:(kb + 1) * 128], Qr2[pb][:, c0:512], False, True,
                   [("KTr", kb // 4), ("Qr", pb), "zpad"], [("ps", sbk)])
            else:
                c0 = 128 * max(i, 0)
                MM(ps[sbk][:, c0:512], kdT[:, kb * 128:(kb + 1) * 128], qdp[pb][hh][:, c0:512],
                   True, True, [("kd", kb // 4), ("qd", pb), "zpad"], [("ps", sbk)])

        def main_stage(p_, part):
            tk = stream[p_]
            T, kb, hh, idx, n, ob = tk["T"], tk["kb"], tk["hh"], tk["idx"], tk["n"], tk["ob"]
            sbk = tk["sb"]
            pslot = p_ % NPT
            pk = ("Pt", pslot)
            i = kb - 4 * T
            if tk["kind"] == "mla":
                c0 = 128 * max(i, 0)
                if part == 0:
                    ACT(Pt[pslot][:, c0:512], ps[sbk][:, c0:512], AF.Exp, [("ps", sbk)], [pk], scale=SC_M)
                    if i >= 0:
                        TTO("dve", Pt[pslot][:, c0:c0 + 128], Pt[pslot][:, c0:c0 + 128], tri, ALU.mult, [pk, "tri"], [pk])
                else:
                    MM(ps[ob][:, c0:512], Vm3[:, kb, :], Pt[pslot][:, c0:512], idx == 0, idx == n - 1,
                       [pk, ("Vm", kb // 4), "Vm_ones"], [("ps", ob)])
                    if idx == 0:
                        P.op("dve", lambda e, pslot=pslot, kb=kb: e.tensor_scalar(
                            out=acc64, in0=Pt[pslot], scalar1=V64[:, kb:kb + 1], scalar2=None, op0=ALU.mult),
                            reads=[pk, ("V64", kb // 4)], writes=["acc64"])
                    else:
                        STT("dve", acc64[:, c0:512], Pt[pslot][:, c0:512], V64[:, kb:kb + 1], acc64[:, c0:512],
                            ALU.mult, ALU.add, [pk, ("V64", kb // 4), "acc64"], ["acc64"])
            else:
                moff = 128 * (4 * T - kb) + 384
                c0 = 128 * max(i, 0)
                if part == 0:
                    ACT(Pt[pslot][:, c0:512], ps[sbk][:, c0:512], AF.Exp, [("ps", sbk)], [pk], scale=SC_D)
                    TTO("dve", Pt[pslot][:, c0:512], Pt[pslot][:, c0:512], Mh[hh][:, 127 + moff + c0:127 + moff + 512],
                        ALU.mult, [pk, ("Mh", hh)], [pk])
                else:
                    MM(ps[ob][:, c0:512], Vd4[:, kb, hh, :], Pt[pslot][:, c0:512], idx == 0, idx == n - 1,
                       [pk, ("Vd", kb // 4), "Vd_ones"], [("ps", ob)])

        def evac(tk):
            ob = tk["ob"]
            if tk["kind"] == "mla":
                ACT(ocp[2], ps[ob][:, :], AF.Copy, [("ps", ob)], [("ocp", 2)])
            else:
                CP("dve", ocp[tk["hh"]][0:65, :], ps[ob][0:65, :], [("ps", ob)], [("ocp", tk["hh"])])

        def tail(tk):
            T, hh = tk["T"], tk["hh"]
            u, col = T // 2, (T % 2) * TT
            if tk["kind"] == "mla":
                o2 = ocp[2]
                MM(ps[6][:, :], e64, acc64, True, True, ["acc64", "cmat"], [("ps", 6)])
                yield
                ACT(o2[64:65, :], o2[64:65, :], AF.Ln, [("ocp", 2)], [("ocp", 2)])
                ACT(o2[64:65, :], o2[64:65, :], AF.Exp, [("ocp", 2)], [("ocp", 2)], scale=-1.0)
                yield
                MM(ps[7][:, :], one_f[64:65, :], o2[64:65, :], True, True, [("ocp", 2), "cmat"], [("ps", 7)])
                yield
                ACT(o2[64:65, :], ps[6][64:65, :], AF.Copy, [("ps", 6), ("ocp", 2)], [("ocp", 2)])
                yield
                TTO("dve", o2, o2, ps[7][:, :], ALU.mult, [("ocp", 2), ("ps", 7)], [("ocp", 2)])
                DMA("sp", ("ocp", 2), b2_d[u][128:256, col:col + TT], ocp[2], [("ocp", 2)], [("b2", u)])
                yield
            else:
                o_ = ocp[hh]
                ACT(o_[64:65, :], o_[64:65, :], AF.Ln, [("ocp", hh)], [("ocp", hh)])
                ACT(o_[64:65, :], o_[64:65, :], AF.Exp, [("ocp", hh)], [("ocp", hh)], scale=-1.0)
                yield
                MM(ps[6][0:64, :], one_f[64:65, 0:64], o_[64:65, :], True, True, [("ocp", hh), "cmat"], [("ps", 6)])
                yield
                TTO("dve", o_[0:64, :], o_[0:64, :], ps[6][0:64, :], ALU.mult, [("ocp", hh), ("ps", 6)], [("ocp", hh)])
                DMA("sp", ("ocp", hh), b2_d[u][64 * hh:64 * hh + 64, col:col + TT], o_[0:64, :],
                    [("ocp", hh)], [("b2", u)])
                yield

        def cc_step(u):
            P.cc(("cc", "g2", u), lambda e, u=u: e.collective_compute(
                "AllGather", ALU.bypass, replica_groups=GROUPS, ins=[b2_d[u][:, :]],
                outs=[g2_d[u * 1024:(u + 1) * 1024, :]]),
                reads=[("b2", u)] + ([("g2", u - 1)] if u > 0 else []), writes=[("g2", u)])
            yield

        from collections import deque
        dq = deque()

        def cast_step(kind, r0, r1):
            src = wo_d if kind == "o" else wf_d[(2, kind)]
            DMA("pool", ("w2bf", kind), w2bf_d[kind][r0:r1, :], src[r0:r1, :], [], [("w2bf", kind)])
            yield

        cast_jobs = ([("g", 256 * i, 256 * i + 256) for i in range(4)] + [("u", 256 * i, 256 * i + 256) for i in range(4)]
                     + [("d", 704 * i, 704 * i + 704) for i in range(4)] + [("o", 0, D)])
        for _ in b0(0):
            pass
        NS = len(stream)
        pend_evac = {}
        next_s = [0]

        def emit_s_upto(p_):
            while next_s[0] < NS and next_s[0] - stream[next_s[0]]["la"] <= p_:
                s_stage(next_s[0])
                next_s[0] += 1

        emit_s_upto(-1)
        for p_ in range(NS):
            tk = stream[p_]
            T = tk["T"]
            if p_ == first_pos[T] and T + 1 < NQT:
                b0s = P.capture(b0(T + 1))
                if 1 <= T <= len(cast_jobs):
                    b0s[30:30] = P.capture(cast_step(*cast_jobs[T - 1]))
                dq.extend(b0s)
            if p_ in pend_evac:
                etk = pend_evac.pop(p_)
                evac(etk)
                dq.extend(P.capture(tail(etk)))
                if etk["kind"] == "dil" and etk["hh"] == 1 and etk["T"] % 2 == 1:
                    dq.extend(P.capture(cc_step(etk["T"] // 2)))
            emit_s_upto(p_)
            main_stage(p_, 0)
            if tk["idx"] == tk["n"] - 1:
                pend_evac[p_ + 2] = tk
            nxt_first = first_pos.get(T + 1, NS)
            rem = nxt_first - LA - 1 - p_
            if rem <= 0:
                k = len(dq)
            else:
                k = -(-len(dq) // rem)
            for _ in range(min(k, len(dq))):
                P.replay(dq.popleft())
            main_stage(p_, 1)
        for p_ in sorted(pend_evac):
            etk = pend_evac[p_]
            evac(etk)
            dq.extend(P.capture(tail(etk)))
            if etk["kind"] == "dil" and etk["hh"] == 1 and etk["T"] % 2 == 1:
                dq.extend(P.capture(cc_step(etk["T"] // 2)))
        while dq:
            P.replay(dq.popleft())
        P.barrier()

        A.release()
        sg = sg_A
        wl = load_ffn_weights(2)
        g2v = g2_d.ap().rearrange("(u k p) n -> u p k n", u=8, k=8, p=128)
        wo_v = w2bf_d["o"].ap().rearrange("(k p) d -> p k d", p=128)
        XK0, XK1 = ("xT", 0), ("xT", 1)

        def oT_load(t):
            col = (t % 2) * TT

            def ld(e, t=t, col=col):
                pid = nc.partition_id([e.engine])
                uu = (pid % 4) * 2 + (t // 2)
                return e.dma_start(out=oT3, in_=g2v[bass.ds(uu, 1), :, :, col:col + TT].rearrange("1 p k n -> p k n"))
            P.dma("sp", XK1, ld, reads=[("g2", 6 + t // 2)], writes=[XK1])

        def stats_pair_g():
            ACT(rstd, oT3[:, 0, :], AF.Square, [XK1], ["rstd"])
            ACT(rstdQ, oT3[:, 1, :], AF.Square, [XK1], ["rstdQ"])
            yield
            for r in range(1, 4):
                ACT(sg[0], oT3[:, 2 * r, :], AF.Square, [XK1], [("sg", 0)])
                ACT(sq2, oT3[:, 2 * r + 1, :], AF.Square, [XK1], ["sq2"])
                yield
                TTO("dve", rstd, rstd, sg[0], ALU.add, ["rstd", ("sg", 0)], ["rstd"])
                TTO("dve", rstdQ, rstdQ, sq2, ALU.add, ["rstdQ", "sq2"], ["rstdQ"])
                yield
            MM(ps[6][:, :], o512, rstd, True, True, ["rstd", "cmat"], [("ps", 6)])
            MM(ps[7][:, :], o512, rstdQ, True, True, ["rstdQ", "cmat"], [("ps", 7)])
            yield
            ACT(rstd, ps[6][:, :], AF.Ln, [("ps", 6), "cmat"], ["rstd"], bias=eps_c)
            ACT(rstdQ, ps[7][:, :], AF.Ln, [("ps", 7), "cmat"], ["rstdQ"], bias=eps_c)
            ACT(rstd, rstd, AF.Exp, ["rstd"], ["rstd"], scale=-0.5)
            ACT(rstdQ, rstdQ, AF.Exp, ["rstdQ"], ["rstdQ"], scale=-0.5)
            yield

        oT_load(0)
        DMA("sp", XK0, xT3, xs_v[:, :, 0:TT], [("xs", 0)], [XK0])
        DMA("sp", "woA", wo_resA, wo_v[:, :, 0:768], [("w2bf", "o")],
            ["woA", ("xin", 0), ("xin", 1), "h2b", ("ostg", 0), ("ostg", 1)])
        DMA("sp", "woB", wo_resB, wo_v[:, :, 768:1024], [("w2bf", "o")], ["woB", "actT"])
        for _ in stats_pair_g():
            pass
        for t in range(NT):
            for r in range(4):
                STT("dve", hT3[:, 2 * r, :], oT3[:, 2 * r, :], gv[:, 35 + r:36 + r], rstd, ALU.mult, ALU.mult,
                    [XK1, "gv", "rstd"], ["hT"])
                STT("dve", hT3[:, 2 * r + 1, :], oT3[:, 2 * r + 1, :], gv[:, 39 + r:40 + r], rstdQ, ALU.mult, ALU.mult,
                    [XK1, "gv", "rstdQ"], ["hT"])
            nxt = deque()
            if t + 1 < NT:
                oT_load(t + 1)
                nxt.extend(P.capture(stats_pair_g()))
            for w in wl[t * 5:(t + 1) * 5] if t + 1 < NT else wl[(NT - 1) * 5:]:
                w()
            for d in range(8):
                s = d % 2
                for k in range(8):
                    if d < 6:
                        MM(ps[4 + s][:, :], wo_resA[:, k, d * 128:(d + 1) * 128], hT3[:, k, :], k == 0, k == 7,
                           ["hT", "woA"], [("ps", 4 + s)])
                    else:
                        MM(ps[4 + s][:, :], wo_resB[:, k, (d - 6) * 128:(d - 5) * 128], hT3[:, k, :], k == 0, k == 7,
                           ["hT", "woB"], [("ps", 4 + s)])
                TTO("dve", xT3[:, d, :], ps[4 + s][:, :], xT3[:, d, :], ALU.add, [("ps", 4 + s), XK0], [XK0])
                if t == NT - 1:
                    if d == 0:
                        ACT(rstd, xT3[:, 0, :], AF.Square, [XK0], ["rstd"])
                    else:
                        ACT(sq2, xT3[:, d, :], AF.Square, [XK0], ["sq2"])
                        TTO("dve", rstd, rstd, sq2, ALU.add, ["rstd", "sq2"], ["rstd"])
                if d >= 3:
                    for _ in range(2):
                        if nxt:
                            P.replay(nxt.popleft())
            while nxt:
                P.replay(nxt.popleft())
            if t == NT - 1:
                MM(ps[6][:, :], o1024, rstd, True, True, ["rstd", "cmat"], [("ps", 6)])
                ACT(rstd, ps[6][:, :], AF.Ln, [("ps", 6), "cmat"], ["rstd"], bias=eps_c)
                ACT(rstd, rstd, AF.Exp, ["rstd"], ["rstd"], scale=-0.5)
            if t + 1 < NT:
                DMA("sp", XK0, xs_v[:, :, t * TT:(t + 1) * TT], xT3, [XK0], [("xs", t)])
                DMA("sp", XK0, xT3, xs_v[:, :, (t + 1) * TT:(t + 2) * TT], [("xs", t + 1)], [XK0])

        stores = []

        C2_ORDER = [NT - 1] + list(range(NT - 1))

        def pre_C(i):
            b = i % 2
            t = C2_ORDER[i]
            if i > 0:
                DMA("sp", ("xT", b), xTb[b], xs_v[:, :, t * TT:(t + 1) * TT], [("xs", t)], [("xT", b)])
                yield
                for _ in range(5):
                    yield "pad"
                yield from stats8_g(xTb[b], ("xT", b), rstd, "rstd", 6)

        def post_C(i):
            b = i % 2
            t = C2_ORDER[i]
            for s_ in range(4):
                slot = s_ % 2
                for half in range(2):
                    bank = 6 + half
                    for cc in range(4):
                        c = half * 4 + cc
                        P.op("pe", lambda e, bank=bank, cc=cc, c=c, s_=s_, b=b: e.transpose(
                            out=ps[bank][:, cc * 128:(cc + 1) * 128], in_=xTb[b][:, c, s_ * 128:(s_ + 1) * 128],
                            identity=ident), reads=[("xT", b), "ident"], writes=[("ps", bank)])
                yield
                ACT(ostg[slot][:, 0:512], ps[6][:, :], AF.Copy, [("ps", 6)], [("ostg", slot)])
                CP("dve", ostg[slot][:, 512:1024], ps[7][:, :], [("ps", 7)], [("ostg", slot)])
                yield
                r0 = t * TT + s_ * 128
                DMA("sp", ("ostg", slot), out_d[r0:r0 + 128, :], ostg[slot], [("ostg", slot)], [("out", t, s_)])
                yield

        for _ in pre_C(0):
            pass
        ht_stage(0, 16)
        dqC = deque()
        post_caps = []
        for t in range(NT):
            if t >= 1:
                dqC.extend(P.capture(post_C(t - 1)))
            if t + 1 < NT:
                dqC.extend(P.capture(pre_C(t + 1)))
            ffn_main(t % 2, dqC, (lambda t=t: ht_stage((t + 1) % 2, 16)) if t + 1 < NT else None)
        for _ in post_C(NT - 1):
            pass

        stores = [P.trk[("out", t, s_)][0] for t in range(NT) for s_ in range(4)]
        P.emit(final_waits=stores)
    return nc


def _t5_bucket(dist):
    max_exact = 16
    d = np.maximum(dist, 1).astype(np.float32)
    large = max_exact + (np.log(d / max_exact) / np.log(2048 / max_exact) * (32 - max_exact)).astype(np.int32)
    large = np.minimum(large, 31)
    return np.where(dist < max_exact, dist, large).astype(np.int32)


def _consts():
    ident = np.eye(128, dtype=np.float32)
    tri = (np.arange(128)[:, None] <= np.arange(128)[None, :]).astype(np.float32)
    dist = np.arange(0, 2049)
    mult = (dist <= 128).astype(np.float32) + ((dist % 4 == 0) & (dist <= 512)) + ((dist % 16 == 0) & (dist <= 2048))
    bucket = _t5_bucket(dist)
    cm = np.zeros((32, FLEN), np.float32)
    cm[bucket, dist + 511] = mult
    inv_freq = (np.float32(10000.0) ** (-np.arange(0, 64, 2, dtype=np.float32) / np.float32(64))).astype(np.float32)
    ang = (np.arange(SEQ, dtype=np.float32)[:, None] * inv_freq[None, :]).astype(np.float32)
    cos = np.cos(ang).astype(np.float32).T
    sin = np.sin(ang).astype(np.float32).T
    cos2 = np.ascontiguousarray(np.concatenate([cos, cos], 0))
    sin2 = np.ascontiguousarray(np.concatenate([-sin, sin], 0))
    return ident, tri, cm, cos2, sin2


def _prep_inputs(inputs):
    f = lambda k: np.asarray(inputs[k], dtype=np.float32)
    x = f("x")
    ident, tri, cm, cos2, sin2 = _consts()
    w_in = f("w_in")[0]
    w_qb = f("mla_w_q_b")[0]
    w_kvb = f("mla_w_kv_b")[0]
    w_out = f("w_out")[0]
    rel = f("rel_bias")
    swp = np.concatenate([np.arange(32, 64), np.arange(0, 32)])

    def pc(v, nc_):
        return np.asarray(v, np.float32).reshape(nc_, 128).T

    gq, gk = f("mla_q_norm")[0], f("mla_k_norm")[0]
    gv = np.zeros((128, 64), np.float32)
    gv[:, 0:8] = pc(f("ffn1_norm")[0], 8)
    gv[:, 8:16] = pc(f("mix_norm")[0], 8)
    gv[:, 16:24] = pc(f("ffn2_norm")[0], 8)
    gv[:, 24] = np.tile(f("dil_q_norm")[0], 2)
    gv[:, 25] = np.tile(f("dil_k_norm")[0], 2)
    gv[:, 26:28] = pc(f("mla_q_a_norm")[0], 2)
    gv[:, 28] = f("mla_kv_a_norm")[0]
    gv[:, 29] = gq[0:128]
    gv[0:64, 30] = gq[128:192]
    gv[0:64, 31] = gq[128:192][swp]
    gv[:, 32] = gk[0:128]
    gv[0:64, 33] = gk[128:192]
    gv[0:64, 34] = gk[128:192][swp]
    gv[:, 35:39] = pc(f("out_norm_dil")[0], 4)
    gv[:, 39:43] = pc(f("out_norm_mla")[0], 4)

    wmaps = {}
    for n, p in ((1, "ffn1"), (2, "ffn2")):
        wmaps["w%dg" % n] = np.ascontiguousarray(f(p + "_w_gate")[0])
        wmaps["w%du" % n] = np.ascontiguousarray(f(p + "_w_up")[0])
        wmaps["w%dd" % n] = np.ascontiguousarray(f(p + "_w_down")[0])
    rows = np.concatenate([np.concatenate([np.arange(128 * r, 128 * r + 128), 512 + np.arange(128 * r, 128 * r + 128)])
                           for r in range(4)])
    wo = np.ascontiguousarray(w_out[rows, :])
    maps = []
    for core in range(8):
        b, j = divmod(core, 4)
        kpe = w_in[:, 1920:1984]
        wsel = np.concatenate([w_in[:, 128 * j:128 * j + 128], w_in[:, 512 + 128 * j:512 + 128 * j + 128],
                               w_in[:, 1024 + 128 * j:1024 + 128 * j + 128], w_in[:, 1536:1792],
                               w_in[:, 1792:1920], kpe, kpe[:, swp]], axis=1)
        qr = w_qb[:, 192 * j + 128:192 * j + 192]
        wqb = np.concatenate([w_qb[:, 192 * j:192 * j + 128], qr, qr[:, swp]], axis=1)
        m = {
            "x": np.ascontiguousarray(x[b, j * TOK:(j + 1) * TOK, :]),
            "ident": ident, "gv": gv, "tri": tri, "cm": cm, "cos2": cos2, "sin2": sin2,
            "wsel": np.ascontiguousarray(wsel), "wqb": np.ascontiguousarray(wqb),
            "wkvb": np.ascontiguousarray(w_kvb[:, 256 * j:256 * j + 256]),
            "wo": wo, "relbT": np.ascontiguousarray(rel[2 * j:2 * j + 2, :].T),
        }
        m.update(wmaps)
        maps.append(m)
    return maps


_NC_CACHE = {}


def run_raw(inputs, stage="full", trace=False):
    if stage not in _NC_CACHE:
        _NC_CACHE[stage] = build_nc(stage)
    nc = _NC_CACHE[stage]
    maps = _prep_inputs(inputs)
    return run_bass_kernel_spmd(nc, maps, core_ids=list(range(8)), trace=trace)


def kernel(**inputs):
    res = run_raw(inputs)
    out = np.zeros((2, SEQ, D), np.float32)
    for core in range(8):
        b, j = divmod(core, 4)
        out[b, j * TOK:(j + 1) * TOK, :] = res.results[core]["out"]
    return out
```

```python
import math
import numpy as np
import ml_dtypes
import concourse.bass as bass
import concourse.mybir as mybir
from concourse.bass_utils import run_bass_kernel_spmd

F32 = mybir.dt.float32
BF16 = mybir.dt.bfloat16
AF = mybir.ActivationFunctionType
ALU = mybir.AluOpType

D = 1024
DFF = 2816
NFF = DFF // 128
SEQ = 8192
TOK = 2048
TT = 512
NT = TOK // TT
EPS = 1e-6
ENGS = ("pe", "act", "dve", "pool", "sp")


class H:
    __slots__ = ("eng", "sig", "val", "dma")

    def __init__(self, eng):
        self.eng = eng
        self.sig = False
        self.val = None
        self.dma = None


class Prog:
    def __init__(self, nc):
        self.nc = nc
        self.streams = {e: [] for e in ENGS}
        self.dma_cnt = {}
        self.trk = {}
        self.pending = {}
        self.all_dma = []
        self.cap = None

    def op(self, eng, fn, deps=(), reads=(), writes=(), _async=False):
        if self.cap is not None:
            self.cap.append(("op", eng, fn, tuple(reads), tuple(writes)))
            return None
        h = H(eng)
        deps = [d for d in deps if d is not None] + self.pending.pop(eng, [])
        trk = self.trk
        for k in list(reads) + list(writes):
            w = trk.setdefault(k, [None, {}])
            if w[0] is not None:
                deps.append(w[0])
        for k in writes:
            deps.extend(trk[k][1].values())
        for k in writes:
            trk[k][0] = h
            trk[k][1] = {}
        for k in reads:
            rk = id(h) if _async else eng
            trk[k][1][rk] = h
        deps = [d for d in deps if d is not h]
        for d in deps:
            if d.dma is None and d.eng != eng:
                d.sig = True
        self.streams[eng].append((h, fn, deps))
        return h

    def dma(self, eng, key, fn, deps=(), reads=(), writes=()):
        if self.cap is not None:
            self.cap.append(("dma", eng, key, fn, tuple(reads), tuple(writes)))
            return None
        h = self.op(eng, fn, deps, reads, writes, _async=True)
        self.dma_cnt[key] = self.dma_cnt.get(key, 0) + 16
        h.dma = (key, self.dma_cnt[key])
        self.all_dma.append(h)
        return h

    def capture(self, gen):
        steps = []
        self.cap = cur = []
        for y in gen:
            if cur or y == "pad":
                steps.append(cur)
            self.cap = cur = []
        if cur:
            steps.append(cur)
        self.cap = None
        return steps

    def replay(self, step):
        for rec in step:
            if rec[0] == "op":
                self.op(rec[1], rec[2], (), rec[3], rec[4])
            elif rec[0] == "dma":
                self.dma(rec[1], rec[2], rec[3], (), rec[4], rec[5])
            else:
                self.cc(rec[1], rec[2], (), rec[3], rec[4])

    def cc(self, key, fn, deps=(), reads=(), writes=()):
        if self.cap is not None:
            self.cap.append(("cc", key, fn, tuple(reads), tuple(writes)))
            return None
        h = self.op("pool", fn, deps, reads, writes, _async=True)
        self.dma_cnt[key] = self.dma_cnt.get(key, 0) + 1
        h.dma = (key, self.dma_cnt[key])
        self.all_dma.append(h)
        return h

    def barrier(self):
        deps = [h for h in self.all_dma if not (isinstance(h.dma[0], tuple) and h.dma[0][0] == "cc")]
        self.all_dma = []
        for e in ENGS:
            for (h, fn, d) in reversed(self.streams[e]):
                if h.dma is None:
                    deps.append(h)
                    break
        for d in deps:
            if d.dma is None:
                d.sig = True
        self.pending = {e: list(deps) for e in ENGS}

    def emit(self, final_waits=()):
        nc = self.nc
        for e in ENGS:
            c = 0
            for (h, fn, deps) in self.streams[e]:
                if h.dma is None and h.sig:
                    c += 1
                    h.val = c
        import contextlib
        with contextlib.ExitStack() as es:
            esem = {e: es.enter_context(nc.semaphore("s_" + e)) for e in ENGS}
            dsem = {k: es.enter_context(nc.semaphore("d%d" % i)) for i, k in enumerate(self.dma_cnt)}
            block = es.enter_context(nc.Block())

            def run(e, engobj):
                seen = {}
                for (h, fn, deps) in self.streams[e]:
                    for d in deps:
                        if d.dma is not None:
                            k, v = ("d", d.dma[0]), d.dma[1]
                            sem = dsem[d.dma[0]]
                        else:
                            if d.eng == e:
                                continue
                            k, v = ("e", d.eng), d.val
                            sem = esem[d.eng]
                        if seen.get(k, 0) >= v:
                            continue
                        seen[k] = v
                        engobj.wait_ge(sem, v)
                    ins = fn(engobj)
                    if h.dma is not None:
                        ins.then_inc(dsem[h.dma[0]], 1 if (isinstance(h.dma[0], tuple) and h.dma[0][0] == "cc") else 16)
                    elif h.sig:
                        ins.then_inc(esem[e], 1)
                if e == "sp":
                    for d in final_waits:
                        engobj.wait_ge(dsem[d.dma[0]], d.dma[1])

            @block.tensor
            def _(eng):
                run("pe", eng)

            @block.scalar
            def _(eng):
                run("act", eng)

            @block.vector
            def _(eng):
                run("dve", eng)

            @block.gpsimd
            def _(eng):
                run("pool", eng)

            @block.sync
            def _(eng):
                run("sp", eng)


class Arena:
    def __init__(self, t, nbytes):
        self.t = t
        self.views = {BF16: t, F32: t.bitcast(F32)}
        self.nbytes = nbytes
        self.off = 0
        self.marks = []

    def alloc(self, cols, dtype, parts=128):
        sz = 4 if dtype == F32 else 2
        self.off = (self.off + 63) // 64 * 64
        a = self.off
        self.last = a
        self.off += cols * sz
        assert self.off <= self.nbytes, ("SBUF arena overflow", self.off, self.nbytes)
        return self.views[dtype][0:parts, a // sz: a // sz + cols]

    def mark(self):
        self.marks.append(self.off)

    def release(self):
        self.off = self.marks.pop()


NB = SEQ // 128
SC_D = 0.125
SC_M = 192.0 ** -0.5
FLEN = 3072
MW = 2944


def build_nc(stage="full"):
    nc = bass.Bass("TRN2", target_bir_lowering=False)
    P = Prog(nc)

    def din(name, shape, dt=F32):
        return nc.dram_tensor(name, list(shape), dt, kind="ExternalInput")

    x_d = din("x", [TOK, D])
    ident_d = din("ident", [128, 128])
    gv_d = din("gv", [128, 64])
    wf_d = {(n, k): din("w%d%s" % (n, k), [D, DFF] if k != "d" else [DFF, D])
            for n in (1, 2) for k in ("g", "u", "d")}
    wsel_d = din("wsel", [D, 896])
    wqb_d = din("wqb", [256, 256])
    wkvb_d = din("wkvb", [128, 256])
    wo_d = din("wo", [D, D])
    relbT_d = din("relbT", [32, 2])
    cm_d = din("cm", [32, FLEN])
    tri_d = din("tri", [128, 128])
    cos_d = din("cos2", [64, SEQ])
    sin_d = din("sin2", [64, SEQ])
    out_d = nc.dram_tensor("out", [TOK, D], F32, kind="ExternalOutput")
    xs_d = nc.dram_tensor("xs", [D, TOK], F32)
    b1_d = [nc.dram_tensor("b1_%d" % t, [D, TT], BF16) for t in range(NT)]
    g1_d = [nc.dram_tensor("g1_%d" % t, [4 * D, TT], BF16) for t in range(NT)]
    b2_d = [nc.dram_tensor("b2_%d" % u, [256, 1024], F32) for u in range(8)]
    g2_d = nc.dram_tensor("g2", [8 * 1024, 1024], F32)
    fvec_d = nc.dram_tensor("fvec", [2, FLEN], BF16)
    w2bf_d = {"g": nc.dram_tensor("w2g_bf", [D, DFF], BF16), "u": nc.dram_tensor("w2u_bf", [D, DFF], BF16),
              "d": nc.dram_tensor("w2d_bf", [DFF, D], BF16), "o": nc.dram_tensor("wo_bf", [D, D], BF16)}
    GROUPS = [[0, 1, 2, 3], [4, 5, 6, 7]]
    wselbf_d = nc.dram_tensor("wsel_bf", [D, 896], BF16)
    wqbbf_d = nc.dram_tensor("wqb_bf", [256, 256], BF16)
    wkvbbf_d = nc.dram_tensor("wkvb_bf", [128, 256], BF16)

    import contextlib
    with contextlib.ExitStack() as es:
        ARENA_BYTES = 206 * 1024
        big = es.enter_context(nc.sbuf_tensor("arena", [128, ARENA_BYTES // 2], BF16))
        A = Arena(big, ARENA_BYTES)
        ps = [es.enter_context(nc.psum_tensor("ps%d" % i, [128, 512], F32)) for i in range(8)]

        def MM(out, lhsT, rhs, start, stop, reads, writes):
            return P.op("pe", lambda e: e.matmul(out, lhsT=lhsT, rhs=rhs, start=start, stop=stop),
                        reads=reads, writes=writes)

        def ACT(out, in_, func, reads, writes, scale=1.0, bias=None):
            if bias is None:
                return P.op("act", lambda e: e.activation(out=out, in_=in_, func=func, scale=scale),
                            reads=reads, writes=writes)
            return P.op("act", lambda e: e.activation(out=out, in_=in_, func=func, scale=scale, bias=bias),
                        reads=reads, writes=writes)

        def STT(eng, out, in0, scalar, in1, op0, op1, reads, writes):
            return P.op(eng, lambda e: e.scalar_tensor_tensor(out=out, in0=in0, scalar=scalar, in1=in1,
                                                              op0=op0, op1=op1), reads=reads, writes=writes)

        def TTO(eng, out, in0, in1, op, reads, writes):
            return P.op(eng, lambda e: e.tensor_tensor(out=out, in0=in0, in1=in1, op=op),
                        reads=reads, writes=writes)

        def CP(eng, out, in_, reads, writes):
            return P.op(eng, lambda e: e.tensor_copy(out=out, in_=in_), reads=reads, writes=writes)

        def RECIP(out, in_, reads, writes):
            return P.op("dve", lambda e: e.reciprocal(out=out, in_=in_), reads=reads, writes=writes)

        def MEMSET(ap, v, writes):
            return P.op("dve", lambda e: e.memset(ap, v), writes=writes)

        def DMA(q, key, out, in_, reads, writes):
            return P.dma(q, key, lambda e: e.dma_start(out=out, in_=in_), reads=reads, writes=writes)

        ident = A.alloc(128, F32)
        o1024 = A.alloc(128, F32)
        o512 = A.alloc(128, F32)
        o256 = A.alloc(128, F32)
        o192 = A.alloc(128, F32)
        o128 = A.alloc(128, F32)
        bd64 = A.alloc(128, F32)
        one_f = A.alloc(128, F32)
        one_bf = A.alloc(128, BF16)
        tri = A.alloc(128, BF16)
        gv = A.alloc(64, F32)
        eps_c = A.alloc(1, F32)
        e64 = A.alloc(128, F32)
        DMA("sp", "c_ident", ident, ident_d[:, :], [], ["ident"])
        DMA("sp", "c_gv", gv, gv_d[:, :], [], ["gv"])
        DMA("pool", "c_tri", tri, tri_d[:, :], [], ["tri"])
        for ap_, v_ in ((o1024, 1.0 / 1024), (o512, 1.0 / 512), (o256, 1.0 / 256), (o192, 1.0 / 192),
                        (o128, 1.0 / 128), (one_f, 1.0), (one_bf, 1.0), (eps_c, EPS), (bd64, 0.0)):
            MEMSET(ap_, v_, ["cmat"])
        MEMSET(e64, 0.0, ["cmat"])
        MEMSET(e64[:, 64:65], 1.0, ["cmat"])
        MEMSET(bd64[0:64, 0:64], 1.0 / 64, ["cmat"])
        MEMSET(bd64[64:128, 64:128], 1.0 / 64, ["cmat"])
        A.mark()

        wg3 = A.alloc(8 * DFF, BF16).rearrange("p (c f) -> p c f", c=8)
        wu3 = A.alloc(8 * DFF, BF16).rearrange("p (c f) -> p c f", c=8)
        wd3 = A.alloc(NFF * D, BF16).rearrange("p (f d) -> p f d", f=NFF)
        A.mark()
        xin = [A.alloc(TT, F32) for _ in range(2)]
        _xo = A.last - TT * 4
        h2b = A.alloc(8 * TT, BF16)
        assert A.last == _xo + 2 * TT * 4
        wo_resA = A.views[BF16][0:128, _xo // 2:_xo // 2 + 8 * 768].rearrange("p (k d) -> p k d", k=8)
        h2v = h2b.rearrange("p (c t) -> p c t", c=8)
        ostg = [h2b.bitcast(F32)[:, i * D:(i + 1) * D] for i in range(2)]
        xTb = [A.alloc(8 * TT, F32).rearrange("p (c t) -> p c t", c=8) for _ in range(2)]
        xT3 = xTb[0]
        oT3 = xTb[1]
        hT3 = A.alloc(8 * TT, BF16).rearrange("p (c t) -> p c t", c=8)
        GF = [(0, 6), (6, 12), (12, 17), (17, 22)]
        actT = A.alloc(6 * TT, BF16)
        actT3 = actT.rearrange("p (f t) -> p f t", f=6)
        wo_resB = actT[:, 0:2048].rearrange("p (k d) -> p k d", k=8)
        sg = [A.alloc(TT, F32) for _ in range(2)]
        rstd = A.alloc(TT, F32)
        rstdQ = A.alloc(TT, F32)
        sq2 = A.alloc(TT, F32)

        FP, FD = 512, 4

        def load_ffn_weights(n):
            ops = []
            if n == 1:
                srcs = {k: wf_d[(n, k)].ap() for k in "gud"}
                q, rd = "pool", {k: [] for k in "gud"}
            else:
                srcs = {k: w2bf_d[k].ap() for k in "gud"}
                q, rd = "act", {k: [("w2bf", k)] for k in "gud"}
            gv_ = srcs["g"].rearrange("(c p) f -> p c f", p=128)
            uv_ = srcs["u"].rearrange("(c p) f -> p c f", p=128)
            dv_ = srcs["d"].rearrange("(f p) d -> p f d", p=128)
            for i, f0 in enumerate(range(0, DFF, FP)):
                f1 = min(DFF, f0 + FP)
                ops.append(lambda i=i, f0=f0, f1=f1: DMA(q, ("wg", i), wg3[:, :, f0:f1], gv_[:, :, f0:f1], rd["g"], [("wg", i)]))
                ops.append(lambda i=i, f0=f0, f1=f1: DMA(q, ("wu", i), wu3[:, :, f0:f1], uv_[:, :, f0:f1], rd["u"], [("wu", i)]))
            for i, f0 in enumerate(range(0, NFF, FD)):
                f1 = min(NFF, f0 + FD)
                ops.append(lambda i=i, f0=f0, f1=f1: DMA(q, ("wd", i), wd3[:, f0:f1, :], dv_[:, f0:f1, :], rd["d"], [("wd", i)]))
            return ops

        def stats(srcs, ones_m, out_rstd, out_key, bank=6):
            for i, (src, p, rk) in enumerate(srcs):
                if i == 0:
                    ACT(out_rstd[0:p, :], src, AF.Square, rk, [out_key])
                else:
                    s_ = i % 2
                    ACT(sg[s_][0:p, :], src, AF.Square, rk, [("sg", s_)])
                    TTO("dve", out_rstd[0:p, :], out_rstd[0:p, :], sg[s_][0:p, :], ALU.add, [out_key, ("sg", s_)], [out_key])
            MM(ps[bank][:, :], ones_m, out_rstd, True, True, [out_key, "cmat"], [("ps", bank)])
            ACT(out_rstd, ps[bank][:, :], AF.Ln, [("ps", bank), "cmat"], [out_key], bias=eps_c)
            ACT(out_rstd, out_rstd, AF.Exp, [out_key], [out_key], scale=-0.5)

        from collections import deque

        def stats8_g(xt3, xkey, out_rstd, out_key, bank):
            ACT(out_rstd, xt3[:, 0, :], AF.Square, [xkey], [out_key])
            yield
            for c in range(1, 8):
                ACT(sq2, xt3[:, c, :], AF.Square, [xkey], ["sq2"])
                yield
                TTO("dve", out_rstd, out_rstd, sq2, ALU.add, [out_key, "sq2"], [out_key])
                yield
            MM(ps[bank][:, :], o1024, out_rstd, True, True, [out_key, "cmat"], [("ps", bank)])
            yield
            ACT(out_rstd, ps[bank][:, :], AF.Ln, [("ps", bank), "cmat"], [out_key], bias=eps_c)
            ACT(out_rstd, out_rstd, AF.Exp, [out_key], [out_key], scale=-0.5)
            yield

        def ht_stage(b, gcol):
            for c in range(8):
                STT("dve", hT3[:, c, :], xTb[b][:, c, :], gv[:, gcol + c:gcol + c + 1], rstd, ALU.mult, ALU.mult,
                    [("xT", b), "gv", "rstd"], ["hT"])

        def ffn_main(b, dq, nxt_ht=None):
            xt = xTb[b]
            xk = ("xT", b)
            slots = [2 * NFF + 8 * (len(GF) - 1)]

            def fill():
                if dq:
                    k = -(-len(dq) // max(1, slots[0]))
                    for _ in range(min(k, len(dq))):
                        P.replay(dq.popleft())
                slots[0] -= 1

            for gi, (f0, f1) in enumerate(GF):
                for f in range(f0, f1):
                    s_ = f % 2
                    for c in range(8):
                        MM(ps[s_][:, :], wg3[:, c, f * 128:(f + 1) * 128], hT3[:, c, :], c == 0, c == 7,
                           ["hT", ("wg", f * 128 // FP)], [("ps", s_)])
                    fill()
                    for c in range(8):
                        MM(ps[2 + s_][:, :], wu3[:, c, f * 128:(f + 1) * 128], hT3[:, c, :], c == 0, c == 7,
                           ["hT", ("wu", f * 128 // FP)], [("ps", 2 + s_)])
                    ACT(sg[s_], ps[s_][:, :], AF.Silu, [("ps", s_)], [("sg", s_)])
                    TTO("dve", actT3[:, f - f0, :], ps[2 + s_][:, :], sg[s_], ALU.mult,
                        [("ps", 2 + s_), ("sg", s_)], ["actT"])
                    fill()
                if gi == len(GF) - 1:
                    while dq:
                        P.replay(dq.popleft())
                    if nxt_ht is not None:
                        nxt_ht()
                for d in range(8):
                    s_ = d % 2
                    for f in range(f0, f1):
                        MM(ps[4 + s_][:, :], wd3[:, f, d * 128:(d + 1) * 128], actT3[:, f - f0, :], f == f0, f == f1 - 1,
                           ["actT", ("wd", f // FD)], [("ps", 4 + s_)])
                    STT("dve", xt[:, d, :], ps[4 + s_][:, :], 0.5, xt[:, d, :], ALU.mult, ALU.add,
                        [("ps", 4 + s_), xk], [xk])
                    if gi < len(GF) - 1:
                        fill()

        xs_v = xs_d.ap().rearrange("(c p) t -> p c t", p=128)

        def pre_A_loads(t):
            for k in range(2):
                s_, hf = divmod(k, 2)
                r0 = t * TT + s_ * 128
                DMA("sp", ("xin", k % 2), xin[k % 2], x_d[r0:r0 + 128, hf * 512:(hf + 1) * 512], [], [("xin", k % 2)])
                yield

        def pre_A(t):
            b = t % 2
            for k in range(8):
                s_, hf = divmod(k, 2)
                slot = k % 2
                for cc in range(4):
                    P.op("pe", lambda e, hf=hf, cc=cc, slot=slot: e.transpose(
                        out=ps[6 + hf][:, cc * 128:(cc + 1) * 128], in_=xin[slot][:, cc * 128:(cc + 1) * 128],
                        identity=ident), reads=[("xin", slot), "ident"], writes=[("ps", 6 + hf)])
                yield
                dst = xTb[b][:, hf * 4:hf * 4 + 4, s_ * 128:(s_ + 1) * 128]
                src = ps[6 + hf][:, :].rearrange("p (c t) -> p c t", c=4)
                if hf == 0:
                    ACT(dst, src, AF.Copy, [("ps", 6)], [("xT", b)])
                else:
                    CP("dve", dst, src, [("ps", 7)], [("xT", b)])
                if k + 2 < 8:
                    s2, hf2 = divmod(k + 2, 2)
                    r0 = t * TT + s2 * 128
                    DMA("sp", ("xin", slot), xin[slot], x_d[r0:r0 + 128, hf2 * 512:(hf2 + 1) * 512], [], [("xin", slot)])
                yield

        def pre_A_stats(t):
            b = t % 2
            yield from stats8_g(xTb[b], ("xT", b), rstd, "rstd", 6)

        def post_A(t):
            b = t % 2
            DMA("sp", ("xT", b), xs_v[:, :, t * TT:(t + 1) * TT], xTb[b], [("xT", b)], [("xs", t)])
            yield
            yield from stats8_g(xTb[b], ("xT", b), rstdQ, "rstdQ", 7)
            for c in range(8):
                STT("dve", h2v[:, c, :], xTb[b][:, c, :], gv[:, 8 + c:9 + c], rstdQ, ALU.mult, ALU.mult,
                    [("xT", b), "gv", "rstdQ"], ["h2b"])
                if c % 4 == 3:
                    yield

        def post_A2(t):
            DMA("sp", "h2b", b1_d[t].ap().rearrange("(c p) n -> p c n", p=128), h2v, ["h2b"], [("b1", t)])
            yield
            P.cc(("cc", "g1", t), lambda e, t=t: e.collective_compute(
                "AllGather", ALU.bypass, replica_groups=GROUPS, ins=[b1_d[t][:, :]], outs=[g1_d[t][:, :]]),
                reads=[("b1", t)], writes=[("g1", t)])
            yield

        wl1 = load_ffn_weights(1)
        gu = lambda i: [wl1[2 * i], wl1[2 * i + 1]]
        dd = lambda i: [wl1[12 + i]]
        order = gu(0) + gu(1) + dd(0) + dd(1) + gu(2) + dd(2) + gu(3) + gu(4) + dd(3) + dd(4) + gu(5) + dd(5)
        assert len(order) == len(wl1) == 18
        for w in order:
            w()
        for _ in pre_A_loads(0):
            pass
        for _ in pre_A(0):
            pass
        for _ in pre_A_stats(0):
            pass
        ht_stage(0, 0)
        dqA = deque()
        for t in range(NT):
            if t + 1 < NT:
                dqA.extend(P.capture(pre_A_loads(t + 1)))
            if t >= 1:
                dqA.extend(P.capture(post_A(t - 1)))
            if t + 1 < NT:
                dqA.extend(P.capture(pre_A(t + 1)))
            if t >= 1:
                dqA.extend(P.capture(post_A2(t - 1)))
            if t + 1 < NT:
                dqA.extend(P.capture(pre_A_stats(t + 1)))
            if t == 1:
                def precast():
                    DMA("pool", ("pc", "wsel"), wselbf_d[:, :], wsel_d[:, :], [], [("pc", "wsel")])
                    yield
                    DMA("pool", ("pc", "wqb"), wqbbf_d[:, :], wqb_d[:, :], [], [("pc", "wqb")])
                    DMA("pool", ("pc", "wkvb"), wkvbbf_d[:, :], wkvb_d[:, :], [], [("pc", "wkvb")])
                    yield
                dqA.extend(P.capture(precast()))
            ffn_main(t % 2, dqA, (lambda t=t: ht_stage((t + 1) % 2, 0)) if t + 1 < NT else None)
        AB = Arena(big, ARENA_BYTES)
        AB.off = A.marks[0]
        e_wsel3 = AB.alloc(8 * 896, BF16).rearrange("p (c f) -> p c f", c=8)
        e_wqb3 = AB.alloc(2 * 256, BF16).rearrange("p (c f) -> p c f", c=2)
        e_wkvb = AB.alloc(256, BF16)
        AB.alloc(FLEN, BF16)
        AB.alloc(FLEN, BF16)
        e_h2t3 = AB.alloc(8 * TT, BF16).rearrange("p (c t) -> p c t", c=8)
        e_end = AB.off
        WGK = [("wg", i) for i in range(6)]
        DMA("sp", "wsel", e_wsel3, wselbf_d.ap().rearrange("(c p) f -> p c f", p=128), [("pc", "wsel")], ["wsel"] + WGK)
        DMA("sp", "wqb", e_wqb3, wqbbf_d.ap().rearrange("(c p) f -> p c f", p=128), [("pc", "wqb")], ["wqb"] + WGK)
        DMA("sp", "wkvb", e_wkvb, wkvbbf_d[:, :], [("pc", "wkvb")], ["wkvb"] + WGK)
        DMA("sp", "h2t", e_h2t3, g1_d[0].ap()[0:D, :].rearrange("(c p) n -> p c n", p=128), [("g1", 0)], ["h2t"] + WGK)
        for _ in post_A(NT - 1):
            pass
        for _ in post_A2(NT - 1):
            pass
        P.barrier()

        A.release()
        A.release()
        A.mark()
        wsel3 = A.alloc(8 * 896, BF16).rearrange("p (c f) -> p c f", c=8)
        wqb3 = A.alloc(2 * 256, BF16).rearrange("p (c f) -> p c f", c=2)
        wkvb = A.alloc(256, BF16)
        Mh = [A.alloc(FLEN, BF16) for _ in range(2)]
        h2t3 = A.alloc(8 * TT, BF16).rearrange("p (c t) -> p c t", c=8)
        assert A.off == e_end and e_end <= A.marks[0] + 8 * DFF * 2, "early-load buffers must sit inside the FFN1 gate-weight region"
        qdp = [[A.alloc(TT, BF16) for _ in range(2)] for _ in range(2)]
        kdT = A.alloc(SEQ, BF16)
        Vd4 = A.alloc(NB * 2 * 128, BF16).rearrange("p (b h c) -> p b h c", b=NB, h=2)
        KTn = A.alloc(SEQ, BF16)
        KTr = A.alloc(SEQ, BF16)
        Vm3 = A.alloc(NB * 128, BF16).rearrange("p (b c) -> p b c", b=NB)
        cqn3 = A.alloc(2 * TT, BF16).rearrange("p (c t) -> p c t", c=2)
        ckvn = A.alloc(TT, BF16)
        Qn2 = [A.alloc(TT, BF16) for _ in range(2)]
        Qr2 = [A.alloc(TT, BF16) for _ in range(2)]
        sg_A = sg
        sg = [A.alloc(TT, F32) for _ in range(2)]
        rs0 = A.alloc(TT, F32)
        rs1 = A.alloc(TT, F32)
        ra = A.alloc(TT, F32)
        rb = A.alloc(TT, F32)
        cst = A.alloc(TT, F32)
        snt = A.alloc(TT, F32)
        Pt = [A.alloc(TT, BF16) for _ in range(6)]
        pe_a = A.alloc(TT, F32)
        pe_b = A.alloc(TT, F32)
        ocp = [A.alloc(TT, F32) for _ in range(3)]
        cq0_s = A.alloc(TT, F32)
        rinv = A.alloc(TT, F32)
        acc64 = A.alloc(TT, F32)
        rinvd = A.alloc(TT, F32)
        cm_s = A.alloc(FLEN, F32)
        rb_s = A.alloc(2, F32)
        et_s = A.alloc(2, F32)
        fsb = A.alloc(FLEN, BF16)
        V64 = cm_s[:, 0:NB]
        pq_a = cm_s[:, 64:64 + TT]
        pq_b = cm_s[:, 64 + TT:64 + 2 * TT]
        ra2 = cm_s[:, 64 + 2 * TT:64 + 3 * TT]
        rb2 = cm_s[:, 64 + 3 * TT:64 + 4 * TT]

        DMA("sp", "cst", cst[0:64, :], cos_d[:, 0:TT], [], ["cst"])
        DMA("sp", "snt", snt[0:64, :], sin_d[:, 0:TT], [], ["snt"])
        MEMSET(Vd4[:, :, :, 64:128], 1.0, ["Vd_ones"])
        MEMSET(KTr[64:128, :], 0.0, ["zpad"])
        MEMSET(Vm3[:, :, 64:65], 1.0, ["Vm_ones"])
        for pb_ in range(2):
            MEMSET(Qr2[pb_][64:128, :], 0.0, ["zpad"])
            MEMSET(qdp[pb_][0][64:128, :], 0.0, ["zpad"])
            MEMSET(qdp[pb_][1][0:64, :], 0.0, ["zpad"])
        DMA("sp", "cm", cm_s[0:32, :], cm_d[:, :], [], ["cm"])
        DMA("sp", "rb", rb_s[0:32, :], relbT_d[:, :], [], ["rb"])
        ACT(et_s[0:32, :], rb_s[0:32, :], AF.Exp, ["rb"], ["et"])
        for n in range(FLEN // 512):
            MM(ps[0][0:2, :], et_s[0:32, 0:2], cm_s[0:32, n * 512:(n + 1) * 512], True, True, ["et", "cm"], [("ps", 0)])
            CP("dve", fsb[0:2, n * 512:(n + 1) * 512], ps[0][0:2, :], [("ps", 0)], ["fsb"])
        DMA("sp", "fsb", fvec_d[:, :], fsb[0:2, :], ["fsb"], ["fvec"])
        LW = FLEN - 1
        MQ = ("sp", "sp")
        for hh in range(2):
            MEMSET(Mh[hh], 0.0, [("Mh", hh)])
        for hh in range(2):
            DMA(MQ[hh], ("Mh", hh), Mh[hh][0:1, 0:LW], fvec_d[hh:hh + 1, 0:LW], ["fvec"], [("Mh", hh)])
        for r_ in range(7):
            n_ = 1 << r_
            for hh in range(2):
                DMA(MQ[hh], ("Mh", hh), Mh[hh][n_:2 * n_, n_:LW], Mh[hh][0:n_, 0:LW - n_], [("Mh", hh)], [("Mh", hh)])

        def stats_g(srcs, ones_m, out_rstd, out_key, bank):
            for i, (src, p, rk) in enumerate(srcs):
                if i == 0:
                    ACT(out_rstd[0:p, :], src, AF.Square, rk, [out_key])
                else:
                    ACT(sg[i % 2][0:p, :], src, AF.Square, rk, [("sg", i % 2)])
            yield
            if len(srcs) > 1:
                for i, (src, p, rk) in enumerate(srcs):
                    if i > 0:
                        TTO("dve", out_rstd[0:p, :], out_rstd[0:p, :], sg[i % 2][0:p, :], ALU.add,
                            [out_key, ("sg", i % 2)], [out_key])
                yield
            MM(ps[bank][:, :], ones_m, out_rstd, True, True, [out_key, "cmat"], [("ps", bank)])
            yield
            ACT(out_rstd, ps[bank][:, :], AF.Ln, [("ps", bank), "cmat"], [out_key], bias=eps_c)
            ACT(out_rstd, out_rstd, AF.Exp, [out_key], [out_key], scale=-0.5)
            yield

        def b0(T):
            r, t = divmod(T, 4)
            pb = T % 2
            Qn, Qr = Qn2[pb], Qr2[pb]
            def load_h2t(T_):
                r_, t_ = divmod(T_, 4)
                DMA("sp", "h2t", h2t3, g1_d[t_].ap()[r_ * D:(r_ + 1) * D, :].rearrange("(c p) n -> p c n", p=128),
                    [("g1", t_)], ["h2t"])

            def load_cs(T_):
                DMA("sp", "cst", cst[0:64, :], cos_d[:, T_ * TT:(T_ + 1) * TT], [], ["cst"])
                DMA("sp", "snt", snt[0:64, :], sin_d[:, T_ * TT:(T_ + 1) * TT], [], ["snt"])


            def proj(bank, lo, ncols):
                for c in range(8):
                    MM(ps[bank][0:ncols, :], wsel3[:, c, lo:lo + ncols], h2t3[:, c, :], c == 0, c == 7,
                       ["h2t", "wsel"], [("ps", bank)])
                    if c == 3:
                        yield
                yield

            def rope(gcol, rs, rskey, dst, dkey, sc=None):
                pa_, pb_, ra_, rb_, ka, kb_, kra, krb = sc or (pe_a, pe_b, ra, rb, "pe_a", "pe_b", "ra", "rb_")
                STT("dve", ra_[0:64, :], pa_[0:64, :], gv[0:64, gcol:gcol + 1], rs[0:64, :], ALU.mult, ALU.mult,
                    [ka, "gv", rskey], [kra])
                STT("dve", rb_[0:64, :], pb_[0:64, :], gv[0:64, gcol + 1:gcol + 2], rs[0:64, :], ALU.mult, ALU.mult,
                    [kb_, "gv", rskey], [krb])
                yield
                TTO("pool", ra_[0:64, :], ra_[0:64, :], cst[0:64, :], ALU.mult, [kra, "cst"], [kra])
                TTO("pool", rb_[0:64, :], rb_[0:64, :], snt[0:64, :], ALU.mult, [krb, "snt"], [krb])
                TTO("pool", dst, ra_[0:64, :], rb_[0:64, :], ALU.add, [kra, krb], [dkey])
                yield

            yield from proj(6, 0, 128)
            yield from stats_g([(ps[6][:, :], 128, [("ps", 6)])], bd64, rs0, "rs0", 7)
            STT("dve", qdp[pb][0][0:64, :], ps[6][0:64, :], gv[0:64, 24:25], rs0[0:64, :], ALU.mult, ALU.mult,
                [("ps", 6), "gv", "rs0"], [("qd", pb)])
            STT("dve", qdp[pb][1][64:128, :], ps[6][64:128, :], gv[64:128, 24:25], rs0[64:128, :], ALU.mult, ALU.mult,
                [("ps", 6), "gv", "rs0"], [("qd", pb)])
            yield
            yield from proj(7, 128, 128)
            yield from stats_g([(ps[7][:, :], 128, [("ps", 7)])], bd64, rs1, "rs1", 6)
            STT("dve", kdT[:, T * TT:(T + 1) * TT], ps[7][:, :], gv[:, 25:26], rs1, ALU.mult, ALU.mult,
                [("ps", 7), "gv", "rs1"], [("kd", T)])
            yield
            for sb_ in range(4):
                for c in range(8):
                    MM(ps[6][:, sb_ * 128:(sb_ + 1) * 128], h2t3[:, c, sb_ * 128:(sb_ + 1) * 128],
                       wsel3[:, c, 256:384], c == 0, c == 7, ["h2t", "wsel"], [("ps", 6)])
                yield
            CP("dve", Vd4[:, 4 * T:4 * T + 4, :, 0:64],
               ps[6][:, :].rearrange("p (b h c) -> p b h c", b=4, h=2), [("ps", 6)], [("Vd", T)])
            yield
            yield from proj(7, 384, 128)
            ACT(cq0_s, ps[7][:, :], AF.Copy, [("ps", 7)], ["cq0"])
            yield
            yield from proj(6, 512, 128)
            yield from stats_g([(cq0_s, 128, ["cq0"]), (ps[6][:, :], 128, [("ps", 6)])], o256, rs0, "rs0", 7)
            STT("dve", cqn3[:, 0, :], cq0_s, gv[:, 26:27], rs0, ALU.mult, ALU.mult, ["cq0", "gv", "rs0"], ["cqn"])
            STT("dve", cqn3[:, 1, :], ps[6][:, :], gv[:, 27:28], rs0, ALU.mult, ALU.mult, [("ps", 6), "gv", "rs0"], ["cqn"])
            yield
            yield from proj(7, 640, 128)
            yield from stats_g([(ps[7][:, :], 128, [("ps", 7)])], o128, rs1, "rs1", 6)
            STT("dve", ckvn, ps[7][:, :], gv[:, 28:29], rs1, ALU.mult, ALU.mult, [("ps", 7), "gv", "rs1"], ["ckvn"])
            yield
            yield from proj(6, 768, 64)
            ACT(pe_a[0:64, :], ps[6][0:64, :], AF.Copy, [("ps", 6)], ["pe_a"])
            yield
            yield from proj(7, 832, 64)
            CP("dve", pe_b[0:64, :], ps[7][0:64, :], [("ps", 7)], ["pe_b"])
            if T + 1 < SEQ // TT:
                load_h2t(T + 1)
            yield
            MM(ps[6][:, :], wkvb[:, 0:128], ckvn, True, True, ["wkvb", "ckvn"], [("ps", 6)])
            yield
            yield from stats_g([(ps[6][:, :], 128, [("ps", 6)]), (pe_a[0:64, :], 64, ["pe_a"])], o192, rs0, "rs0", 7)
            STT("dve", KTn[:, T * TT:(T + 1) * TT], ps[6][:, :], gv[:, 32:33], rs0, ALU.mult, ALU.mult,
                [("ps", 6), "gv", "rs0"], [("KTn", T)])
            yield
            yield from rope(33, rs0, "rs0", KTr[0:64, T * TT:(T + 1) * TT], ("KTr", T))
            for sb_ in range(4):
                MM(ps[7][:, sb_ * 128:(sb_ + 1) * 128], ckvn[:, sb_ * 128:(sb_ + 1) * 128], wkvb[:, 128:256],
                   True, True, ["wkvb", "ckvn"], [("ps", 7)])
            yield
            pv_ = ps[7][:, :].rearrange("p (b c) -> p b c", b=4)
            CP("dve", Vm3[:, 4 * T:4 * T + 4, 0:64], pv_[:, :, 0:64], [("ps", 7)], [("Vm", T)])
            CP("dve", Vm3[:, 4 * T:4 * T + 4, 65:128], pv_[:, :, 65:128], [("ps", 7)], [("Vm", T)])
            CP("dve", V64[:, 4 * T:4 * T + 4], pv_[:, :, 64], [("ps", 7)], [("V64", T), "cm"])
            yield
            for c in range(2):
                MM(ps[6][0:64, :], wqb3[:, c, 128:192], cqn3[:, c, :], c == 0, c == 1, ["wqb", "cqn"], [("ps", 6)])
            yield
            ACT(pq_a[0:64, :], ps[6][0:64, :], AF.Copy, [("ps", 6)], ["pq_a", "cm"])
            yield
            for c in range(2):
                MM(ps[7][0:64, :], wqb3[:, c, 192:256], cqn3[:, c, :], c == 0, c == 1, ["wqb", "cqn"], [("ps", 7)])
            yield
            CP("dve", pq_b[0:64, :], ps[7][0:64, :], [("ps", 7)], ["pq_b", "cm"])
            yield
            for c in range(2):
                MM(ps[6][:, :], wqb3[:, c, 0:128], cqn3[:, c, :], c == 0, c == 1, ["wqb", "cqn"], [("ps", 6)])
            yield
            yield from stats_g([(ps[6][:, :], 128, [("ps", 6)]), (pq_a[0:64, :], 64, ["pq_a"])], o192, rs1, "rs1", 7)
            STT("dve", Qn, ps[6][:, :], gv[:, 29:30], rs1, ALU.mult, ALU.mult, [("ps", 6), "gv", "rs1"], [("Qn", pb)])
            yield
            yield from rope(30, rs1, "rs1", Qr[0:64, :], ("Qr", pb),
                            sc=(pq_a, pq_b, ra2, rb2, "pq_a", "pq_b", "ra2", "rb2"))
            if T + 1 < SEQ // TT:
                load_cs(T + 1)
                yield

        LA = 2
        NPT = len(Pt)
        NQT = SEQ // TT
        stream = []
        job_id = 0
        for T in range(NQT):
            for kind, hh in (("mla", 0), ("dil", 0), ("dil", 1)):
                kbs = list(range(0, 4 * T + 4)) if kind == "mla" else list(range(max(0, 4 * T - 16), 4 * T + 4))
                for idx, kb in enumerate(kbs):
                    stream.append(dict(T=T, kind=kind, hh=hh, kb=kb, idx=idx, n=len(kbs), ob=3 + job_id % 2))
                job_id += 1
        first_pos = {}
        for p_, tk in enumerate(stream):
            first_pos.setdefault(tk["T"], p_)
        last_use = {0: -10, 1: -9, 2: -8, 5: -7}
        for p_, tk in enumerate(stream):
            four = True
            allowed = (0, 1, 2, 5) if four else (0, 1, 2)
            b_ = min(allowed, key=lambda x: last_use[x])
            last_use[b_] = p_
            tk["sb"] = b_
            tk["la"] = 3 if four else 2

        def s_stage(p_):
            tk = stream[p_]
            T, kb, hh = tk["T"], tk["kb"], tk["hh"]
            pb = T % 2
            sbk = tk["sb"]
            i = kb - 4 * T
            if tk["kind"] == "mla":
                c0 = 128 * max(i, 0)
                MM(ps[sbk][:, c0:512], KTn[:, kb * 128:(kb + 1) * 128], Qn2[pb][:, c0:512], True, False,
                   [("KTn", kb // 4), ("Qn", pb)], [("ps", sbk)])
                MM(ps[sbk][:, c0:512], KTr[:, kb * 128:(kb + 1) * 128], Qr2[pb][:, c0:512], False, True,
                   [("KTr", kb // 4), ("Qr", pb), "zpad"], [("ps", sbk)])
            else:
                c0 = 128 * max(i, 0)
                MM(ps[sbk][:, c0:512], kdT[:, kb * 128:(kb + 1) * 128], qdp[pb][hh][:, c0:512],
                   True, True, [("kd", kb // 4), ("qd", pb), "zpad"], [("ps", sbk)])

        def main_stage(p_, part):
            tk = stream[p_]
            T, kb, hh, idx, n, ob = tk["T"], tk["kb"], tk["hh"], tk["idx"], tk["n"], tk["ob"]
            sbk = tk["sb"]
            pslot = p_ % NPT
            pk = ("Pt", pslot)
            i = kb - 4 * T
            if tk["kind"] == "mla":
                c0 = 128 * max(i, 0)
                if part == 0:
                    ACT(Pt[pslot][:, c0:512], ps[sbk][:, c0:512], AF.Exp, [("ps", sbk)], [pk], scale=SC_M)
                    if i >= 0:
                        TTO("dve", Pt[pslot][:, c0:c0 + 128], Pt[pslot][:, c0:c0 + 128], tri, ALU.mult, [pk, "tri"], [pk])
                else:
                    MM(ps[ob][:, c0:512], Vm3[:, kb, :], Pt[pslot][:, c0:512], idx == 0, idx == n - 1,
                       [pk, ("Vm", kb // 4), "Vm_ones"], [("ps", ob)])
                    if idx == 0:
                        P.op("dve", lambda e, pslot=pslot, kb=kb: e.tensor_scalar(
                            out=acc64, in0=Pt[pslot], scalar1=V64[:, kb:kb + 1], scalar2=None, op0=ALU.mult),
                            reads=[pk, ("V64", kb // 4)], writes=["acc64"])
                    else:
                        STT("dve", acc64[:, c0:512], Pt[pslot][:, c0:512], V64[:, kb:kb + 1], acc64[:, c0:512],
                            ALU.mult, ALU.add, [pk, ("V64", kb // 4), "acc64"], ["acc64"])
            else:
                moff = 128 * (4 * T - kb) + 384
                c0 = 128 * max(i, 0)
                if part == 0:
                    ACT(Pt[pslot][:, c0:512], ps[sbk][:, c0:512], AF.Exp, [("ps", sbk)], [pk], scale=SC_D)
                    TTO("dve", Pt[pslot][:, c0:512], Pt[pslot][:, c0:512], Mh[hh][:, 127 + moff + c0:127 + moff + 512],
                        ALU.mult, [pk, ("Mh", hh)], [pk])
                else:
                    MM(ps[ob][:, c0:512], Vd4[:, kb, hh, :], Pt[pslot][:, c0:512], idx == 0, idx == n - 1,
                       [pk, ("Vd", kb // 4), "Vd_ones"], [("ps", ob)])

        def evac(tk):
            ob = tk["ob"]
            if tk["kind"] == "mla":
                ACT(ocp[2], ps[ob][:, :], AF.Copy, [("ps", ob)], [("ocp", 2)])
            else:
                CP("dve", ocp[tk["hh"]][0:65, :], ps[ob][0:65, :], [("ps", ob)], [("ocp", tk["hh"])])

        def tail(tk):
            T, hh = tk["T"], tk["hh"]
            u, col = T // 2, (T % 2) * TT
            if tk["kind"] == "mla":
                o2 = ocp[2]
                MM(ps[6][:, :], e64, acc64, True, True, ["acc64", "cmat"], [("ps", 6)])
                yield
                ACT(o2[64:65, :], o2[64:65, :], AF.Ln, [("ocp", 2)], [("ocp", 2)])
                ACT(o2[64:65, :], o2[64:65, :], AF.Exp, [("ocp", 2)], [("ocp", 2)], scale=-1.0)
                yield
                MM(ps[7][:, :], one_f[64:65, :], o2[64:65, :], True, True, [("ocp", 2), "cmat"], [("ps", 7)])
                yield
                ACT(o2[64:65, :], ps[6][64:65, :], AF.Copy, [("ps", 6), ("ocp", 2)], [("ocp", 2)])
                yield
                TTO("dve", o2, o2, ps[7][:, :], ALU.mult, [("ocp", 2), ("ps", 7)], [("ocp", 2)])
                DMA("sp", ("ocp", 2), b2_d[u][128:256, col:col + TT], ocp[2], [("ocp", 2)], [("b2", u)])
                yield
            else:
                o_ = ocp[hh]
                ACT(o_[64:65, :], o_[64:65, :], AF.Ln, [("ocp", hh)], [("ocp", hh)])
                ACT(o_[64:65, :], o_[64:65, :], AF.Exp, [("ocp", hh)], [("ocp", hh)], scale=-1.0)
                yield
                MM(ps[6][0:64, :], one_f[64:65, 0:64], o_[64:65, :], True, True, [("ocp", hh), "cmat"], [("ps", 6)])
                yield
                TTO("dve", o_[0:64, :], o_[0:64, :], ps[6][0:64, :], ALU.mult, [("ocp", hh), ("ps", 6)], [("ocp", hh)])
                DMA("sp", ("ocp", hh), b2_d[u][64 * hh:64 * hh + 64, col:col + TT], o_[0:64, :],
                    [("ocp", hh)], [("b2", u)])
                yield

        def cc_step(u):
            P.cc(("cc", "g2", u), lambda e, u=u: e.collective_compute(
                "AllGather", ALU.bypass, replica_groups=GROUPS, ins=[b2_d[u][:, :]],
                outs=[g2_d[u * 1024:(u + 1) * 1024, :]]),
                reads=[("b2", u)] + ([("g2", u - 1)] if u > 0 else []), writes=[("g2", u)])
            yield

        from collections import deque
        dq = deque()

        def cast_step(kind, r0, r1):
            src = wo_d if kind == "o" else wf_d[(2, kind)]
            DMA("pool", ("w2bf", kind), w2bf_d[kind][r0:r1, :], src[r0:r1, :], [], [("w2bf", kind)])
            yield

        cast_jobs = ([("g", 256 * i, 256 * i + 256) for i in range(4)] + [("u", 256 * i, 256 * i + 256) for i in range(4)]
                     + [("d", 704 * i, 704 * i + 704) for i in range(4)] + [("o", 0, D)])
        for _ in b0(0):
            pass
        NS = len(stream)
        pend_evac = {}
        next_s = [0]

        def emit_s_upto(p_):
            while next_s[0] < NS and next_s[0] - stream[next_s[0]]["la"] <= p_:
                s_stage(next_s[0])
                next_s[0] += 1

        emit_s_upto(-1)
        for p_ in range(NS):
            tk = stream[p_]
            T = tk["T"]
            if p_ == first_pos[T] and T + 1 < NQT:
                b0s = P.capture(b0(T + 1))
                if 1 <= T <= len(cast_jobs):
                    b0s[30:30] = P.capture(cast_step(*cast_jobs[T - 1]))
                dq.extend(b0s)
            if p_ in pend_evac:
                etk = pend_evac.pop(p_)
                evac(etk)
                dq.extend(P.capture(tail(etk)))
                if etk["kind"] == "dil" and etk["hh"] == 1 and etk["T"] % 2 == 1:
                    dq.extend(P.capture(cc_step(etk["T"] // 2)))
            emit_s_upto(p_)
            main_stage(p_, 0)
            if tk["idx"] == tk["n"] - 1:
                pend_evac[p_ + 2] = tk
            nxt_first = first_pos.get(T + 1, NS)
            rem = nxt_first - LA - 1 - p_
            if rem <= 0:
                k = len(dq)
            else:
                k = -(-len(dq) // rem)
            for _ in range(min(k, len(dq))):
                P.replay(dq.popleft())
            main_stage(p_, 1)
        for p_ in sorted(pend_evac):
            etk = pend_evac[p_]
            evac(etk)
            dq.extend(P.capture(tail(etk)))
            if etk["kind"] == "dil" and etk["hh"] == 1 and etk["T"] % 2 == 1:
                dq.extend(P.capture(cc_step(etk["T"] // 2)))
        while dq:
            P.replay(dq.popleft())
        P.barrier()

        A.release()
        sg = sg_A
        wl = load_ffn_weights(2)
        g2v = g2_d.ap().rearrange("(u k p) n -> u p k n", u=8, k=8, p=128)
        wo_v = w2bf_d["o"].ap().rearrange("(k p) d -> p k d", p=128)
        XK0, XK1 = ("xT", 0), ("xT", 1)

        def oT_load(t):
            col = (t % 2) * TT

            def ld(e, t=t, col=col):
                pid = nc.partition_id([e.engine])
                uu = (pid % 4) * 2 + (t // 2)
                return e.dma_start(out=oT3, in_=g2v[bass.ds(uu, 1), :, :, col:col + TT].rearrange("1 p k n -> p k n"))
            P.dma("sp", XK1, ld, reads=[("g2", 6 + t // 2)], writes=[XK1])

        def stats_pair_g():
            ACT(rstd, oT3[:, 0, :], AF.Square, [XK1], ["rstd"])
            ACT(rstdQ, oT3[:, 1, :], AF.Square, [XK1], ["rstdQ"])
            yield
            for r in range(1, 4):
                ACT(sg[0], oT3[:, 2 * r, :], AF.Square, [XK1], [("sg", 0)])
                ACT(sq2, oT3[:, 2 * r + 1, :], AF.Square, [XK1], ["sq2"])
                yield
                TTO("dve", rstd, rstd, sg[0], ALU.add, ["rstd", ("sg", 0)], ["rstd"])
                TTO("dve", rstdQ, rstdQ, sq2, ALU.add, ["rstdQ", "sq2"], ["rstdQ"])
                yield
            MM(ps[6][:, :], o512, rstd, True, True, ["rstd", "cmat"], [("ps", 6)])
            MM(ps[7][:, :], o512, rstdQ, True, True, ["rstdQ", "cmat"], [("ps", 7)])
            yield
            ACT(rstd, ps[6][:, :], AF.Ln, [("ps", 6), "cmat"], ["rstd"], bias=eps_c)
            ACT(rstdQ, ps[7][:, :], AF.Ln, [("ps", 7), "cmat"], ["rstdQ"], bias=eps_c)
            ACT(rstd, rstd, AF.Exp, ["rstd"], ["rstd"], scale=-0.5)
            ACT(rstdQ, rstdQ, AF.Exp, ["rstdQ"], ["rstdQ"], scale=-0.5)
            yield

        oT_load(0)
        DMA("sp", XK0, xT3, xs_v[:, :, 0:TT], [("xs", 0)], [XK0])
        DMA("sp", "woA", wo_resA, wo_v[:, :, 0:768], [("w2bf", "o")],
            ["woA", ("xin", 0), ("xin", 1), "h2b", ("ostg", 0), ("ostg", 1)])
        DMA("sp", "woB", wo_resB, wo_v[:, :, 768:1024], [("w2bf", "o")], ["woB", "actT"])
        for _ in stats_pair_g():
            pass
        for t in range(NT):
            for r in range(4):
                STT("dve", hT3[:, 2 * r, :], oT3[:, 2 * r, :], gv[:, 35 + r:36 + r], rstd, ALU.mult, ALU.mult,
                    [XK1, "gv", "rstd"], ["hT"])
                STT("dve", hT3[:, 2 * r + 1, :], oT3[:, 2 * r + 1, :], gv[:, 39 + r:40 + r], rstdQ, ALU.mult, ALU.mult,
                    [XK1, "gv", "rstdQ"], ["hT"])
            nxt = deque()
            if t + 1 < NT:
                oT_load(t + 1)
                nxt.extend(P.capture(stats_pair_g()))
            for w in wl[t * 5:(t + 1) * 5] if t + 1 < NT else wl[(NT - 1) * 5:]:
                w()
            for d in range(8):
                s = d % 2
                for k in range(8):
                    if d < 6:
                        MM(ps[4 + s][:, :], wo_resA[:, k, d * 128:(d + 1) * 128], hT3[:, k, :], k == 0, k == 7,
                           ["hT", "woA"], [("ps", 4 + s)])
                    else:
                        MM(ps[4 + s][:, :], wo_resB[:, k, (d - 6) * 128:(d - 5) * 128], hT3[:, k, :], k == 0, k == 7,
                           ["hT", "woB"], [("ps", 4 + s)])
                TTO("dve", xT3[:, d, :], ps[4 + s][:, :], xT3[:, d, :], ALU.add, [("ps", 4 + s), XK0], [XK0])
                if t == NT - 1:
                    if d == 0:
                        ACT(rstd, xT3[:, 0, :], AF.Square, [XK0], ["rstd"])
                    else:
                        ACT(sq2, xT3[:, d, :], AF.Square, [XK0], ["sq2"])
                        TTO("dve", rstd, rstd, sq2, ALU.add, ["rstd", "sq2"], ["rstd"])
                if d >= 3:
                    for _ in range(2):
                        if nxt:
                            P.replay(nxt.popleft())
            while nxt:
                P.replay(nxt.popleft())
            if t == NT - 1:
                MM(ps[6][:, :], o1024, rstd, True, True, ["rstd", "cmat"], [("ps", 6)])
                ACT(rstd, ps[6][:, :], AF.Ln, [("ps", 6), "cmat"], ["rstd"], bias=eps_c)
                ACT(rstd, rstd, AF.Exp, ["rstd"], ["rstd"], scale=-0.5)
            if t + 1 < NT:
                DMA("sp", XK0, xs_v[:, :, t * TT:(t + 1) * TT], xT3, [XK0], [("xs", t)])
                DMA("sp", XK0, xT3, xs_v[:, :, (t + 1) * TT:(t + 2) * TT], [("xs", t + 1)], [XK0])

        stores = []

        C2_ORDER = [NT - 1] + list(range(NT - 1))

        def pre_C(i):
            b = i % 2
            t = C2_ORDER[i]
            if i > 0:
                DMA("sp", ("xT", b), xTb[b], xs_v[:, :, t * TT:(t + 1) * TT], [("xs", t)], [("xT", b)])
                yield
                for _ in range(5):
                    yield "pad"
                yield from stats8_g(xTb[b], ("xT", b), rstd, "rstd", 6)

        def post_C(i):
            b = i % 2
            t = C2_ORDER[i]
            for s_ in range(4):
                slot = s_ % 2
                for half in range(2):
                    bank = 6 + half
                    for cc in range(4):
                        c = half * 4 + cc
                        P.op("pe", lambda e, bank=bank, cc=cc, c=c, s_=s_, b=b: e.transpose(
                            out=ps[bank][:, cc * 128:(cc + 1) * 128], in_=xTb[b][:, c, s_ * 128:(s_ + 1) * 128],
                            identity=ident), reads=[("xT", b), "ident"], writes=[("ps", bank)])
                yield
                ACT(ostg[slot][:, 0:512], ps[6][:, :], AF.Copy, [("ps", 6)], [("ostg", slot)])
                CP("dve", ostg[slot][:, 512:1024], ps[7][:, :], [("ps", 7)], [("ostg", slot)])
                yield
                r0 = t * TT + s_ * 128
                DMA("sp", ("ostg", slot), out_d[r0:r0 + 128, :], ostg[slot], [("ostg", slot)], [("out", t, s_)])
                yield

        for _ in pre_C(0):
            pass
        ht_stage(0, 16)
        dqC = deque()
        post_caps = []
        for t in range(NT):
            if t >= 1:
                dqC.extend(P.capture(post_C(t - 1)))
            if t + 1 < NT:
                dqC.extend(P.capture(pre_C(t + 1)))
            ffn_main(t % 2, dqC, (lambda t=t: ht_stage((t + 1) % 2, 16)) if t + 1 < NT else None)
        for _ in post_C(NT - 1):
            pass

        stores = [P.trk[("out", t, s_)][0] for t in range(NT) for s_ in range(4)]
        P.emit(final_waits=stores)
    return nc


def _t5_bucket(dist):
    max_exact = 16
    d = np.maximum(dist, 1).astype(np.float32)
    large = max_exact + (np.log(d / max_exact) / np.log(2048 / max_exact) * (32 - max_exact)).astype(np.int32)
    large = np.minimum(large, 31)
    return np.where(dist < max_exact, dist, large).astype(np.int32)


def _consts():
    ident = np.eye(128, dtype=np.float32)
    tri = (np.arange(128)[:, None] <= np.arange(128)[None, :]).astype(np.float32)
    dist = np.arange(0, 2049)
    mult = (dist <= 128).astype(np.float32) + ((dist % 4 == 0) & (dist <= 512)) + ((dist % 16 == 0) & (dist <= 2048))
    bucket = _t5_bucket(dist)
    cm = np.zeros((32, FLEN), np.float32)
    cm[bucket, dist + 511] = mult
    inv_freq = (np.float32(10000.0) ** (-np.arange(0, 64, 2, dtype=np.float32) / np.float32(64))).astype(np.float32)
    ang = (np.arange(SEQ, dtype=np.float32)[:, None] * inv_freq[None, :]).astype(np.float32)
    cos = np.cos(ang).astype(np.float32).T
    sin = np.sin(ang).astype(np.float32).T
    cos2 = np.ascontiguousarray(np.concatenate([cos, cos], 0))
    sin2 = np.ascontiguousarray(np.concatenate([-sin, sin], 0))
    return ident, tri, cm, cos2, sin2


def _prep_inputs(inputs):
    f = lambda k: np.asarray(inputs[k], dtype=np.float32)
    x = f("x")
    ident, tri, cm, cos2, sin2 = _consts()
    w_in = f("w_in")[0]
    w_qb = f("mla_w_q_b")[0]
    w_kvb = f("mla_w_kv_b")[0]
    w_out = f("w_out")[0]
    rel = f("rel_bias")
    swp = np.concatenate([np.arange(32, 64), np.arange(0, 32)])

    def pc(v, nc_):
        return np.asarray(v, np.float32).reshape(nc_, 128).T

    gq, gk = f("mla_q_norm")[0], f("mla_k_norm")[0]
    gv = np.zeros((128, 64), np.float32)
    gv[:, 0:8] = pc(f("ffn1_norm")[0], 8)
    gv[:, 8:16] = pc(f("mix_norm")[0], 8)
    gv[:, 16:24] = pc(f("ffn2_norm")[0], 8)
    gv[:, 24] = np.tile(f("dil_q_norm")[0], 2)
    gv[:, 25] = np.tile(f("dil_k_norm")[0], 2)
    gv[:, 26:28] = pc(f("mla_q_a_norm")[0], 2)
    gv[:, 28] = f("mla_kv_a_norm")[0]
    gv[:, 29] = gq[0:128]
    gv[0:64, 30] = gq[128:192]
    gv[0:64, 31] = gq[128:192][swp]
    gv[:, 32] = gk[0:128]
    gv[0:64, 33] = gk[128:192]
    gv[0:64, 34] = gk[128:192][swp]
    gv[:, 35:39] = pc(f("out_norm_dil")[0], 4)
    gv[:, 39:43] = pc(f("out_norm_mla")[0], 4)

    wmaps = {}
    for n, p in ((1, "ffn1"), (2, "ffn2")):
        wmaps["w%dg" % n] = np.ascontiguousarray(f(p + "_w_gate")[0])
        wmaps["w%du" % n] = np.ascontiguousarray(f(p + "_w_up")[0])
        wmaps["w%dd" % n] = np.ascontiguousarray(f(p + "_w_down")[0])
    rows = np.concatenate([np.concatenate([np.arange(128 * r, 128 * r + 128), 512 + np.arange(128 * r, 128 * r + 128)])
                           for r in range(4)])
    wo = np.ascontiguousarray(w_out[rows, :])
    maps = []
    for core in range(8):
        b, j = divmod(core, 4)
        kpe = w_in[:, 1920:1984]
        wsel = np.concatenate([w_in[:, 128 * j:128 * j + 128], w_in[:, 512 + 128 * j:512 + 128 * j + 128],
                               w_in[:, 1024 + 128 * j:1024 + 128 * j + 128], w_in[:, 1536:1792],
                               w_in[:, 1792:1920], kpe, kpe[:, swp]], axis=1)
        qr = w_qb[:, 192 * j + 128:192 * j + 192]
        wqb = np.concatenate([w_qb[:, 192 * j:192 * j + 128], qr, qr[:, swp]], axis=1)
        m = {
            "x": np.ascontiguousarray(x[b, j * TOK:(j + 1) * TOK, :]),
            "ident": ident, "gv": gv, "tri": tri, "cm": cm, "cos2": cos2, "sin2": sin2,
            "wsel": np.ascontiguousarray(wsel), "wqb": np.ascontiguousarray(wqb),
            "wkvb": np.ascontiguousarray(w_kvb[:, 256 * j:256 * j + 256]),
            "wo": wo, "relbT": np.ascontiguousarray(rel[2 * j:2 * j + 2, :].T),
        }
        m.update(wmaps)
        maps.append(m)
    return maps


_NC_CACHE = {}


def run_raw(inputs, stage="full", trace=False):
    if stage not in _NC_CACHE:
        _NC_CACHE[stage] = build_nc(stage)
    nc = _NC_CACHE[stage]
    maps = _prep_inputs(inputs)
    return run_bass_kernel_spmd(nc, maps, core_ids=list(range(8)), trace=trace)


def kernel(**inputs):
    res = run_raw(inputs)
    out = np.zeros((2, SEQ, D), np.float32)
    for core in range(8):
        b, j = divmod(core, 4)
        out[b, j * TOK:(j + 1) * TOK, :] = res.results[core]["out"]
    return out
```

```python
import math
import numpy as np
import ml_dtypes
import concourse.bass as bass
import concourse.mybir as mybir
from concourse.bass_utils import run_bass_kernel_spmd

F32 = mybir.dt.float32
BF16 = mybir.dt.bfloat16
AF = mybir.ActivationFunctionType
ALU = mybir.AluOpType

D = 1024
DFF = 2816
NFF = DFF // 128
SEQ = 8192
TOK = 2048
TT = 512
NT = TOK // TT
EPS = 1e-6
ENGS = ("pe", "act", "dve", "pool", "sp")


class H:
    __slots__ = ("eng", "sig", "val", "dma")

    def __init__(self, eng):
        self.eng = eng
        self.sig = False
        self.val = None
        self.dma = None


class Prog:
    def __init__(self, nc):
        self.nc = nc
        self.streams = {e: [] for e in ENGS}
        self.dma_cnt = {}
        self.trk = {}
        self.pending = {}
        self.all_dma = []
        self.cap = None

    def op(self, eng, fn, deps=(), reads=(), writes=(), _async=False):
        if self.cap is not None:
            self.cap.append(("op", eng, fn, tuple(reads), tuple(writes)))
            return None
        h = H(eng)
        deps = [d for d in deps if d is not None] + self.pending.pop(eng, [])
        trk = self.trk
        for k in list(reads) + list(writes):
            w = trk.setdefault(k, [None, {}])
            if w[0] is not None:
                deps.append(w[0])
        for k in writes:
            deps.extend(trk[k][1].values())
        for k in writes:
            trk[k][0] = h
            trk[k][1] = {}
        for k in reads:
            rk = id(h) if _async else eng
            trk[k][1][rk] = h
        deps = [d for d in deps if d is not h]
        for d in deps:
            if d.dma is None and d.eng != eng:
                d.sig = True
        self.streams[eng].append((h, fn, deps))
        return h

    def dma(self, eng, key, fn, deps=(), reads=(), writes=()):
        if self.cap is not None:
            self.cap.append(("dma", eng, key, fn, tuple(reads), tuple(writes)))
            return None
        h = self.op(eng, fn, deps, reads, writes, _async=True)
        self.dma_cnt[key] = self.dma_cnt.get(key, 0) + 16
        h.dma = (key, self.dma_cnt[key])
        self.all_dma.append(h)
        return h

    def capture(self, gen):
        steps = []
        self.cap = cur = []
        for y in gen:
            if cur or y == "pad":
                steps.append(cur)
            self.cap = cur = []
        if cur:
            steps.append(cur)
        self.cap = None
        return steps

    def replay(self, step):
        for rec in step:
            if rec[0] == "op":
                self.op(rec[1], rec[2], (), rec[3], rec[4])
            elif rec[0] == "dma":
                self.dma(rec[1], rec[2], rec[3], (), rec[4], rec[5])
            else:
                self.cc(rec[1], rec[2], (), rec[3], rec[4])

    def cc(self, key, fn, deps=(), reads=(), writes=()):
        if self.cap is not None:
            self.cap.append(("cc", key, fn, tuple(reads), tuple(writes)))
            return None
        h = self.op("pool", fn, deps, reads, writes, _async=True)
        self.dma_cnt[key] = self.dma_cnt.get(key, 0) + 1
        h.dma = (key, self.dma_cnt[key])
        self.all_dma.append(h)
        return h

    def barrier(self):
        deps = [h for h in self.all_dma if not (isinstance(h.dma[0], tuple) and h.dma[0][0] == "cc")]
        self.all_dma = []
        for e in ENGS:
            for (h, fn, d) in reversed(self.streams[e]):
                if h.dma is None:
                    deps.append(h)
                    break
        for d in deps:
            if d.dma is None:
                d.sig = True
        self.pending = {e: list(deps) for e in ENGS}

    def emit(self, final_waits=()):
        nc = self.nc
        for e in ENGS:
            c = 0
            for (h, fn, deps) in self.streams[e]:
                if h.dma is None and h.sig:
                    c += 1
                    h.val = c
        import contextlib
        with contextlib.ExitStack() as es:
            esem = {e: es.enter_context(nc.semaphore("s_" + e)) for e in ENGS}
            dsem = {k: es.enter_context(nc.semaphore("d%d" % i)) for i, k in enumerate(self.dma_cnt)}
            block = es.enter_context(nc.Block())

            def run(e, engobj):
                seen = {}
                for (h, fn, deps) in self.streams[e]:
                    for d in deps:
                        if d.dma is not None:
                            k, v = ("d", d.dma[0]), d.dma[1]
                            sem = dsem[d.dma[0]]
                        else:
                            if d.eng == e:
                                continue
                            k, v = ("e", d.eng), d.val
                            sem = esem[d.eng]
                        if seen.get(k, 0) >= v:
                            continue
                        seen[k] = v
                        engobj.wait_ge(sem, v)
                    ins = fn(engobj)
                    if h.dma is not None:
                        ins.then_inc(dsem[h.dma[0]], 1 if (isinstance(h.dma[0], tuple) and h.dma[0][0] == "cc") else 16)
                    elif h.sig:
                        ins.then_inc(esem[e], 1)
                if e == "sp":
                    for d in final_waits:
                        engobj.wait_ge(dsem[d.dma[0]], d.dma[1])

            @block.tensor
            def _(eng):
                run("pe", eng)

            @block.scalar
            def _(eng):
                run("act", eng)

            @block.vector
            def _(eng):
                run("dve", eng)

            @block.gpsimd
            def _(eng):
                run("pool", eng)

            @block.sync
            def _(eng):
                run("sp", eng)


class Arena:
    def __init__(self, t, nbytes):
        self.t = t
        self.views = {BF16: t, F32: t.bitcast(F32)}
        self.nbytes = nbytes
        self.off = 0
        self.marks = []

    def alloc(self, cols, dtype, parts=128):
        sz = 4 if dtype == F32 else 2
        self.off = (self.off + 63) // 64 * 64
        a = self.off
        self.last = a
        self.off += cols * sz
        assert self.off <= self.nbytes, ("SBUF arena overflow", self.off, self.nbytes)
        return self.views[dtype][0:parts, a // sz: a // sz + cols]

    def mark(self):
        self.marks.append(self.off)

    def release(self):
        self.off = self.marks.pop()


NB = SEQ // 128
SC_D = 0.125
SC_M = 192.0 ** -0.5
FLEN = 3072
MW = 2944


def build_nc(stage="full"):
    nc = bass.Bass("TRN2", target_bir_lowering=False)
    P = Prog(nc)

    def din(name, shape, dt=F32):
        return nc.dram_tensor(name, list(shape), dt, kind="ExternalInput")

    x_d = din("x", [TOK, D])
    ident_d = din("ident", [128, 128])
    gv_d = din("gv", [128, 64])
    wf_d = {(n, k): din("w%d%s" % (n, k), [D, DFF] if k != "d" else [DFF, D])
            for n in (1, 2) for k in ("g", "u", "d")}
    wsel_d = din("wsel", [D, 896])
    wqb_d = din("wqb", [256, 256])
    wkvb_d = din("wkvb", [128, 256])
    wo_d = din("wo", [D, D])
    relbT_d = din("relbT", [32, 2])
    cm_d = din("cm", [32, FLEN])
    tri_d = din("tri", [128, 128])
    cos_d = din("cos2", [64, SEQ])
    sin_d = din("sin2", [64, SEQ])
    out_d = nc.dram_tensor("out", [TOK, D], F32, kind="ExternalOutput")
    xs_d = nc.dram_tensor("xs", [D, TOK], F32)
    b1_d = [nc.dram_tensor("b1_%d" % t, [D, TT], BF16) for t in range(NT)]
    g1_d = [nc.dram_tensor("g1_%d" % t, [4 * D, TT], BF16) for t in range(NT)]
    b2_d = [nc.dram_tensor("b2_%d" % u, [256, 1024], F32) for u in range(8)]
    g2_d = nc.dram_tensor("g2", [8 * 1024, 1024], F32)
    fvec_d = nc.dram_tensor("fvec", [2, FLEN], BF16)
    w2bf_d = {"g": nc.dram_tensor("w2g_bf", [D, DFF], BF16), "u": nc.dram_tensor("w2u_bf", [D, DFF], BF16),
              "d": nc.dram_tensor("w2d_bf", [DFF, D], BF16), "o": nc.dram_tensor("wo_bf", [D, D], BF16)}
    GROUPS = [[0, 1, 2, 3], [4, 5, 6, 7]]
    wselbf_d = nc.dram_tensor("wsel_bf", [D, 896], BF16)
    wqbbf_d = nc.dram_tensor("wqb_bf", [256, 256], BF16)
    wkvbbf_d = nc.dram_tensor("wkvb_bf", [128, 256], BF16)

    import contextlib
    with contextlib.ExitStack() as es:
        ARENA_BYTES = 206 * 1024
        big = es.enter_context(nc.sbuf_tensor("arena", [128, ARENA_BYTES // 2], BF16))
        A = Arena(big, ARENA_BYTES)
        ps = [es.enter_context(nc.psum_tensor("ps%d" % i, [128, 512], F32)) for i in range(8)]

        def MM(out, lhsT, rhs, start, stop, reads, writes):
            return P.op("pe", lambda e: e.matmul(out, lhsT=lhsT, rhs=rhs, start=start, stop=stop),
                        reads=reads, writes=writes)

        def ACT(out, in_, func, reads, writes, scale=1.0, bias=None):
            if bias is None:
                return P.op("act", lambda e: e.activation(out=out, in_=in_, func=func, scale=scale),
                            reads=reads, writes=writes)
            return P.op("act", lambda e: e.activation(out=out, in_=in_, func=func, scale=scale, bias=bias),
                        reads=reads, writes=writes)

        def STT(eng, out, in0, scalar, in1, op0, op1, reads, writes):
            return P.op(eng, lambda e: e.scalar_tensor_tensor(out=out, in0=in0, scalar=scalar, in1=in1,
                                                              op0=op0, op1=op1), reads=reads, writes=writes)

        def TTO(eng, out, in0, in1, op, reads, writes):
            return P.op(eng, lambda e: e.tensor_tensor(out=out, in0=in0, in1=in1, op=op),
                        reads=reads, writes=writes)

        def CP(eng, out, in_, reads, writes):
            return P.op(eng, lambda e: e.tensor_copy(out=out, in_=in_), reads=reads, writes=writes)

        def RECIP(out, in_, reads, writes):
            return P.op("dve", lambda e: e.reciprocal(out=out, in_=in_), reads=reads, writes=writes)

        def MEMSET(ap, v, writes):
            return P.op("dve", lambda e: e.memset(ap, v), writes=writes)

        def DMA(q, key, out, in_, reads, writes):
            return P.dma(q, key, lambda e: e.dma_start(out=out, in_=in_), reads=reads, writes=writes)

        ident = A.alloc(128, F32)
        o1024 = A.alloc(128, F32)
        o512 = A.alloc(128, F32)
        o256 = A.alloc(128, F32)
        o192 = A.alloc(128, F32)
        o128 = A.alloc(128, F32)
        bd64 = A.alloc(128, F32)
        one_f = A.alloc(128, F32)
        one_bf = A.alloc(128, BF16)
        tri = A.alloc(128, BF16)
        gv = A.alloc(64, F32)
        eps_c = A.alloc(1, F32)
        e64 = A.alloc(128, F32)
        DMA("sp", "c_ident", ident, ident_d[:, :], [], ["ident"])
        DMA("sp", "c_gv", gv, gv_d[:, :], [], ["gv"])
        DMA("pool", "c_tri", tri, tri_d[:, :], [], ["tri"])
        for ap_, v_ in ((o1024, 1.0 / 1024), (o512, 1.0 / 512), (o256, 1.0 / 256), (o192, 1.0 / 192),
                        (o128, 1.0 / 128), (one_f, 1.0), (one_bf, 1.0), (eps_c, EPS), (bd64, 0.0)):
            MEMSET(ap_, v_, ["cmat"])
        MEMSET(e64, 0.0, ["cmat"])
        MEMSET(e64[:, 64:65], 1.0, ["cmat"])
        MEMSET(bd64[0:64, 0:64], 1.0 / 64, ["cmat"])
        MEMSET(bd64[64:128, 64:128], 1.0 / 64, ["cmat"])
        A.mark()

        wg3 = A.alloc(8 * DFF, BF16).rearrange("p (c f) -> p c f", c=8)
        wu3 = A.alloc(8 * DFF, BF16).rearrange("p (c f) -> p c f", c=8)
        wd3 = A.alloc(NFF * D, BF16).rearrange("p (f d) -> p f d", f=NFF)
        A.mark()
        xin = [A.alloc(TT, F32) for _ in range(2)]
        _xo = A.last - TT * 4
        h2b = A.alloc(8 * TT, BF16)
        assert A.last == _xo + 2 * TT * 4
        wo_resA = A.views[BF16][0:128, _xo // 2:_xo // 2 + 8 * 768].rearrange("p (k d) -> p k d", k=8)
        h2v = h2b.rearrange("p (c t) -> p c t", c=8)
        ostg = [h2b.bitcast(F32)[:, i * D:(i + 1) * D] for i in range(2)]
        xTb = [A.alloc(8 * TT, F32).rearrange("p (c t) -> p c t", c=8) for _ in range(2)]
        xT3 = xTb[0]
        oT3 = xTb[1]
        hT3 = A.alloc(8 * TT, BF16).rearrange("p (c t) -> p c t", c=8)
        GF = [(0, 6), (6, 12), (12, 17), (17, 22)]
        actT = A.alloc(6 * TT, BF16)
        actT3 = actT.rearrange("p (f t) -> p f t", f=6)
        wo_resB = actT[:, 0:2048].rearrange("p (k d) -> p k d", k=8)
        sg = [A.alloc(TT, F32) for _ in range(2)]
        rstd = A.alloc(TT, F32)
        rstdQ = A.alloc(TT, F32)
        sq2 = A.alloc(TT, F32)

        FP, FD = 512, 4

        def load_ffn_weights(n):
            ops = []
            if n == 1:
                srcs = {k: wf_d[(n, k)].ap() for k in "gud"}
                q, rd = "pool", {k: [] for k in "gud"}
            else:
                srcs = {k: w2bf_d[k].ap() for k in "gud"}
                q, rd = "act", {k: [("w2bf", k)] for k in "gud"}
            gv_ = srcs["g"].rearrange("(c p) f -> p c f", p=128)
            uv_ = srcs["u"].rearrange("(c p) f -> p c f", p=128)
            dv_ = srcs["d"].rearrange("(f p) d -> p f d", p=128)
            for i, f0 in enumerate(range(0, DFF, FP)):
                f1 = min(DFF, f0 + FP)
                ops.append(lambda i=i, f0=f0, f1=f1: DMA(q, ("wg", i), wg3[:, :, f0:f1], gv_[:, :, f0:f1], rd["g"], [("wg", i)]))
                ops.append(lambda i=i, f0=f0, f1=f1: DMA(q, ("wu", i), wu3[:, :, f0:f1], uv_[:, :, f0:f1], rd["u"], [("wu", i)]))
            for i, f0 in enumerate(range(0, NFF, FD)):
                f1 = min(NFF, f0 + FD)
                ops.append(lambda i=i, f0=f0, f1=f1: DMA(q, ("wd", i), wd3[:, f0:f1, :], dv_[:, f0:f1, :], rd["d"], [("wd", i)]))
            return ops

        def stats(srcs, ones_m, out_rstd, out_key, bank=6):
            for i, (src, p, rk) in enumerate(srcs):
                if i == 0:
                    ACT(out_rstd[0:p, :], src, AF.Square, rk, [out_key])
                else:
                    s_ = i % 2
                    ACT(sg[s_][0:p, :], src, AF.Square, rk, [("sg", s_)])
                    TTO("dve", out_rstd[0:p, :], out_rstd[0:p, :], sg[s_][0:p, :], ALU.add, [out_key, ("sg", s_)], [out_key])
            MM(ps[bank][:, :], ones_m, out_rstd, True, True, [out_key, "cmat"], [("ps", bank)])
            ACT(out_rstd, ps[bank][:, :], AF.Ln, [("ps", bank), "cmat"], [out_key], bias=eps_c)
            ACT(out_rstd, out_rstd, AF.Exp, [out_key], [out_key], scale=-0.5)

        from collections import deque

        def stats8_g(xt3, xkey, out_rstd, out_key, bank):
            ACT(out_rstd, xt3[:, 0, :], AF.Square, [xkey], [out_key])
            yield
            for c in range(1, 8):
                ACT(sq2, xt3[:, c, :], AF.Square, [xkey], ["sq2"])
                yield
                TTO("dve", out_rstd, out_rstd, sq2, ALU.add, [out_key, "sq2"], [out_key])
                yield
            MM(ps[bank][:, :], o1024, out_rstd, True, True, [out_key, "cmat"], [("ps", bank)])
            yield
            ACT(out_rstd, ps[bank][:, :], AF.Ln, [("ps", bank), "cmat"], [out_key], bias=eps_c)
            ACT(out_rstd, out_rstd, AF.Exp, [out_key], [out_key], scale=-0.5)
            yield

        def ht_stage(b, gcol):
            for c in range(8):
                STT("dve", hT3[:, c, :], xTb[b][:, c, :], gv[:, gcol + c:gcol + c + 1], rstd, ALU.mult, ALU.mult,
                    [("xT", b), "gv", "rstd"], ["hT"])

        def ffn_main(b, dq, nxt_ht=None):
            xt = xTb[b]
            xk = ("xT", b)
            slots = [2 * NFF + 8 * (len(GF) - 1)]

            def fill():
                if dq:
                    k = -(-len(dq) // max(1, slots[0]))
                    for _ in range(min(k, len(dq))):
                        P.replay(dq.popleft())
                slots[0] -= 1

            for gi, (f0, f1) in enumerate(GF):
                for f in range(f0, f1):
                    s_ = f % 2
                    for c in range(8):
                        MM(ps[s_][:, :], wg3[:, c, f * 128:(f + 1) * 128], hT3[:, c, :], c == 0, c == 7,
                           ["hT", ("wg", f * 128 // FP)], [("ps", s_)])
                    fill()
                    for c in range(8):
                        MM(ps[2 + s_][:, :], wu3[:, c, f * 128:(f + 1) * 128], hT3[:, c, :], c == 0, c == 7,
                           ["hT", ("wu", f * 128 // FP)], [("ps", 2 + s_)])
                    ACT(sg[s_], ps[s_][:, :], AF.Silu, [("ps", s_)], [("sg", s_)])
                    TTO("dve", actT3[:, f - f0, :], ps[2 + s_][:, :], sg[s_], ALU.mult,
                        [("ps", 2 + s_), ("sg", s_)], ["actT"])
                    fill()
                if gi == len(GF) - 1:
                    while dq:
                        P.replay(dq.popleft())
                    if nxt_ht is not None:
                        nxt_ht()
                for d in range(8):
                    s_ = d % 2
                    for f in range(f0, f1):
                        MM(ps[4 + s_][:, :], wd3[:, f, d * 128:(d + 1) * 128], actT3[:, f - f0, :], f == f0, f == f1 - 1,
                           ["actT", ("wd", f // FD)], [("ps", 4 + s_)])
                    STT("dve", xt[:, d, :], ps[4 + s_][:, :], 0.5, xt[:, d, :], ALU.mult, ALU.add,
                        [("ps", 4 + s_), xk], [xk])
                    if gi < len(GF) - 1:
                        fill()

        xs_v = xs_d.ap().rearrange("(c p) t -> p c t", p=128)

        def pre_A_loads(t):
            for k in range(2):
                s_, hf = divmod(k, 2)
                r0 = t * TT + s_ * 128
                DMA("sp", ("xin", k % 2), xin[k % 2], x_d[r0:r0 + 128, hf * 512:(hf + 1) * 512], [], [("xin", k % 2)])
                yield

        def pre_A(t):
            b = t % 2
            for k in range(8):
                s_, hf = divmod(k, 2)
                slot = k % 2
                for cc in range(4):
                    P.op("pe", lambda e, hf=hf, cc=cc, slot=slot: e.transpose(
                        out=ps[6 + hf][:, cc * 128:(cc + 1) * 128], in_=xin[slot][:, cc * 128:(cc + 1) * 128],
                        identity=ident), reads=[("xin", slot), "ident"], writes=[("ps", 6 + hf)])
                yield
                dst = xTb[b][:, hf * 4:hf * 4 + 4, s_ * 128:(s_ + 1) * 128]
                src = ps[6 + hf][:, :].rearrange("p (c t) -> p c t", c=4)
                if hf == 0:
                    ACT(dst, src, AF.Copy, [("ps", 6)], [("xT", b)])
                else:
                    CP("dve", dst, src, [("ps", 7)], [("xT", b)])
                if k + 2 < 8:
                    s2, hf2 = divmod(k + 2, 2)
                    r0 = t * TT + s2 * 128
                    DMA("sp", ("xin", slot), xin[slot], x_d[r0:r0 + 128, hf2 * 512:(hf2 + 1) * 512], [], [("xin", slot)])
                yield

        def pre_A_stats(t):
            b = t % 2
            yield from stats8_g(xTb[b], ("xT", b), rstd, "rstd", 6)

        def post_A(t):
            b = t % 2
            DMA("sp", ("xT", b), xs_v[:, :, t * TT:(t + 1) * TT], xTb[b], [("xT", b)], [("xs", t)])
            yield
            yield from stats8_g(xTb[b], ("xT", b), rstdQ, "rstdQ", 7)
            for c in range(8):
                STT("dve", h2v[:, c, :], xTb[b][:, c, :], gv[:, 8 + c:9 + c], rstdQ, ALU.mult, ALU.mult,
                    [("xT", b), "gv", "rstdQ"], ["h2b"])
                if c % 4 == 3:
                    yield

        def post_A2(t):
            DMA("sp", "h2b", b1_d[t].ap().rearrange("(c p) n -> p c n", p=128), h2v, ["h2b"], [("b1", t)])
            yield
            P.cc(("cc", "g1", t), lambda e, t=t: e.collective_compute(
                "AllGather", ALU.bypass, replica_groups=GROUPS, ins=[b1_d[t][:, :]], outs=[g1_d[t][:, :]]),
                reads=[("b1", t)], writes=[("g1", t)])
            yield

        wl1 = load_ffn_weights(1)
        gu = lambda i: [wl1[2 * i], wl1[2 * i + 1]]
        dd = lambda i: [wl1[12 + i]]
        order = gu(0) + gu(1) + dd(0) + dd(1) + gu(2) + dd(2) + gu(3) + gu(4) + dd(3) + dd(4) + gu(5) + dd(5)
        assert len(order) == len(wl1) == 18
        for w in order:
            w()
        for _ in pre_A_loads(0):
            pass
        for _ in pre_A(0):
            pass
        for _ in pre_A_stats(0):
            pass
        ht_stage(0, 0)
        dqA = deque()
        for t in range(NT):
            if t + 1 < NT:
                dqA.extend(P.capture(pre_A_loads(t + 1)))
            if t >= 1:
                dqA.extend(P.capture(post_A(t - 1)))
            if t + 1 < NT:
                dqA.extend(P.capture(pre_A(t + 1)))
            if t >= 1:
                dqA.extend(P.capture(post_A2(t - 1)))
            if t + 1 < NT:
                dqA.extend(P.capture(pre_A_stats(t + 1)))
            if t == 1:
                def precast():
                    DMA("pool", ("pc", "wsel"), wselbf_d[:, :], wsel_d[:, :], [], [("pc", "wsel")])
                    yield
                    DMA("pool", ("pc", "wqb"), wqbbf_d[:, :], wqb_d[:, :], [], [("pc", "wqb")])
                    DMA("pool", ("pc", "wkvb"), wkvbbf_d[:, :], wkvb_d[:, :], [], [("pc", "wkvb")])
                    yield
                dqA.extend(P.capture(precast()))
            ffn_main(t % 2, dqA, (lambda t=t: ht_stage((t + 1) % 2, 0)) if t + 1 < NT else None)
        AB = Arena(big, ARENA_BYTES)
        AB.off = A.marks[0]
        e_wsel3 = AB.alloc(8 * 896, BF16).rearrange("p (c f) -> p c f", c=8)
        e_wqb3 = AB.alloc(2 * 256, BF16).rearrange("p (c f) -> p c f", c=2)
        e_wkvb = AB.alloc(256, BF16)
        AB.alloc(FLEN, BF16)
        AB.alloc(FLEN, BF16)
        e_h2t3 = AB.alloc(8 * TT, BF16).rearrange("p (c t) -> p c t", c=8)
        e_end = AB.off
        WGK = [("wg", i) for i in range(6)]
        DMA("sp", "wsel", e_wsel3, wselbf_d.ap().rearrange("(c p) f -> p c f", p=128), [("pc", "wsel")], ["wsel"] + WGK)
        DMA("sp", "wqb", e_wqb3, wqbbf_d.ap().rearrange("(c p) f -> p c f", p=128), [("pc", "wqb")], ["wqb"] + WGK)
        DMA("sp", "wkvb", e_wkvb, wkvbbf_d[:, :], [("pc", "wkvb")], ["wkvb"] + WGK)
        DMA("sp", "h2t", e_h2t3, g1_d[0].ap()[0:D, :].rearrange("(c p) n -> p c n", p=128), [("g1", 0)], ["h2t"] + WGK)
        for _ in post_A(NT - 1):
            pass
        for _ in post_A2(NT - 1):
            pass
        P.barrier()

        A.release()
        A.release()
        A.mark()
        wsel3 = A.alloc(8 * 896, BF16).rearrange("p (c f) -> p c f", c=8)
        wqb3 = A.alloc(2 * 256, BF16).rearrange("p (c f) -> p c f", c=2)
        wkvb = A.alloc(256, BF16)
        Mh = [A.alloc(FLEN, BF16) for _ in range(2)]
        h2t3 = A.alloc(8 * TT, BF16).rearrange("p (c t) -> p c t", c=8)
        assert A.off == e_end and e_end <= A.marks[0] + 8 * DFF * 2, "early-load buffers must sit inside the FFN1 gate-weight region"
        qdp = [[A.alloc(TT, BF16) for _ in range(2)] for _ in range(2)]
        kdT = A.alloc(SEQ, BF16)
        Vd4 = A.alloc(NB * 2 * 128, BF16).rearrange("p (b h c) -> p b h c", b=NB, h=2)
        KTn = A.alloc(SEQ, BF16)
        KTr = A.alloc(SEQ, BF16)
        Vm3 = A.alloc(NB * 128, BF16).rearrange("p (b c) -> p b c", b=NB)
        cqn3 = A.alloc(2 * TT, BF16).rearrange("p (c t) -> p c t", c=2)
        ckvn = A.alloc(TT, BF16)
        Qn2 = [A.alloc(TT, BF16) for _ in range(2)]
        Qr2 = [A.alloc(TT, BF16) for _ in range(2)]
        sg_A = sg
        sg = [A.alloc(TT, F32) for _ in range(2)]
        rs0 = A.alloc(TT, F32)
        rs1 = A.alloc(TT, F32)
        ra = A.alloc(TT, F32)
        rb = A.alloc(TT, F32)
        cst = A.alloc(TT, F32)
        snt = A.alloc(TT, F32)
        Pt = [A.alloc(TT, BF16) for _ in range(6)]
        pe_a = A.alloc(TT, F32)
        pe_b = A.alloc(TT, F32)
        ocp = [A.alloc(TT, F32) for _ in range(3)]
        cq0_s = A.alloc(TT, F32)
        rinv = A.alloc(TT, F32)
        acc64 = A.alloc(TT, F32)
        rinvd = A.alloc(TT, F32)
        cm_s = A.alloc(FLEN, F32)
        rb_s = A.alloc(2, F32)
        et_s = A.alloc(2, F32)
        fsb = A.alloc(FLEN, BF16)
        V64 = cm_s[:, 0:NB]

        DMA("sp", "cst", cst[0:64, :], cos_d[:, 0:TT], [], ["cst"])
        DMA("sp", "snt", snt[0:64, :], sin_d[:, 0:TT], [], ["snt"])
        MEMSET(Vd4[:, :, :, 64:128], 1.0, ["Vd_ones"])
        MEMSET(KTr[64:128, :], 0.0, ["zpad"])
        MEMSET(Vm3[:, :, 64:65], 1.0, ["Vm_ones"])
        for pb_ in range(2):
            MEMSET(Qr2[pb_][64:128, :], 0.0, ["zpad"])
            MEMSET(qdp[pb_][0][64:128, :], 0.0, ["zpad"])
            MEMSET(qdp[pb_][1][0:64, :], 0.0, ["zpad"])
        DMA("sp", "cm", cm_s[0:32, :], cm_d[:, :], [], ["cm"])
        DMA("sp", "rb", rb_s[0:32, :], relbT_d[:, :], [], ["rb"])
        ACT(et_s[0:32, :], rb_s[0:32, :], AF.Exp, ["rb"], ["et"])
        for n in range(FLEN // 512):
            MM(ps[0][0:2, :], et_s[0:32, 0:2], cm_s[0:32, n * 512:(n + 1) * 512], True, True, ["et", "cm"], [("ps", 0)])
            CP("dve", fsb[0:2, n * 512:(n + 1) * 512], ps[0][0:2, :], [("ps", 0)], ["fsb"])
        DMA("sp", "fsb", fvec_d[:, :], fsb[0:2, :], ["fsb"], ["fvec"])
        LW = FLEN - 1
        MQ = ("sp", "sp")
        for hh in range(2):
            MEMSET(Mh[hh], 0.0, [("Mh", hh)])
        for hh in range(2):
            DMA(MQ[hh], ("Mh", hh), Mh[hh][0:1, 0:LW], fvec_d[hh:hh + 1, 0:LW], ["fvec"], [("Mh", hh)])
        for r_ in range(7):
            n_ = 1 << r_
            for hh in range(2):
                DMA(MQ[hh], ("Mh", hh), Mh[hh][n_:2 * n_, n_:LW], Mh[hh][0:n_, 0:LW - n_], [("Mh", hh)], [("Mh", hh)])

        def stats_g(srcs, ones_m, out_rstd, out_key, bank):
            for i, (src, p, rk) in enumerate(srcs):
                if i == 0:
                    ACT(out_rstd[0:p, :], src, AF.Square, rk, [out_key])
                else:
                    ACT(sg[i % 2][0:p, :], src, AF.Square, rk, [("sg", i % 2)])
            yield
            if len(srcs) > 1:
                for i, (src, p, rk) in enumerate(srcs):
                    if i > 0:
                        TTO("dve", out_rstd[0:p, :], out_rstd[0:p, :], sg[i % 2][0:p, :], ALU.add,
                            [out_key, ("sg", i % 2)], [out_key])
                yield
            MM(ps[bank][:, :], ones_m, out_rstd, True, True, [out_key, "cmat"], [("ps", bank)])
            yield
            ACT(out_rstd, ps[bank][:, :], AF.Ln, [("ps", bank), "cmat"], [out_key], bias=eps_c)
            ACT(out_rstd, out_rstd, AF.Exp, [out_key], [out_key], scale=-0.5)
            yield

        def b0(T):
            r, t = divmod(T, 4)
            pb = T % 2
            Qn, Qr = Qn2[pb], Qr2[pb]
            def load_h2t(T_):
                r_, t_ = divmod(T_, 4)
                DMA("sp", "h2t", h2t3, g1_d[t_].ap()[r_ * D:(r_ + 1) * D, :].rearrange("(c p) n -> p c n", p=128),
                    [("g1", t_)], ["h2t"])

            def load_cs(T_):
                DMA("sp", "cst", cst[0:64, :], cos_d[:, T_ * TT:(T_ + 1) * TT], [], ["cst"])
                DMA("sp", "snt", snt[0:64, :], sin_d[:, T_ * TT:(T_ + 1) * TT], [], ["snt"])


            def proj(bank, lo, ncols):
                for c in range(8):
                    MM(ps[bank][0:ncols, :], wsel3[:, c, lo:lo + ncols], h2t3[:, c, :], c == 0, c == 7,
                       ["h2t", "wsel"], [("ps", bank)])
                    if c == 3:
                        yield
                yield

            def rope(gcol, rs, rskey, dst, dkey):
                STT("dve", ra[0:64, :], pe_a[0:64, :], gv[0:64, gcol:gcol + 1], cst[0:64, :], ALU.mult, ALU.mult,
                    ["pe_a", "gv", "cst"], ["ra"])
                STT("dve", rb[0:64, :], pe_b[0:64, :], gv[0:64, gcol + 1:gcol + 2], snt[0:64, :], ALU.mult, ALU.mult,
                    ["pe_b", "gv", "snt"], ["rb_"])
                yield
                TTO("pool", ra[0:64, :], ra[0:64, :], rb[0:64, :], ALU.add, ["ra", "rb_"], ["ra"])
                TTO("pool", dst, ra[0:64, :], rs[0:64, :], ALU.mult, ["ra", rskey], [dkey])
                yield

            yield from proj(6, 0, 128)
            yield from stats_g([(ps[6][:, :], 128, [("ps", 6)])], bd64, rs0, "rs0", 7)
            STT("dve", qdp[pb][0][0:64, :], ps[6][0:64, :], gv[0:64, 24:25], rs0[0:64, :], ALU.mult, ALU.mult,
                [("ps", 6), "gv", "rs0"], [("qd", pb)])
            STT("dve", qdp[pb][1][64:128, :], ps[6][64:128, :], gv[64:128, 24:25], rs0[64:128, :], ALU.mult, ALU.mult,
                [("ps", 6), "gv", "rs0"], [("qd", pb)])
            yield
            yield from proj(7, 128, 128)
            yield from stats_g([(ps[7][:, :], 128, [("ps", 7)])], bd64, rs1, "rs1", 6)
            STT("dve", kdT[:, T * TT:(T + 1) * TT], ps[7][:, :], gv[:, 25:26], rs1, ALU.mult, ALU.mult,
                [("ps", 7), "gv", "rs1"], [("kd", T)])
            yield
            for sb_ in range(4):
                for c in range(8):
                    MM(ps[6][:, sb_ * 128:(sb_ + 1) * 128], h2t3[:, c, sb_ * 128:(sb_ + 1) * 128],
                       wsel3[:, c, 256:384], c == 0, c == 7, ["h2t", "wsel"], [("ps", 6)])
                yield
            CP("dve", Vd4[:, 4 * T:4 * T + 4, :, 0:64],
               ps[6][:, :].rearrange("p (b h c) -> p b h c", b=4, h=2), [("ps", 6)], [("Vd", T)])
            yield
            yield from proj(7, 384, 128)
            ACT(cq0_s, ps[7][:, :], AF.Copy, [("ps", 7)], ["cq0"])
            yield
            yield from proj(6, 512, 128)
            yield from stats_g([(cq0_s, 128, ["cq0"]), (ps[6][:, :], 128, [("ps", 6)])], o256, rs0, "rs0", 7)
            STT("dve", cqn3[:, 0, :], cq0_s, gv[:, 26:27], rs0, ALU.mult, ALU.mult, ["cq0", "gv", "rs0"], ["cqn"])
            STT("dve", cqn3[:, 1, :], ps[6][:, :], gv[:, 27:28], rs0, ALU.mult, ALU.mult, [("ps", 6), "gv", "rs0"], ["cqn"])
            yield
            yield from proj(7, 640, 128)
            yield from stats_g([(ps[7][:, :], 128, [("ps", 7)])], o128, rs1, "rs1", 6)
            STT("dve", ckvn, ps[7][:, :], gv[:, 28:29], rs1, ALU.mult, ALU.mult, [("ps", 7), "gv", "rs1"], ["ckvn"])
            yield
            yield from proj(6, 768, 64)
            ACT(pe_a[0:64, :], ps[6][0:64, :], AF.Copy, [("ps", 6)], ["pe_a"])
            yield
            yield from proj(7, 832, 64)
            CP("dve", pe_b[0:64, :], ps[7][0:64, :], [("ps", 7)], ["pe_b"])
            if T + 1 < SEQ // TT:
                load_h2t(T + 1)
            yield
            MM(ps[6][:, :], wkvb[:, 0:128], ckvn, True, True, ["wkvb", "ckvn"], [("ps", 6)])
            yield
            yield from stats_g([(ps[6][:, :], 128, [("ps", 6)]), (pe_a[0:64, :], 64, ["pe_a"])], o192, rs0, "rs0", 7)
            STT("dve", KTn[:, T * TT:(T + 1) * TT], ps[6][:, :], gv[:, 32:33], rs0, ALU.mult, ALU.mult,
                [("ps", 6), "gv", "rs0"], [("KTn", T)])
            yield
            yield from rope(33, rs0, "rs0", KTr[0:64, T * TT:(T + 1) * TT], ("KTr", T))
            for sb_ in range(4):
                MM(ps[7][:, sb_ * 128:(sb_ + 1) * 128], ckvn[:, sb_ * 128:(sb_ + 1) * 128], wkvb[:, 128:256],
                   True, True, ["wkvb", "ckvn"], [("ps", 7)])
            yield
            pv_ = ps[7][:, :].rearrange("p (b c) -> p b c", b=4)
            CP("dve", Vm3[:, 4 * T:4 * T + 4, 0:64], pv_[:, :, 0:64], [("ps", 7)], [("Vm", T)])
            CP("dve", Vm3[:, 4 * T:4 * T + 4, 65:128], pv_[:, :, 65:128], [("ps", 7)], [("Vm", T)])
            CP("dve", V64[:, 4 * T:4 * T + 4], pv_[:, :, 64], [("ps", 7)], [("V64", T), "cm"])
            yield
            for c in range(2):
                MM(ps[6][0:64, :], wqb3[:, c, 128:192], cqn3[:, c, :], c == 0, c == 1, ["wqb", "cqn"], [("ps", 6)])
            yield
            ACT(pe_a[0:64, :], ps[6][0:64, :], AF.Copy, [("ps", 6)], ["pe_a"])
            yield
            for c in range(2):
                MM(ps[7][0:64, :], wqb3[:, c, 192:256], cqn3[:, c, :], c == 0, c == 1, ["wqb", "cqn"], [("ps", 7)])
            yield
            CP("dve", pe_b[0:64, :], ps[7][0:64, :], [("ps", 7)], ["pe_b"])
            yield
            for c in range(2):
                MM(ps[6][:, :], wqb3[:, c, 0:128], cqn3[:, c, :], c == 0, c == 1, ["wqb", "cqn"], [("ps", 6)])
            yield
            yield from stats_g([(ps[6][:, :], 128, [("ps", 6)]), (pe_a[0:64, :], 64, ["pe_a"])], o192, rs1, "rs1", 7)
            STT("dve", Qn, ps[6][:, :], gv[:, 29:30], rs1, ALU.mult, ALU.mult, [("ps", 6), "gv", "rs1"], [("Qn", pb)])
            yield
            yield from rope(30, rs1, "rs1", Qr[0:64, :], ("Qr", pb))
            if T + 1 < SEQ // TT:
                load_cs(T + 1)
                yield

        LA = 2
        NPT = len(Pt)
        NQT = SEQ // TT
        stream = []
        job_id = 0
        for T in range(NQT):
            for kind, hh in (("mla", 0), ("dil", 0), ("dil", 1)):
                kbs = list(range(0, 4 * T + 4)) if kind == "mla" else list(range(max(0, 4 * T - 16), 4 * T + 4))
                for idx, kb in enumerate(kbs):
                    stream.append(dict(T=T, kind=kind, hh=hh, kb=kb, idx=idx, n=len(kbs), ob=3 + job_id % 2))
                job_id += 1
        first_pos = {}
        for p_, tk in enumerate(stream):
            first_pos.setdefault(tk["T"], p_)
        last_use = {0: -10, 1: -9, 2: -8, 5: -7}
        for p_, tk in enumerate(stream):
            four = True
            allowed = (0, 1, 2, 5) if four else (0, 1, 2)
            b_ = min(allowed, key=lambda x: last_use[x])
            last_use[b_] = p_
            tk["sb"] = b_
            tk["la"] = 3 if four else 2

        def s_stage(p_):
            tk = stream[p_]
            T, kb, hh = tk["T"], tk["kb"], tk["hh"]
            pb = T % 2
            sbk = tk["sb"]
            i = kb - 4 * T
            if tk["kind"] == "mla":
                c0 = 128 * max(i, 0)
                MM(ps[sbk][:, c0:512], KTn[:, kb * 128:(kb + 1) * 128], Qn2[pb][:, c0:512], True, False,
                   [("KTn", kb // 4), ("Qn", pb)], [("ps", sbk)])
                MM(ps[sbk][:, c0:512], KTr[:, kb * 128:(kb + 1) * 128], Qr2[pb][:, c0:512], False, True,
                   [("KTr", kb // 4), ("Qr", pb), "zpad"], [("ps", sbk)])
            else:
                c0 = 128 * max(i, 0)
                MM(ps[sbk][:, c0:512], kdT[:, kb * 128:(kb + 1) * 128], qdp[pb][hh][:, c0:512],
                   True, True, [("kd", kb // 4), ("qd", pb), "zpad"], [("ps", sbk)])

        def main_stage(p_, part):
            tk = stream[p_]
            T, kb, hh, idx, n, ob = tk["T"], tk["kb"], tk["hh"], tk["idx"], tk["n"], tk["ob"]
            sbk = tk["sb"]
            pslot = p_ % NPT
            pk = ("Pt", pslot)
            i = kb - 4 * T
            if tk["kind"] == "mla":
                c0 = 128 * max(i, 0)
                if part == 0:
                    ACT(Pt[pslot][:, c0:512], ps[sbk][:, c0:512], AF.Exp, [("ps", sbk)], [pk], scale=SC_M)
                    if i >= 0:
                        TTO("dve", Pt[pslot][:, c0:c0 + 128], Pt[pslot][:, c0:c0 + 128], tri, ALU.mult, [pk, "tri"], [pk])
                else:
                    MM(ps[ob][:, c0:512], Vm3[:, kb, :], Pt[pslot][:, c0:512], idx == 0, idx == n - 1,
                       [pk, ("Vm", kb // 4), "Vm_ones"], [("ps", ob)])
                    if idx == 0:
                        P.op("dve", lambda e, pslot=pslot, kb=kb: e.tensor_scalar(
                            out=acc64, in0=Pt[pslot], scalar1=V64[:, kb:kb + 1], scalar2=None, op0=ALU.mult),
                            reads=[pk, ("V64", kb // 4)], writes=["acc64"])
                    else:
                        STT("dve", acc64[:, c0:512], Pt[pslot][:, c0:512], V64[:, kb:kb + 1], acc64[:, c0:512],
                            ALU.mult, ALU.add, [pk, ("V64", kb // 4), "acc64"], ["acc64"])
            else:
                moff = 128 * (4 * T - kb) + 384
                c0 = 128 * max(i, 0)
                if part == 0:
                    ACT(Pt[pslot][:, c0:512], ps[sbk][:, c0:512], AF.Exp, [("ps", sbk)], [pk], scale=SC_D)
                    TTO("dve", Pt[pslot][:, c0:512], Pt[pslot][:, c0:512], Mh[hh][:, 127 + moff + c0:127 + moff + 512],
                        ALU.mult, [pk, ("Mh", hh)], [pk])
                else:
                    MM(ps[ob][:, c0:512], Vd4[:, kb, hh, :], Pt[pslot][:, c0:512], idx == 0, idx == n - 1,
                       [pk, ("Vd", kb // 4), "Vd_ones"], [("ps", ob)])

        def evac(tk):
            ob = tk["ob"]
            if tk["kind"] == "mla":
                ACT(ocp[2], ps[ob][:, :], AF.Copy, [("ps", ob)], [("ocp", 2)])
            else:
                CP("dve", ocp[tk["hh"]][0:65, :], ps[ob][0:65, :], [("ps", ob)], [("ocp", tk["hh"])])

        def tail(tk):
            T, hh = tk["T"], tk["hh"]
            u, col = T // 2, (T % 2) * TT
            if tk["kind"] == "mla":
                o2 = ocp[2]
                MM(ps[6][:, :], e64, acc64, True, True, ["acc64", "cmat"], [("ps", 6)])
                yield
                ACT(o2[64:65, :], o2[64:65, :], AF.Ln, [("ocp", 2)], [("ocp", 2)])
                ACT(o2[64:65, :], o2[64:65, :], AF.Exp, [("ocp", 2)], [("ocp", 2)], scale=-1.0)
                yield
                MM(ps[7][:, :], one_f[64:65, :], o2[64:65, :], True, True, [("ocp", 2), "cmat"], [("ps", 7)])
                yield
                ACT(o2[64:65, :], ps[6][64:65, :], AF.Copy, [("ps", 6), ("ocp", 2)], [("ocp", 2)])
                yield
                TTO("dve", o2, o2, ps[7][:, :], ALU.mult, [("ocp", 2), ("ps", 7)], [("ocp", 2)])
                DMA("sp", ("ocp", 2), b2_d[u][128:256, col:col + TT], ocp[2], [("ocp", 2)], [("b2", u)])
                yield
            else:
                o_ = ocp[hh]
                ACT(o_[64:65, :], o_[64:65, :], AF.Ln, [("ocp", hh)], [("ocp", hh)])
                ACT(o_[64:65, :], o_[64:65, :], AF.Exp, [("ocp", hh)], [("ocp", hh)], scale=-1.0)
                yield
                MM(ps[6][0:64, :], one_f[64:65, 0:64], o_[64:65, :], True, True, [("ocp", hh), "cmat"], [("ps", 6)])
                yield
                TTO("dve", o_[0:64, :], o_[0:64, :], ps[6][0:64, :], ALU.mult, [("ocp", hh), ("ps", 6)], [("ocp", hh)])
                DMA("sp", ("ocp", hh), b2_d[u][64 * hh:64 * hh + 64, col:col + TT], o_[0:64, :],
                    [("ocp", hh)], [("b2", u)])
                yield

        def cc_step(u):
            P.cc(("cc", "g2", u), lambda e, u=u: e.collective_compute(
                "AllGather", ALU.bypass, replica_groups=GROUPS, ins=[b2_d[u][:, :]],
                outs=[g2_d[u * 1024:(u + 1) * 1024, :]]),
                reads=[("b2", u)] + ([("g2", u - 1)] if u > 0 else []), writes=[("g2", u)])
            yield

        from collections import deque
        dq = deque()

        def cast_step(kind, r0, r1):
            src = wo_d if kind == "o" else wf_d[(2, kind)]
            DMA("pool", ("w2bf", kind), w2bf_d[kind][r0:r1, :], src[r0:r1, :], [], [("w2bf", kind)])
            yield

        cast_jobs = ([("g", 256 * i, 256 * i + 256) for i in range(4)] + [("u", 256 * i, 256 * i + 256) for i in range(4)]
                     + [("d", 704 * i, 704 * i + 704) for i in range(4)] + [("o", 0, D)])
        for _ in b0(0):
            pass
        NS = len(stream)
        pend_evac = {}
        next_s = [0]

        def emit_s_upto(p_):
            while next_s[0] < NS and next_s[0] - stream[next_s[0]]["la"] <= p_:
                s_stage(next_s[0])
                next_s[0] += 1

        emit_s_upto(-1)
        for p_ in range(NS):
            tk = stream[p_]
            T = tk["T"]
            if p_ == first_pos[T] and T + 1 < NQT:
                b0s = P.capture(b0(T + 1))
                if 1 <= T <= len(cast_jobs):
                    b0s[30:30] = P.capture(cast_step(*cast_jobs[T - 1]))
                dq.extend(b0s)
            if p_ in pend_evac:
                etk = pend_evac.pop(p_)
                evac(etk)
                dq.extend(P.capture(tail(etk)))
                if etk["kind"] == "dil" and etk["hh"] == 1 and etk["T"] % 2 == 1:
                    dq.extend(P.capture(cc_step(etk["T"] // 2)))
            emit_s_upto(p_)
            main_stage(p_, 0)
            if tk["idx"] == tk["n"] - 1:
                pend_evac[p_ + 2] = tk
            nxt_first = first_pos.get(T + 1, NS)
            rem = nxt_first - LA - 1 - p_
            if rem <= 0:
                k = len(dq)
            else:
                k = -(-len(dq) // rem)
            for _ in range(min(k, len(dq))):
                P.replay(dq.popleft())
            main_stage(p_, 1)
        for p_ in sorted(pend_evac):
            etk = pend_evac[p_]
            evac(etk)
            dq.extend(P.capture(tail(etk)))
            if etk["kind"] == "dil" and etk["hh"] == 1 and etk["T"] % 2 == 1:
                dq.extend(P.capture(cc_step(etk["T"] // 2)))
        while dq:
            P.replay(dq.popleft())
        P.barrier()

        A.release()
        sg = sg_A
        wl = load_ffn_weights(2)
        g2v = g2_d.ap().rearrange("(u k p) n -> u p k n", u=8, k=8, p=128)
        wo_v = w2bf_d["o"].ap().rearrange("(k p) d -> p k d", p=128)
        XK0, XK1 = ("xT", 0), ("xT", 1)

        def oT_load(t):
            col = (t % 2) * TT

            def ld(e, t=t, col=col):
                pid = nc.partition_id([e.engine])
                uu = (pid % 4) * 2 + (t // 2)
                return e.dma_start(out=oT3, in_=g2v[bass.ds(uu, 1), :, :, col:col + TT].rearrange("1 p k n -> p k n"))
            P.dma("sp", XK1, ld, reads=[("g2", 6 + t // 2)], writes=[XK1])

        def stats_pair_g():
            ACT(rstd, oT3[:, 0, :], AF.Square, [XK1], ["rstd"])
            ACT(rstdQ, oT3[:, 1, :], AF.Square, [XK1], ["rstdQ"])
            yield
            for r in range(1, 4):
                ACT(sg[0], oT3[:, 2 * r, :], AF.Square, [XK1], [("sg", 0)])
                ACT(sq2, oT3[:, 2 * r + 1, :], AF.Square, [XK1], ["sq2"])
                yield
                TTO("dve", rstd, rstd, sg[0], ALU.add, ["rstd", ("sg", 0)], ["rstd"])
                TTO("dve", rstdQ, rstdQ, sq2, ALU.add, ["rstdQ", "sq2"], ["rstdQ"])
                yield
            MM(ps[6][:, :], o512, rstd, True, True, ["rstd", "cmat"], [("ps", 6)])
            MM(ps[7][:, :], o512, rstdQ, True, True, ["rstdQ", "cmat"], [("ps", 7)])
            yield
            ACT(rstd, ps[6][:, :], AF.Ln, [("ps", 6), "cmat"], ["rstd"], bias=eps_c)
            ACT(rstdQ, ps[7][:, :], AF.Ln, [("ps", 7), "cmat"], ["rstdQ"], bias=eps_c)
            ACT(rstd, rstd, AF.Exp, ["rstd"], ["rstd"], scale=-0.5)
            ACT(rstdQ, rstdQ, AF.Exp, ["rstdQ"], ["rstdQ"], scale=-0.5)
            yield

        oT_load(0)
        DMA("sp", XK0, xT3, xs_v[:, :, 0:TT], [("xs", 0)], [XK0])
        DMA("sp", "woA", wo_resA, wo_v[:, :, 0:768], [("w2bf", "o")],
            ["woA", ("xin", 0), ("xin", 1), "h2b", ("ostg", 0), ("ostg", 1)])
        DMA("sp", "woB", wo_resB, wo_v[:, :, 768:1024], [("w2bf", "o")], ["woB", "actT"])
        for _ in stats_pair_g():
            pass
        for t in range(NT):
            for r in range(4):
                STT("dve", hT3[:, 2 * r, :], oT3[:, 2 * r, :], gv[:, 35 + r:36 + r], rstd, ALU.mult, ALU.mult,
                    [XK1, "gv", "rstd"], ["hT"])
                STT("dve", hT3[:, 2 * r + 1, :], oT3[:, 2 * r + 1, :], gv[:, 39 + r:40 + r], rstdQ, ALU.mult, ALU.mult,
                    [XK1, "gv", "rstdQ"], ["hT"])
            nxt = deque()
            if t + 1 < NT:
                oT_load(t + 1)
                nxt.extend(P.capture(stats_pair_g()))
            for w in wl[t * 5:(t + 1) * 5] if t + 1 < NT else wl[(NT - 1) * 5:]:
                w()
            for d in range(8):
                s = d % 2
                for k in range(8):
                    if d < 6:
                        MM(ps[4 + s][:, :], wo_resA[:, k, d * 128:(d + 1) * 128], hT3[:, k, :], k == 0, k == 7,
                           ["hT", "woA"], [("ps", 4 + s)])
                    else:
                        MM(ps[4 + s][:, :], wo_resB[:, k, (d - 6) * 128:(d - 5) * 128], hT3[:, k, :], k == 0, k == 7,
                           ["hT", "woB"], [("ps", 4 + s)])
                TTO("dve", xT3[:, d, :], ps[4 + s][:, :], xT3[:, d, :], ALU.add, [("ps", 4 + s), XK0], [XK0])
                if t == NT - 1:
                    if d == 0:
                        ACT(rstd, xT3[:, 0, :], AF.Square, [XK0], ["rstd"])
                    else:
                        ACT(sq2, xT3[:, d, :], AF.Square, [XK0], ["sq2"])
                        TTO("dve", rstd, rstd, sq2, ALU.add, ["rstd", "sq2"], ["rstd"])
                if d >= 3:
                    for _ in range(2):
                        if nxt:
                            P.replay(nxt.popleft())
            while nxt:
                P.replay(nxt.popleft())
            if t == NT - 1:
                MM(ps[6][:, :], o1024, rstd, True, True, ["rstd", "cmat"], [("ps", 6)])
                ACT(rstd, ps[6][:, :], AF.Ln, [("ps", 6), "cmat"], ["rstd"], bias=eps_c)
                ACT(rstd, rstd, AF.Exp, ["rstd"], ["rstd"], scale=-0.5)
            if t + 1 < NT:
                DMA("sp", XK0, xs_v[:, :, t * TT:(t + 1) * TT], xT3, [XK0], [("xs", t)])
                DMA("sp", XK0, xT3, xs_v[:, :, (t + 1) * TT:(t + 2) * TT], [("xs", t + 1)], [XK0])

        stores = []

        C2_ORDER = [NT - 1] + list(range(NT - 1))

        def pre_C(i):
            b = i % 2
            t = C2_ORDER[i]
            if i > 0:
                DMA("sp", ("xT", b), xTb[b], xs_v[:, :, t * TT:(t + 1) * TT], [("xs", t)], [("xT", b)])
                yield
                for _ in range(5):
                    yield "pad"
                yield from stats8_g(xTb[b], ("xT", b), rstd, "rstd", 6)

        def post_C(i):
            b = i % 2
            t = C2_ORDER[i]
            for s_ in range(4):
                slot = s_ % 2
                for half in range(2):
                    bank = 6 + half
                    for cc in range(4):
                        c = half * 4 + cc
                        P.op("pe", lambda e, bank=bank, cc=cc, c=c, s_=s_, b=b: e.transpose(
                            out=ps[bank][:, cc * 128:(cc + 1) * 128], in_=xTb[b][:, c, s_ * 128:(s_ + 1) * 128],
                            identity=ident), reads=[("xT", b), "ident"], writes=[("ps", bank)])
                yield
                ACT(ostg[slot][:, 0:512], ps[6][:, :], AF.Copy, [("ps", 6)], [("ostg", slot)])
                CP("dve", ostg[slot][:, 512:1024], ps[7][:, :], [("ps", 7)], [("ostg", slot)])
                yield
                r0 = t * TT + s_ * 128
                DMA("sp", ("ostg", slot), out_d[r0:r0 + 128, :], ostg[slot], [("ostg", slot)], [("out", t, s_)])
                yield

        for _ in pre_C(0):
            pass
        ht_stage(0, 16)
        dqC = deque()
        post_caps = []
        for t in range(NT):
            if t >= 1:
                dqC.extend(P.capture(post_C(t - 1)))
            if t + 1 < NT:
                dqC.extend(P.capture(pre_C(t + 1)))
            ffn_main(t % 2, dqC, (lambda t=t: ht_stage((t + 1) % 2, 16)) if t + 1 < NT else None)
        for _ in post_C(NT - 1):
            pass

        stores = [P.trk[("out", t, s_)][0] for t in range(NT) for s_ in range(4)]
        P.emit(final_waits=stores)
    return nc


def _t5_bucket(dist):
    max_exact = 16
    d = np.maximum(dist, 1).astype(np.float32)
    large = max_exact + (np.log(d / max_exact) / np.log(2048 / max_exact) * (32 - max_exact)).astype(np.int32)
    large = np.minimum(large, 31)
    return np.where(dist < max_exact, dist, large).astype(np.int32)


def _consts():
    ident = np.eye(128, dtype=np.float32)
    tri = (np.arange(128)[:, None] <= np.arange(128)[None, :]).astype(np.float32)
    dist = np.arange(0, 2049)
    mult = (dist <= 128).astype(np.float32) + ((dist % 4 == 0) & (dist <= 512)) + ((dist % 16 == 0) & (dist <= 2048))
    bucket = _t5_bucket(dist)
    cm = np.zeros((32, FLEN), np.float32)
    cm[bucket, dist + 511] = mult
    inv_freq = (np.float32(10000.0) ** (-np.arange(0, 64, 2, dtype=np.float32) / np.float32(64))).astype(np.float32)
    ang = (np.arange(SEQ, dtype=np.float32)[:, None] * inv_freq[None, :]).astype(np.float32)
    cos = np.cos(ang).astype(np.float32).T
    sin = np.sin(ang).astype(np.float32).T
    cos2 = np.ascontiguousarray(np.concatenate([cos, cos], 0))
    sin2 = np.ascontiguousarray(np.concatenate([-sin, sin], 0))
    return ident, tri, cm, cos2, sin2


def _prep_inputs(inputs):
    f = lambda k: np.asarray(inputs[k], dtype=np.float32)
    x = f("x")
    ident, tri, cm, cos2, sin2 = _consts()
    w_in = f("w_in")[0]
    w_qb = f("mla_w_q_b")[0]
    w_kvb = f("mla_w_kv_b")[0]
    w_out = f("w_out")[0]
    rel = f("rel_bias")
    swp = np.concatenate([np.arange(32, 64), np.arange(0, 32)])

    def pc(v, nc_):
        return np.asarray(v, np.float32).reshape(nc_, 128).T

    gq, gk = f("mla_q_norm")[0], f("mla_k_norm")[0]
    gv = np.zeros((128, 64), np.float32)
    gv[:, 0:8] = pc(f("ffn1_norm")[0], 8)
    gv[:, 8:16] = pc(f("mix_norm")[0], 8)
    gv[:, 16:24] = pc(f("ffn2_norm")[0], 8)
    gv[:, 24] = np.tile(f("dil_q_norm")[0], 2)
    gv[:, 25] = np.tile(f("dil_k_norm")[0], 2)
    gv[:, 26:28] = pc(f("mla_q_a_norm")[0], 2)
    gv[:, 28] = f("mla_kv_a_norm")[0]
    gv[:, 29] = gq[0:128]
    gv[0:64, 30] = gq[128:192]
    gv[0:64, 31] = gq[128:192][swp]
    gv[:, 32] = gk[0:128]
    gv[0:64, 33] = gk[128:192]
    gv[0:64, 34] = gk[128:192][swp]
    gv[:, 35:39] = pc(f("out_norm_dil")[0], 4)
    gv[:, 39:43] = pc(f("out_norm_mla")[0], 4)

    wmaps = {}
    for n, p in ((1, "ffn1"), (2, "ffn2")):
        wmaps["w%dg" % n] = np.ascontiguousarray(f(p + "_w_gate")[0])
        wmaps["w%du" % n] = np.ascontiguousarray(f(p + "_w_up")[0])
        wmaps["w%dd" % n] = np.ascontiguousarray(f(p + "_w_down")[0])
    rows = np.concatenate([np.concatenate([np.arange(128 * r, 128 * r + 128), 512 + np.arange(128 * r, 128 * r + 128)])
                           for r in range(4)])
    wo = np.ascontiguousarray(w_out[rows, :])
    maps = []
    for core in range(8):
        b, j = divmod(core, 4)
        kpe = w_in[:, 1920:1984]
        wsel = np.concatenate([w_in[:, 128 * j:128 * j + 128], w_in[:, 512 + 128 * j:512 + 128 * j + 128],
                               w_in[:, 1024 + 128 * j:1024 + 128 * j + 128], w_in[:, 1536:1792],
                               w_in[:, 1792:1920], kpe, kpe[:, swp]], axis=1)
        qr = w_qb[:, 192 * j + 128:192 * j + 192]
        wqb = np.concatenate([w_qb[:, 192 * j:192 * j + 128], qr, qr[:, swp]], axis=1)
        m = {
            "x": np.ascontiguousarray(x[b, j * TOK:(j + 1) * TOK, :]),
            "ident": ident, "gv": gv, "tri": tri, "cm": cm, "cos2": cos2, "sin2": sin2,
            "wsel": np.ascontiguousarray(wsel), "wqb": np.ascontiguousarray(wqb),
            "wkvb": np.ascontiguousarray(w_kvb[:, 256 * j:256 * j + 256]),
            "wo": wo, "relbT": np.ascontiguousarray(rel[2 * j:2 * j + 2, :].T),
        }
        m.update(wmaps)
        maps.append(m)
    return maps


_NC_CACHE = {}


def run_raw(inputs, stage="full", trace=False):
    if stage not in _NC_CACHE:
        _NC_CACHE[stage] = build_nc(stage)
    nc = _NC_CACHE[stage]
    maps = _prep_inputs(inputs)
    return run_bass_kernel_spmd(nc, maps, core_ids=list(range(8)), trace=trace)


def kernel(**inputs):
    res = run_raw(inputs)
    out = np.zeros((2, SEQ, D), np.float32)
    for core in range(8):
        b, j = divmod(core, 4)
        out[b, j * TOK:(j + 1) * TOK, :] = res.results[core]["out"]
    return out
```
